# Optimizing a Trainium2 kernel written in Bass

```python
import math
import jax
import jax.numpy as jnp
from jax import lax
import numpy as np

D_MODEL = 2048
BATCH = 8
SEQ = 2048
DEPTH = 2

GRID_W = 64
CTX_LEN = 256
EPS = 1e-6
N_MOD = 9

FFN_HIDDEN = 5632

MLA_HEADS = D_MODEL // 256
MLA_NOPE = 128
MLA_ROPE = 64
MLA_V = 128
MLA_Q_LORA = D_MODEL // 4
MLA_KV_LORA = D_MODEL // 8
MLA_WIDTH = MLA_HEADS * MLA_V
ROPE_BASE = 10000.0
ATTN_BLOCK = 128

HY_WIDTH = D_MODEL // 4
HY_ORDER = 2
HY_SHORT = 3
HY_EMB = 33
HY_FFN = 64
HY_SIN_FREQ = 1.0
HY_MIN_DECAY = math.log(1e-2) / 1.5
HY_MAX_DECAY = math.log(1e-2) / 0.3
HY_WINDOW_SHIFT = 0.05

RET_HEADS = D_MODEL // 512
RET_DK = 64
RET_DV = 128
RET_WIDTH = RET_HEADS * RET_DV
RET_CHUNK = 128

COL_SIZES = (MLA_KV_LORA, MLA_ROPE, RET_HEADS * RET_DK, RET_WIDTH,
             MLA_Q_LORA, RET_HEADS * RET_DK, RET_WIDTH, (HY_ORDER + 1) * HY_WIDTH)
N_KV_COLS = sum(COL_SIZES[:4])
N_IN = sum(COL_SIZES)

kernel_name = 'hybrid_mla_hyena_retention_dit_block'


def split_cols(z, sizes):
    return jnp.split(z, [int(s) for s in np.cumsum(sizes)[:-1]], axis=-1)


def rms_norm(x, gain=None):
    xf = x.astype(jnp.float32)
    y = xf * lax.rsqrt(jnp.mean(xf * xf, axis=-1, keepdims=True) + EPS)
    if gain is not None:
        y = y * gain.astype(jnp.float32)
    return y.astype(x.dtype)


def modulate(x, shift, scale):
    return rms_norm(x) * (1 + scale) + shift


def modulation(cond, w, b):
    m = jax.nn.silu(cond) @ w + b
    m = m.reshape((-1, 1, m.shape[-1]))
    return jnp.split(m, N_MOD, axis=-1)


def swiglu(h, w_gate, w_up, w_down):
    return (jax.nn.silu(h @ w_gate) * (h @ w_up)) @ w_down


def axial_rope_tables(rows, dtype):
    n_freq = MLA_ROPE // 4
    inv_freq = ROPE_BASE ** (-jnp.arange(n_freq, dtype=jnp.float32) / n_freq)
    row = jnp.repeat(jnp.arange(rows), GRID_W).astype(jnp.float32)
    col = jnp.tile(jnp.arange(GRID_W), rows).astype(jnp.float32)
    tabs = []
    for pos in (row, col):
        ang = pos[:, None] * inv_freq[None, :]
        tabs += [jnp.cos(ang)[None, :, None, :].astype(dtype), jnp.sin(ang)[None, :, None, :].astype(dtype)]
    return tuple(tabs)


def rotate(x, cos, sin):
    x1, x2 = jnp.split(x, 2, axis=-1)
    return jnp.concatenate([x1 * cos - x2 * sin, x1 * sin + x2 * cos], axis=-1)


def apply_axial_rope(x, rope):
    cos_r, sin_r, cos_c, sin_c = rope
    x_row, x_col = jnp.split(x, 2, axis=-1)
    return jnp.concatenate([rotate(x_row, cos_r, sin_r), rotate(x_col, cos_c, sin_c)], axis=-1)


def mla_queries(q_lat, p, rope):
    b, n, _ = q_lat.shape
    q = (rms_norm(q_lat, p['mla_q_norm']) @ p['mla_wuq']).reshape(b, n, MLA_HEADS, MLA_NOPE + MLA_ROPE)
    q_nope = rms_norm(q[..., :MLA_NOPE], p['mla_qn_nope'])
    q_rope = rms_norm(q[..., MLA_NOPE:], p['mla_qn_rope'])
    if rope is not None:
        q_rope = apply_axial_rope(q_rope, rope)
    return jnp.concatenate([q_nope, q_rope], axis=-1)


def mla_keys_values(kv_lat, k_rope, p, rope):
    b, n, _ = kv_lat.shape
    kv = (rms_norm(kv_lat, p['mla_kv_norm']) @ p['mla_wukv']).reshape(b, n, MLA_HEADS, MLA_NOPE + MLA_V)
    k_nope = rms_norm(kv[..., :MLA_NOPE], p['mla_kn_nope'])
    k_r = rms_norm(k_rope, p['mla_kn_rope'])[:, :, None, :]
    if rope is not None:
        k_r = apply_axial_rope(k_r, rope)
    k = jnp.concatenate([k_nope, jnp.broadcast_to(k_r, (b, n, MLA_HEADS, MLA_ROPE))], axis=-1)
    return k, kv[..., MLA_NOPE:]


def block_attention(q, k, v):
    b, lq, h, dk = q.shape
    scale = dk ** -0.5
    qb = q.reshape(b, lq // ATTN_BLOCK, ATTN_BLOCK, h, dk).transpose(1, 0, 2, 3, 4)

    def one_block(qi):
        s = jnp.einsum('bqhd,bkhd->bhqk', qi, k).astype(jnp.float32) * scale
        pr = jax.nn.softmax(s, axis=-1).astype(v.dtype)
        return jnp.einsum('bhqk,bkhd->bqhd', pr, v)

    o = lax.map(one_block, qb)
    return o.transpose(1, 0, 2, 3, 4).reshape(b, lq, h, v.shape[-1])


def hyena_filter_spectra(n, p):
    f32 = jnp.float32
    t = jnp.linspace(0.0, 1.0, n, dtype=f32)[:, None]
    bands = (HY_EMB - 1) // 2
    w = 2 * math.pi * jnp.arange(n, dtype=f32)[:, None] / n
    f = jnp.linspace(1e-4, bands - 1, bands, dtype=f32)[None, :]
    feats = jnp.concatenate([t, jnp.cos(f * w), -jnp.sin(f * w)], axis=-1)
    h = jnp.sin(HY_SIN_FREQ * (feats @ p['hy_ffn_w1'].astype(f32) + p['hy_ffn_b1'].astype(f32)))
    h = jnp.sin(HY_SIN_FREQ * (h @ p['hy_ffn_w2'].astype(f32) + p['hy_ffn_b2'].astype(f32)))
    h = (h @ p['hy_ffn_w3'].astype(f32)).reshape(n, 2, HY_ORDER, HY_WIDTH)
    deltas = jnp.abs(jnp.linspace(HY_MIN_DECAY, HY_MAX_DECAY, HY_WIDTH, dtype=f32))
    window = jnp.exp(-t * deltas[None, :]) + HY_WINDOW_SHIFT
    h = h * window[:, None, None, :]
    h_fwd, h_bwd = h[:, 0], h[:, 1]
    two_sided = jnp.concatenate([h_fwd[:1] + h_bwd[:1], h_fwd[1:], jnp.zeros_like(h_fwd[:1]), h_bwd[:0:-1]], axis=0)
    return jnp.fft.rfft(two_sided, axis=0)


def long_conv(u, h_spec, skip):
    n = u.shape[1]
    uf = u.astype(jnp.float32)
    y = jnp.fft.irfft(jnp.fft.rfft(uf, n=2 * n, axis=1) * h_spec[None], n=2 * n, axis=1)[:, :n]
    return (y + uf * skip.astype(jnp.float32)).astype(u.dtype)


def short_conv(u, w, b):
    n = u.shape[1]
    pad = (HY_SHORT - 1) // 2
    up = jnp.pad(u, ((0, 0), (pad, pad), (0, 0)))
    return sum(up[:, j:j + n] * w[j] for j in range(HY_SHORT)) + b


def hyena_mix(u, p):
    n = u.shape[1]
    x1, x2, v = jnp.split(short_conv(u, p['hy_conv_w'], p['hy_conv_b']), HY_ORDER + 1, axis=-1)
    spec = hyena_filter_spectra(n, p)
    y = v
    for o, gate in enumerate((x1, x2)):
        y = gate * long_conv(y, spec[:, o], p['hy_skip'][o])
    return y


def to_heads(z, d):
    b, n, _ = z.shape
    return z.reshape(b, n, -1, d).transpose(0, 2, 1, 3).astype(jnp.float32)


def retention_scan(q, k, v, log_gamma, s0):
    b, h, n, dk = q.shape
    dv = v.shape[-1]
    nc = n // RET_CHUNK
    idx = jnp.arange(RET_CHUNK, dtype=jnp.float32)
    diff = idx[:, None] - idx[None, :]
    lg = log_gamma[:, None, None]
    intra_decay = jnp.where(diff >= 0, jnp.exp(lg * jnp.maximum(diff, 0.0)), 0.0)
    q_decay = jnp.exp(log_gamma[:, None] * (idx + 1.0))
    k_decay = jnp.exp(log_gamma[:, None] * (RET_CHUNK - 1.0 - idx))
    chunk_decay = jnp.exp(log_gamma * RET_CHUNK)
    qc = q.reshape(b, h, nc, RET_CHUNK, dk)
    kc = k.reshape(b, h, nc, RET_CHUNK, dk)
    vc = v.reshape(b, h, nc, RET_CHUNK, dv)
    scores = jnp.einsum('bhcid,bhcjd->bhcij', qc, kc) * intra_decay[None, :, None]
    o_intra = jnp.einsum('bhcij,bhcjv->bhciv', scores, vc)
    kv_chunk = jnp.einsum('bhcjd,bhcjv->cbhdv', kc * k_decay[None, :, None, :, None], vc)

    def step(s, kv):
        return chunk_decay[None, :, None, None] * s + kv, s

    _, s_prev = lax.scan(step, s0, kv_chunk)
    o_cross = jnp.einsum('bhcid,cbhdv->bhciv', qc, s_prev) * q_decay[None, :, None, :, None]
    return (o_intra + o_cross).reshape(b, h, n, dv)


def retention_context_state(k, v, log_gamma, backward):
    n = k.shape[2]
    pos = jnp.arange(n, dtype=jnp.float32)
    dist = pos if backward else (n - 1.0) - pos
    w = jnp.exp(log_gamma[:, None] * dist[None, :])
    return jnp.einsum('hn,bhnd,bhne->bhde', w, k, v)


def retention_mix(q, k, v, log_gamma, s0):
    fwd = retention_scan(q, k, v, log_gamma[0], s0[0])
    bwd = retention_scan(q[:, :, ::-1], k[:, :, ::-1], v[:, :, ::-1], log_gamma[1], s0[1])[:, :, ::-1]
    return fwd + bwd


def retention_output(o, gate, gn_w, gn_b):
    b, h, n, dv = o.shape
    mu = jnp.mean(o, axis=-1, keepdims=True)
    var = jnp.mean(jnp.square(o - mu), axis=-1, keepdims=True)
    o = ((o - mu) * lax.rsqrt(var + EPS)).transpose(0, 2, 1, 3).reshape(b, n, h * dv)
    o = o * gn_w.astype(jnp.float32) + gn_b.astype(jnp.float32)
    return (jax.nn.silu(gate.astype(jnp.float32)) * o).astype(gate.dtype)


def token_mixers(q_lat, keys, values, rq, rk, rv, s0, r_gate, hy_in, log_gamma, p, rope):
    b, n, _ = q_lat.shape
    q = mla_queries(q_lat, p, rope)
    attn = block_attention(q, keys, values).reshape(b, n, MLA_WIDTH)
    hy = hyena_mix(hy_in, p)
    ret = retention_mix(rq, rk, rv, log_gamma, s0)
    merged = jnp.concatenate([rms_norm(attn, p['mla_out_norm']),
                              rms_norm(hy, p['hy_out_norm']),
                              retention_output(ret, r_gate, p['ret_gn_w'], p['ret_gn_b'])], axis=-1)
    return merged @ p['w_out']


def trunk_layer(x, ctx, mod_x, mod_c, p, rope, need_ctx_out):
    sx1, cx1, gx1, sx2, cx2, gx2, sx3, cx3, gx3 = mod_x
    sc1, cc1, gc1, sc2, cc2, gc2, sc3, cc3, gc3 = mod_c
    ffn1 = (p['ffn1_gate'], p['ffn1_up'], p['ffn1_down'])
    ffn2 = (p['ffn2_gate'], p['ffn2_up'], p['ffn2_down'])

    x = x + 0.5 * gx1 * swiglu(modulate(x, sx1, cx1), *ffn1)
    ctx = ctx + 0.5 * gc1 * swiglu(modulate(ctx, sc1, cc1), *ffn1)

    log_gamma = jax.nn.log_sigmoid(p['ret_decay'].astype(jnp.float32))

    hc = modulate(ctx, sc2, cc2)
    if need_ctx_out:
        zc = split_cols(hc @ p['w_in'], COL_SIZES)
    else:
        zc = split_cols(hc @ p['w_in'][:, :N_KV_COLS], COL_SIZES[:4])
    k_c, v_c = mla_keys_values(zc[0], zc[1], p, None)
    rk_c = to_heads(zc[2], RET_DK) * RET_DK ** -0.5
    rv_c = to_heads(zc[3], RET_DV)
    s_ctx = (retention_context_state(rk_c, rv_c, log_gamma[0], False),
             retention_context_state(rk_c, rv_c, log_gamma[1], True))

    kv_lat, k_rope, r_k, r_v, q_lat, r_q, r_gate, hy_in = split_cols(modulate(x, sx2, cx2) @ p['w_in'], COL_SIZES)
    k_x, v_x = mla_keys_values(kv_lat, k_rope, p, rope)
    mixed = token_mixers(q_lat, jnp.concatenate([k_c, k_x], axis=1), jnp.concatenate([v_c, v_x], axis=1),
                         to_heads(r_q, RET_DK), to_heads(r_k, RET_DK) * RET_DK ** -0.5, to_heads(r_v, RET_DV),
                         s_ctx, r_gate, hy_in, log_gamma, p, rope)
    x = x + gx2 * mixed

    x = x + 0.5 * gx3 * swiglu(modulate(x, sx3, cx3), *ffn2)

    if need_ctx_out:
        zero = jnp.zeros_like(s_ctx[0])
        mixed_c = token_mixers(zc[4], k_c, v_c, to_heads(zc[5], RET_DK), rk_c, rv_c, (zero, zero),
                               zc[6], zc[7], log_gamma, p, None)
        ctx = ctx + gc2 * mixed_c
        ctx = ctx + 0.5 * gc3 * swiglu(modulate(ctx, sc3, cc3), *ffn2)
    return x, ctx


def setup_inputs(seed: int = 0) -> dict:
    key = jax.random.key(seed)
    keys = iter(jax.random.split(key, 48))
    f32 = jnp.float32

    def normal(shape, scale):
        return jax.random.normal(next(keys), shape, f32) * scale

    def gain(shape):
        return 1.0 + normal(shape, 0.02)

    D, L, F = D_MODEL, DEPTH, FFN_HIDDEN
    gam = 1.0 - 2.0 ** (-5.0 - np.arange(RET_HEADS, dtype=np.float32))
    decay_logit = jnp.asarray(np.log(gam / (1.0 - gam)), f32)
    return {
        'x': normal((BATCH, SEQ, D), 1.0),
        'c': normal((BATCH, D), 1.0),
        'ctx': normal((BATCH, CTX_LEN, D), 1.0),
        'c_ctx': normal((D,), 1.0),
        'ada_w': normal((L, D, N_MOD * D), 0.5 * D ** -0.5),
        'ada_b': normal((L, N_MOD * D), 0.02),
        'ffn1_gate': normal((L, D, F), D ** -0.5),
        'ffn1_up': normal((L, D, F), D ** -0.5),
        'ffn1_down': normal((L, F, D), F ** -0.5),
        'w_in': normal((L, D, N_IN), D ** -0.5),
        'mla_q_norm': gain((L, MLA_Q_LORA)),
        'mla_wuq': normal((L, MLA_Q_LORA, MLA_HEADS * (MLA_NOPE + MLA_ROPE)), MLA_Q_LORA ** -0.5),
        'mla_kv_norm': gain((L, MLA_KV_LORA)),
        'mla_wukv': normal((L, MLA_KV_LORA, MLA_HEADS * (MLA_NOPE + MLA_V)), MLA_KV_LORA ** -0.5),
        'mla_qn_nope': gain((L, MLA_NOPE)),
        'mla_qn_rope': gain((L, MLA_ROPE)),
        'mla_kn_nope': gain((L, MLA_NOPE)),
        'mla_kn_rope': gain((L, MLA_ROPE)),
        'mla_out_norm': gain((L, MLA_WIDTH)),
        'hy_conv_w': normal((L, HY_SHORT, (HY_ORDER + 1) * HY_WIDTH), HY_SHORT ** -0.5),
        'hy_conv_b': normal((L, (HY_ORDER + 1) * HY_WIDTH), 0.02),
        'hy_ffn_w1': normal((L, HY_EMB, HY_FFN), HY_EMB ** -0.5),
        'hy_ffn_b1': normal((L, HY_FFN), 0.02),
        'hy_ffn_w2': normal((L, HY_FFN, HY_FFN), HY_FFN ** -0.5),
        'hy_ffn_b2': normal((L, HY_FFN), 0.02),
        'hy_ffn_w3': normal((L, HY_FFN, 2 * HY_ORDER * HY_WIDTH), HY_FFN ** -0.5),
        'hy_skip': normal((L, HY_ORDER, HY_WIDTH), 1.0),
        'hy_out_norm': gain((L, HY_WIDTH)),
        'ret_decay': decay_logit[None, None, :] + normal((L, 2, RET_HEADS), 0.05),
        'ret_gn_w': gain((L, RET_WIDTH)),
        'ret_gn_b': normal((L, RET_WIDTH), 0.02),
        'w_out': normal((L, D, D), D ** -0.5),
        'ffn2_gate': normal((L, D, F), D ** -0.5),
        'ffn2_up': normal((L, D, F), D ** -0.5),
        'ffn2_down': normal((L, F, D), F ** -0.5),
    }


def reference(x, c, ctx, c_ctx, ada_w, ada_b, ffn1_gate, ffn1_up, ffn1_down, w_in,
              mla_q_norm, mla_wuq, mla_kv_norm, mla_wukv, mla_qn_nope, mla_qn_rope,
              mla_kn_nope, mla_kn_rope, mla_out_norm, hy_conv_w, hy_conv_b, hy_ffn_w1,
              hy_ffn_b1, hy_ffn_w2, hy_ffn_b2, hy_ffn_w3, hy_skip, hy_out_norm, ret_decay,
              ret_gn_w, ret_gn_b, w_out, ffn2_gate, ffn2_up, ffn2_down):
    ROWS = x.shape[1] // GRID_W
    rope = axial_rope_tables(ROWS, x.dtype)
    for l in range(DEPTH):
        p = {
            'ffn1_gate': ffn1_gate[l], 'ffn1_up': ffn1_up[l], 'ffn1_down': ffn1_down[l],
            'w_in': w_in[l],
            'mla_q_norm': mla_q_norm[l], 'mla_wuq': mla_wuq[l],
            'mla_kv_norm': mla_kv_norm[l], 'mla_wukv': mla_wukv[l],
            'mla_qn_nope': mla_qn_nope[l], 'mla_qn_rope': mla_qn_rope[l],
            'mla_kn_nope': mla_kn_nope[l], 'mla_kn_rope': mla_kn_rope[l],
            'mla_out_norm': mla_out_norm[l],
            'hy_conv_w': hy_conv_w[l], 'hy_conv_b': hy_conv_b[l],
            'hy_ffn_w1': hy_ffn_w1[l], 'hy_ffn_b1': hy_ffn_b1[l],
            'hy_ffn_w2': hy_ffn_w2[l], 'hy_ffn_b2': hy_ffn_b2[l], 'hy_ffn_w3': hy_ffn_w3[l],
            'hy_skip': hy_skip[l], 'hy_out_norm': hy_out_norm[l],
            'ret_decay': ret_decay[l], 'ret_gn_w': ret_gn_w[l], 'ret_gn_b': ret_gn_b[l],
            'w_out': w_out[l],
            'ffn2_gate': ffn2_gate[l], 'ffn2_up': ffn2_up[l], 'ffn2_down': ffn2_down[l],
        }
        mod_x = modulation(c, ada_w[l], ada_b[l])
        mod_c = modulation(c_ctx, ada_w[l], ada_b[l])
        x, ctx = trunk_layer(x, ctx, mod_x, mod_c, p, rope, l < DEPTH - 1)
    return x
```

```python
import math
import numpy as np
import ml_dtypes
import concourse.bass as bass
import concourse.mybir as mybir
from concourse.bass_utils import run_bass_kernel_spmd
from contextlib import ExitStack

F32 = mybir.dt.float32
BF16 = mybir.dt.bfloat16
AF = mybir.ActivationFunctionType
ALU = mybir.AluOpType
AX = mybir.AxisListType

ENGS = ('pe', 'act', 'dve', 'pool', 'sp')
CAP = 30000
NDSEM = 56

D = 2048
NCH = 16
T = 2304
TC = 256
TL = 2048
FH = 5632
NJ = 44
EPS = 1e-6
NIN = 3904


class Buf:
    __slots__ = ('name', 'last_w', 'readers', 'psum', 'lw_real')
    registry = []

    def __init__(self, name='', psum=False):
        self.name = name
        self.last_w = None
        self.readers = []
        self.psum = psum
        self.lw_real = True
        Buf.registry.append(self)


class Op:
    __slots__ = ('idx', 'eng', 'fn', 'deps', 'is_dma', 'ticket', 'dsem', 'dval',
                 'needs_inc', 'waits', 'ndma', 'eidx')


class Prog:
    def __init__(self, nc):
        self.nc = nc
        self.ops = []
        self.es = ExitStack()
        self.last_eng = {}
        self.dmas_since = []
        Buf.registry = []

    def sb(self, name, shape, dtype):
        return self.es.enter_context(self.nc.sbuf_tensor(name, list(shape), dtype))

    def ps(self, name, shape, dtype=F32):
        return self.es.enter_context(self.nc.psum_tensor(name, list(shape), dtype))

    def add(self, eng, fn, reads=(), writes=(), dma=False, ndma=1, extra=()):
        op = Op()
        op.idx = len(self.ops)
        op.eng = eng
        op.fn = fn
        op.is_dma = dma
        op.ndma = ndma
        op.needs_inc = False
        op.ticket = None
        op.dsem = None
        op.dval = None
        op.waits = None
        deps = {}
        for b in reads:
            if b.last_w is not None:
                raw = b.lw_real
                if b.last_w.idx in deps:
                    raw = raw or deps[b.last_w.idx][1]
                deps[b.last_w.idx] = (b.last_w, raw)
        for b in writes:
            if b.last_w is not None and b.last_w.idx not in deps:
                deps[b.last_w.idx] = (b.last_w, False)
            for r in b.readers:
                if r.idx not in deps:
                    deps[r.idx] = (r, False)
        for d in extra:
            if d.idx not in deps:
                deps[d.idx] = (d, True)
        op.deps = list(deps.values())
        for b in writes:
            b.last_w = op
            b.lw_real = True
            b.readers = []
        for b in reads:
            if b.psum:
                if b.last_w is not op:
                    b.last_w = op
                    b.lw_real = False
                    b.readers = []
            else:
                b.readers.append(op)
        self.ops.append(op)
        if fn is not None:
            if dma:
                self.dmas_since.append(op)
            else:
                self.last_eng[eng] = op
        return op

    def dma(self, q, out, in_, reads, writes, **kw):
        return self.add(q, lambda e: e.dma_start(out=out, in_=in_, **kw), reads, writes, dma=True)

    def barrier(self):
        ex = list(self.last_eng.values()) + list(self.dmas_since)
        for e in ENGS:
            self.add(e, None, extra=ex)
        self.dmas_since = []
        for b in Buf.registry:
            b.last_w = None
            b.readers = []
            b.lw_real = True

    def emit(self):
        nc = self.nc
        ops = self.ops
        for op in ops:
            need = []
            for (d, raw) in op.deps:
                if d.fn is None and not d.is_dma:
                    continue
                if d.eng == op.eng and not d.is_dma and not op.is_dma and op.fn is not None:
                    if not raw or op.eng == 'pe':
                        continue
                if d.eng == op.eng and not d.is_dma and op.fn is None:
                    continue
                need.append(d)
                if not d.is_dma:
                    d.needs_inc = True
            op.deps = need
        cnt = {e: 0 for e in ENGS}
        eidx = {e: 0 for e in ENGS}
        for op in ops:
            op.eidx = eidx[op.eng]
            eidx[op.eng] += 1
            if op.needs_inc:
                cnt[op.eng] += 1
                op.ticket = cnt[op.eng]
        nep = {e: (cnt[e] + CAP - 1) // CAP for e in ENGS}
        sems = {e: [self.es.enter_context(nc.semaphore(f"s_{e}{i}")) for i in range(nep[e])]
                for e in ENGS}
        dsems = [self.es.enter_context(nc.semaphore(f"d{i}")) for i in range(NDSEM)]
        dval = [0] * NDSEM
        known = {f: {e: -1 for e in ENGS} for f in ENGS}
        dknown = {f: [0] * NDSEM for f in ENGS}
        ndma = 0
        for op in ops:
            w = []
            f = op.eng
            if op.is_dma:
                si = ndma % NDSEM
                ndma += 1
                if dval[si] > dknown[f][si]:
                    w.append((dsems[si], dval[si]))
                    dknown[f][si] = dval[si]
                dval[si] += 16 * op.ndma
                op.dsem = si
                op.dval = dval[si]
            best = {}
            for d in op.deps:
                if d.is_dma:
                    if d.dval > dknown[f][d.dsem]:
                        w.append((dsems[d.dsem], d.dval))
                        dknown[f][d.dsem] = d.dval
                else:
                    if d.eidx > known[f][d.eng]:
                        if d.eng not in best or d.eidx > best[d.eng].eidx:
                            best[d.eng] = d
            for e, d in best.items():
                known[f][e] = d.eidx
                t = d.ticket - 1
                w.append((sems[e][t // CAP], t % CAP + 1))
            op.waits = w
        per = {e: [op for op in ops if op.eng == e] for e in ENGS}
        self.stats = {e: len(per[e]) for e in ENGS}
        self.stats['waits'] = sum(len(op.waits) for op in ops)
        self.stats['ndma'] = ndma

        def run(e, engobj):
            for op in per[e]:
                for (s, v) in op.waits:
                    engobj.wait_ge(s, v)
                if op.fn is None:
                    continue
                ins = op.fn(engobj)
                if op.is_dma:
                    if not isinstance(ins, (list, tuple)):
                        ins = [ins]
                    assert len(ins) == op.ndma
                    for i_ in ins:
                        i_.then_inc(dsems[op.dsem], 16)
                elif op.needs_inc:
                    t = op.ticket - 1
                    ins.then_inc(sems[e][t // CAP], 1)

        with nc.Block() as block:
            @block.tensor
            def _(eng):
                run('pe', eng)

            @block.scalar
            def _(eng):
                run('act', eng)

            @block.vector
            def _(eng):
                run('dve', eng)

            @block.gpsimd
            def _(eng):
                run('pool', eng)

            @block.sync
            def _(eng):
                run('sp', eng)
        self.es.close()


class Tl:
    __slots__ = ('ap', 'bufs')

    def __init__(self, ap, bufs):
        self.ap = ap
        self.bufs = bufs

    def __getitem__(self, k):
        return self.ap[k]


AW = 50688


class KB:
    def __init__(self, nc):
        self.nc = nc
        self.P = Prog(nc)
        P = self.P
        self.arena = P.sb("arena", [128, AW], F32)
        self.off = 0
        self.mark = 0
        self.psum = [P.ps(f"ps{i}", [128, 512], F32) for i in range(8)]
        self.pb = [Buf(f"ps{i}", psum=True) for i in range(8)]
        self.qi = 0

    def tile(self, shape, dtype, bufs=None, at=None, name=''):
        free = 1
        for s in shape[1:]:
            free *= s
        words = free if dtype == F32 else (free + 1) // 2
        words = (words + 7) // 8 * 8
        if at is None:
            o = self.off
            self.off += words
            assert self.off <= AW, f"arena overflow {self.off} ({name})"
        else:
            o = at
            assert o + words <= AW
        ap = self.arena[0:shape[0], o:o + words]
        if dtype != F32:
            ap = ap.bitcast(dtype)
        ap = ap[:, 0:free]
        if len(shape) == 3:
            ap = ap.rearrange("p (a b) -> p a b", a=shape[1])
        elif len(shape) == 4:
            ap = ap.rearrange("p (a b c) -> p a b c", a=shape[1], b=shape[2])
        t = Tl(ap, bufs if bufs is not None else [Buf(name)])
        return t

    def persist(self):
        self.mark = self.off

    def reset(self):
        self.P.barrier()
        self.off = self.mark

    def mm(self, out, lhsT, rhs, start, stop, reads, writes):
        self.P.add('pe', lambda e: e.matmul(out, lhsT, rhs, start=start, stop=stop), reads, writes)

    def tr(self, out, in_, ident, reads, writes):
        self.P.add('pe', lambda e: e.transpose(out, in_, ident), reads, writes)

    def act(self, out, in_, func, reads, writes, **kw):
        self.P.add('act', lambda e: e.activation(out, in_, func, **kw), reads, writes)

    def tt(self, eng, out, in0, in1, op, reads, writes):
        self.P.add(eng, lambda e: e.tensor_tensor(out, in0, in1, op), reads, writes)

    def ts(self, eng, out, in0, s1, s2, op0, op1, reads, writes):
        if op1 is None:
            self.P.add(eng, lambda e: e.tensor_scalar(out, in0, s1, None, op0), reads, writes)
        else:
            self.P.add(eng, lambda e: e.tensor_scalar(out, in0, s1, s2, op0, op1), reads, writes)

    def stt(self, eng, out, in0, scalar, in1, op0, op1, reads, writes):
        eng = 'dve'
        self.P.add(eng, lambda e: e.scalar_tensor_tensor(out, in0, scalar, in1, op0, op1), reads, writes)

    def cp(self, eng, out, in_, reads, writes):
        if eng == 'act':
            self.P.add('act', lambda e: e.copy(out, in_), reads, writes)
        else:
            self.P.add(eng, lambda e: e.tensor_copy(out, in_), reads, writes)

    def memset(self, eng, out, val, writes):
        self.P.add(eng, lambda e: e.memset(out, val), [], writes)

    def dma(self, q, out, in_, reads, writes):
        self.P.dma(q, out, in_, reads, writes)

    def rstd(self, out, ss, scale, reads, writes, eps=EPS):
        self.P.add('act', lambda e: e.activation(out, ss, AF.Ln, bias=self.epsc(eps, out.shape[0]), scale=scale),
                   reads + self.epst.bufs, writes)
        self.P.add('act', lambda e: e.activation(out, out, AF.Exp, scale=-0.5), writes, writes)

    def sqrt_ms(self, out, ss, scale, reads, writes, eps=EPS):
        self.P.add('act', lambda e: e.activation(out, ss, AF.Ln, bias=self.epsc(eps, out.shape[0]), scale=scale),
                   reads + self.epst.bufs, writes)
        self.P.add('act', lambda e: e.activation(out, out, AF.Exp, scale=-0.5), writes, writes)

    def epsc(self, eps, n):
        assert eps == EPS
        return self.epst.ap[0:n, 0:1]


def rows(ap, p=128):
    return ap.rearrange("(k p) n -> p k n", p=p)


SV_ROWS = 88
SV = dict(q_norm=0, kv_norm=4, qn_nope=6, qn_rope=7, kn_nope=8, kn_rope=9, out_norm=10,
          conv_w=18, conv_b=54, skip=66, hy_norm=74, gn_w=78, gn_b=82, b1=86, b2=87)

LAYER_W = [('ada_w', [D, 9 * D]), ('ada_b', [144, 128]), ('ffn1_gate', [D, FH]), ('ffn1_up', [D, FH]),
           ('ffn1_down', [FH, D]), ('w_in', [D, NIN]), ('wuq', [512, 1536]), ('wukv', [256, 2048]),
           ('w_out', [D, D]), ('ffn2_gate', [D, FH]), ('ffn2_up', [D, FH]), ('ffn2_down', [FH, D]),
           ('hy_w1', [33, 64]), ('hy_w2', [64, 64]), ('hy_w3', [64, 2048]), ('sv', [SV_ROWS, 128]),
           ('hy_skip', [1, 1024]), ('hy_convw', [1, 3 * 1536]), ('hy_convb', [1, 1536]),
           ('ret_decay', [1, 8]), ('hy_normrow', [1, 512])]


class LazyW(dict):
    def __init__(self, kern, l):
        super().__init__()
        self.kern, self.l = kern, l
        self.shapes = dict(LAYER_W)

    def __missing__(self, n):
        ap = self.kern.din(f"{n}_{self.l}", self.shapes[n])
        self[n] = ap
        return ap


class Kern(KB):
    def __init__(self, nc, nlayers=2, dbg=None):
        super().__init__(nc)
        self.nl = nlayers
        self.dbg = dbg or {}
        self.feeds = self.dbg.get('feeds', {})
        self.dumps = self.dbg.get('dumps', [])
        self.inputs = {}
        self.scratch = {}
        self.W = [LazyW(self, l) for l in range(nlayers)]
        self.ident_d = self.din("ident", [128, 128])
        self.XT = self.dscr("XT", [T // 128, 128, NCH, 128], F32)
        self.bXT = [[Buf(f"XT{k}_{i}") for i in range(9)] for k in range(NCH)]
        self.outbufs = []

    def din(self, name, shape, dt=F32):
        if name not in self.inputs:
            self.inputs[name] = self.nc.dram_tensor(name, list(shape), dt, kind="ExternalInput").ap()
        return self.inputs[name]

    def dscr(self, name, shape, dt=F32):
        if name not in self.scratch:
            kind = "ExternalOutput" if (name in self.dumps or name in self.feeds) else "Internal"
            self.scratch[name] = self.nc.dram_tensor(name, list(shape), dt, kind=kind).ap()
        return self.scratch[name]

    def phase_feed(self):
        for name, arr in self.feeds.items():
            dt_ = F32 if arr.dtype == np.float32 else BF16
            dst = self.dscr(name, list(arr.shape), dt_)
            src = self.din("feed_" + name, list(arr.shape), dt_)
            ob = Buf('feed')
            self.outbufs.append(ob)
            self.dma('sp', dst, src, [], [ob])
        if 'modtab' in self.dbg:
            src = self.din("modtab", [128, 288 + 96 + 96])
            self.dma('sp', self.mod.ap, src[:, 0:288], [], self.mod.bufs)
            self.dma('sp', self.s1p.ap, src[:, 288:384], [], self.s1p.bufs)
            self.dma('sp', self.hg.ap, src[:, 384:480], [], self.hg.bufs)
        self.reset()

    def xtr(self, m, T0, SB):
        return self.XT[T0 // 128:(T0 + SB) // 128, :, m, :].rearrange("b p t -> p b t")

    def xt_bufs(self, ks, t0, n):
        out = []
        for k in ks:
            for i in range(t0 // 256, (t0 + n + 255) // 256):
                out.append(self.bXT[k][i])
        return out

    def consts(self):
        P = self.P
        self.ident = self.tile([128, 128], F32, name='ident')
        self.identb = self.tile([128, 128], BF16, name='identb')
        self.onesb = self.tile([128, 128], BF16, name='onesb')
        self.onesf = self.tile([128, 128], F32, name='onesf')
        self.dma('sp', self.ident.ap, self.ident_d, [], self.ident.bufs)
        self.cp('dve', self.identb.ap, self.ident.ap, self.ident.bufs, self.identb.bufs)
        self.memset('dve', self.onesb.ap, 1.0, self.onesb.bufs)
        self.memset('dve', self.onesf.ap, 1.0, self.onesf.bufs)
        self.epst = self.tile([128, 8], F32, name='epst')
        self.memset('dve', self.epst.ap, EPS, self.epst.bufs)
        self.modT = []
        for i in range(2):
            self.modT.append((self.tile([128, 9 * 16 * 2], F32, name=f'mod{i}'),
                              self.tile([128, 3 * 16 * 2], F32, name=f's1p{i}'),
                              self.tile([128, 3 * 16 * 2], F32, name=f'hg{i}')))
        self.set_layer(0)
        self.persist()

    def set_layer(self, l):
        self.mod, self.s1p, self.hg = self.modT[l % 2]

    def modcol(self, v, k, cond):
        i = (v * 16 + k) * 2 + cond
        return self.mod.ap[:, i:i + 1]

    def s1pcol(self, s, k, cond):
        i = (s * 16 + k) * 2 + cond
        return self.s1p.ap[:, i:i + 1]

    def hgcol(self, s, k, cond):
        i = (s * 16 + k) * 2 + cond
        return self.hg.ap[:, i:i + 1]

    def phase_init(self):
        self.x = self.din("x", [TL, D])
        self.ctx = self.din("ctx", [TC, D])
        groups = [(self.ctx, 0, 0, 256)] + [(self.x, 256, i * 512, 512) for i in range(4)]
        xs = [self.tile([128, D], F32, name=f'xin{i}') for i in range(2)]
        xo = [self.tile([128, NCH, 128], F32, name=f'xo{i}') for i in range(3)]
        cnt = 0
        bi = 0
        for gi, (src, tg, s0, n) in enumerate(groups):
            for tt_ in range(n // 128):
                xi = xs[cnt % 2]
                o = xo[cnt % 3]
                cnt += 1
                self.dma('sp', xi.ap, src[s0 + tt_ * 128:s0 + (tt_ + 1) * 128, :], [], xi.bufs)
                for kg in range(4):
                    b = bi % 8
                    bi += 1
                    for kk in range(4):
                        k = kg * 4 + kk
                        self.tr(self.psum[b][:, kk * 128:(kk + 1) * 128], xi.ap[:, k * 128:(k + 1) * 128],
                                self.ident.ap, xi.bufs + self.ident.bufs, [self.pb[b]])
                    src_ap = self.psum[b][:, :].rearrange("p (a b) -> p a b", a=4)
                    self.cp('dve' if kg % 2 == 0 else 'act', o.ap[:, kg * 4:(kg + 1) * 4, :], src_ap, [self.pb[b]], o.bufs)
                tok = tg + s0 + tt_ * 128
                self.dma('sp', self.XT[tok // 128], o.ap, o.bufs, self.xt_bufs(range(NCH), tok, 128))
        self.reset()

    def phase_dumpmod(self, l=0):
        dst = self.dscr("modout", [128, 480], F32)
        ob = Buf('mo')
        self.outbufs.append(ob)
        self.dma('sp', dst[:, 0:288], self.mod.ap, self.mod.bufs, [ob])
        ob = Buf('mo')
        self.outbufs.append(ob)
        self.dma('sp', dst[:, 288:384], self.s1p.ap, self.s1p.bufs, [ob])
        ob = Buf('mo')
        self.outbufs.append(ob)
        self.dma('sp', dst[:, 384:480], self.hg.ap, self.hg.bufs, [ob])
        self.reset()

    def phase_final(self):
        self.out = self.nc.dram_tensor("out", [TL, D], F32, kind="ExternalOutput").ap()
        xi = [self.tile([128, NCH, 128], F32, name=f'fi{i}') for i in range(3)]
        xo = [self.tile([128, D], F32, name=f'fo{i}') for i in range(2)]
        cnt = 0
        bi = 0
        for g in range(4):
            for tt_ in range(4):
                t0 = 256 + g * 512 + tt_ * 128
                i_ = xi[cnt % 3]
                self.dma('sp', i_.ap, self.XT[t0 // 128], self.xt_bufs(range(NCH), t0, 128), i_.bufs)
                o = xo[cnt % 2]
                cnt += 1
                for kg in range(4):
                    b = bi % 8
                    bi += 1
                    for kk in range(4):
                        k = kg * 4 + kk
                        self.tr(self.psum[b][:, kk * 128:(kk + 1) * 128], i_.ap[:, k, :],
                                self.ident.ap, i_.bufs + self.ident.bufs, [self.pb[b]])
                    self.cp('dve' if kg % 2 == 0 else 'act', o.ap[:, kg * 512:(kg + 1) * 512], self.psum[b][:, :],
                            [self.pb[b]], o.bufs)
                r0 = g * 512 + tt_ * 128
                ob = Buf('outrow')
                self.outbufs.append(ob)
                self.dma('sp', self.out[r0:r0 + 128, :], o.ap, o.bufs, [ob])
        self.reset()

    def load_cols(self, dst, src2d, nrows):
        tmp = self.tile([128, 128], F32, name='lc_tmp')
        self.dma('sp', tmp.ap[0:nrows, :], src2d, [], tmp.bufs)
        b = 7
        self.tr(self.psum[b][:, 0:nrows], tmp.ap[0:nrows, :], self.ident.ap[0:nrows, 0:nrows],
                tmp.bufs + self.ident.bufs, [self.pb[b]])
        return b

    def mod_steps(self, l, accbank):
        W = self.W[l]
        mod_t, s1p_t, hg_t = self.modT[l % 2]
        st = {}
        NSL = 3

        def prologue():
            self.cvec = self.din("cvec", [32, 128])
            b = self.load_cols(None, self.cvec, 32)
            sc = self.tile([128, 16, 2], BF16, name='sc')
            src = self.psum[b][:, 0:32].rearrange("p (c k) -> p k c", c=2)
            self.act(sc.ap, src, AF.Silu, [self.pb[b]], sc.bufs)
            abT = self.tile([128, 144], F32, name='abT')
            b = self.load_cols(None, W['ada_b'][0:128, :], 128)
            self.cp('dve', abT.ap[:, 0:128], self.psum[b][:, 0:128], [self.pb[b]], abT.bufs)
            b = self.load_cols(None, W['ada_b'][128:144, :], 16)
            self.cp('dve', abT.ap[:, 128:144], self.psum[b][:, 0:16], [self.pb[b]], abT.bufs)
            st['sc'], st['abT'] = sc, abT
            st['slabs'] = [self.tile([128, 16, 512], BF16, name=f'adaw{i}') for i in range(NSL)]
            for i in range(NSL - 1):
                ld(i)

        def ld(s_):
            sl = st['slabs'][s_ % NSL]
            self.dma('pool', sl.ap, rows(W['ada_w'])[:, :, s_ * 512:(s_ + 1) * 512], [], sl.bufs)

        def slab(s_):
            sc, abT = st['sc'], st['abT']
            if s_ + NSL - 1 < 36:
                ld(s_ + NSL - 1)
            sl = st['slabs'][s_ % NSL]
            acc = self.psum[accbank][:, 0:8].rearrange("p (j c) -> p j c", c=2)
            for jj in range(4):
                for k in range(16):
                    self.mm(acc[:, jj, :], sl.ap[:, k, jj * 128:(jj + 1) * 128], sc.ap[:, k, :],
                            k == 0, k == 15, sl.bufs + sc.bufs, [self.pb[accbank]])
            modv = mod_t.ap.rearrange("p (j c) -> p j c", c=2)
            self.tt('dve', modv[:, s_ * 4:(s_ + 1) * 4, :], acc, abT.ap[:, s_ * 4:(s_ + 1) * 4].unsqueeze(2).to_broadcast([128, 4, 2]),
                    ALU.add, [self.pb[accbank]] + abT.bufs, mod_t.bufs)

        def epilogue():
            for s_ in range(3):
                self.ts('dve', s1p_t.ap[:, s_ * 32:(s_ + 1) * 32], mod_t.ap[:, (3 * s_ + 1) * 32:(3 * s_ + 2) * 32],
                        1.0, None, ALU.add, None, mod_t.bufs, s1p_t.bufs)
                self.ts('dve', hg_t.ap[:, s_ * 32:(s_ + 1) * 32], mod_t.ap[:, (3 * s_ + 2) * 32:(3 * s_ + 3) * 32],
                        1.0 if s_ == 1 else 0.5, None, ALU.mult, None, mod_t.bufs, hg_t.bufs)
        return prologue, [(lambda s_=s_: slab(s_)) for s_ in range(36)], epilogue

    def phase_mod(self, l):
        pro, slabs, epi = self.mod_steps(l, 0)
        pro()
        for f in slabs:
            f()
        epi()
        self.reset()

    def normmod(self, tiles, hT, s, work_at):
        o = work_at
        xts, sqs = [], []
        for i in range(2):
            xts.append(self.tile([128, NCH, 256], F32, at=o, name=f'nm_x{i}'))
            o += NCH * 256
            sqs.append(self.tile([128, NCH, 256], BF16, at=o, name=f'nm_s{i}'))
            o += NCH * 128
        rs = [self.tile([128, 256], F32, at=o + i * 256, name=f'nm_r{i}') for i in range(2)]
        allb = []
        for t_ in xts + sqs + rs:
            allb += t_.bufs
        off = 0
        it = 0
        for (t0, n, cond) in tiles:
            for u0 in range(0, n, 256):
                xt, sq, r = xts[it % 2], sqs[it % 2], rs[it % 2]
                pbk = 6 + it % 2
                it += 1
                ta = t0 + u0
                self.dma('sp', xt.ap, rows(self.XT)[:, :, ta:ta + 256], self.xt_bufs(range(NCH), ta, 256), xt.bufs)
                self.act(sq.ap, xt.ap, AF.Square, xt.bufs, sq.bufs)
                for k in range(NCH):
                    self.mm(self.psum[pbk][:, 0:256], self.onesb.ap, sq.ap[:, k, :], k == 0, k == NCH - 1,
                            self.onesb.bufs + sq.bufs, [self.pb[pbk]])
                self.rstd(r.ap, self.psum[pbk][:, 0:256], 1.0 / D, [self.pb[pbk]], r.bufs)
                self.tt('dve', xt.ap, xt.ap, r.ap.unsqueeze(1).to_broadcast([128, NCH, 256]), ALU.mult,
                        xt.bufs + r.bufs, xt.bufs)
                for k in range(NCH):
                    dst = hT.ap[:, k, off + u0:off + u0 + 256]
                    if k % 2 == 0:
                        self.act(dst, xt.ap[:, k, :], AF.Identity, xt.bufs + self.s1p.bufs + self.mod.bufs, hT.bufs,
                                 scale=self.s1pcol(s, k, cond), bias=self.modcol(3 * s, k, cond))
                    else:
                        self.ts('dve', dst, xt.ap[:, k, :], self.s1pcol(s, k, cond), self.modcol(3 * s, k, cond),
                                ALU.mult, ALU.add, xt.bufs + self.s1p.bufs + self.mod.bufs, hT.bufs)
            off += n
        return allb

    def nm_scratch(self):
        xts = [self.tile([128, NCH, 128], F32, name=f'nmx{i}') for i in range(3)]
        sqs = [self.tile([128, NCH, 128], BF16, name=f'nms{i}') for i in range(2)]
        rs = [self.tile([128, 128], F32, name=f'nmr{i}') for i in range(2)]
        return (xts, sqs, rs)

    def normmod_steps(self, tiles, hT, s, scr):
        xts, sqs, rs = scr
        subs = []
        off = 0
        for (t0, n, cond) in tiles:
            for u0 in range(0, n, 128):
                subs.append((t0 + u0, off + u0, cond))
            off += n

        def ld(k):
            ta, o_, cond = subs[k]
            xt = xts[k % 3]
            self.dma('sp', xt.ap, self.XT[ta // 128], self.xt_bufs(range(NCH), ta, 128), xt.bufs)

        def sqr(k):
            xt, sq = xts[k % 3], sqs[k % 2]
            self.act(sq.ap, xt.ap, AF.Square, xt.bufs, sq.bufs)

        def b(k):
            ta, o_, cond = subs[k]
            xt, sq, r = xts[k % 3], sqs[k % 2], rs[k % 2]
            pbk = 6 + k % 2
            for kk in range(NCH):
                self.mm(self.psum[pbk][:, 0:128], self.onesb.ap, sq.ap[:, kk, :], kk == 0, kk == NCH - 1,
                        self.onesb.bufs + sq.bufs, [self.pb[pbk]])
            self.rstd(r.ap, self.psum[pbk][:, 0:128], 1.0 / D, [self.pb[pbk]], r.bufs)
            self.tt('dve', xt.ap, xt.ap, r.ap.unsqueeze(1).to_broadcast([128, NCH, 128]), ALU.mult, xt.bufs + r.bufs, xt.bufs)
            for kk in range(NCH):
                dst = hT.ap[:, kk, o_:o_ + 128]
                if kk % 2 == 0:
                    self.act(dst, xt.ap[:, kk, :], AF.Identity, xt.bufs + self.s1p.bufs + self.mod.bufs, hT.bufs,
                             scale=self.s1pcol(s, kk, cond), bias=self.modcol(3 * s, kk, cond))
                else:
                    self.ts('dve', dst, xt.ap[:, kk, :], self.s1pcol(s, kk, cond), self.modcol(3 * s, kk, cond),
                            ALU.mult, ALU.add, xt.bufs + self.s1p.bufs + self.mod.bufs, hT.bufs)
        ns = len(subs)

        def first():
            ld(0)
            if ns > 1:
                ld(1)
            sqr(0)
        steps = [first]
        for k in range(ns):
            def st(k=k):
                if k + 2 < ns:
                    ld(k + 2)
                b(k)
                if k + 1 < ns:
                    sqr(k + 1)
            steps.append(st)
        return steps

    def phase_ffn(self, l, which, sbs):
        W = self.W[l]
        wg, wu, wd = W[f'ffn{which}_gate'], W[f'ffn{which}_up'], W[f'ffn{which}_down']
        s = 0 if which == 1 else 2
        maxSB = max(sum(n for (_, n, _) in tiles) for tiles in sbs)
        hT_full = self.tile([128, NCH, maxSB], BF16, name='hT')
        aT_full = self.tile([128, NJ, maxSB], BF16, name='aT')
        scr = self.nm_scratch()
        ring = [self.tile([128, 4096 * 2], BF16, name=f'wr{i}') for i in range(3)]
        sg = [self.tile([128, 512], F32, name=f'sg{i}') for i in range(3)]
        xr = [self.tile([128, maxSB], F32, name=f'xr{i}') for i in range(2)]
        xo = [self.tile([128, maxSB], F32, name=f'xo{i}') for i in range(2)]
        ri = 0
        for st_ in self.normmod_steps(sbs[0], hT_full, s, scr):
            st_()
        for sbi, tiles in enumerate(sbs):
            SB = sum(n for (_, n, _) in tiles)
            T0 = tiles[0][0]
            hT, aT = hT_full, aT_full
            regs = []
            cur = [0, 0]

            def alloc(n):
                if cur[1] + n > 512:
                    cur[0] += 1
                    cur[1] = 0
                r = (cur[0], cur[1])
                cur[1] += n
                return r
            for (_, n, _) in tiles:
                regs.append((alloc(n), alloc(n)))
            nbank = cur[0] + 1
            assert nbank * 2 <= 6
            for js in range(NJ // 2):
                slot = ring[ri % 3]
                ri += 1
                g_ap = slot.ap[:, 0:4096].rearrange("p (k n) -> p k n", k=16)
                u_ap = slot.ap[:, 4096:8192].rearrange("p (k n) -> p k n", k=16)
                self.dma('pool', g_ap, rows(wg)[:, :, js * 256:(js + 1) * 256], [], slot.bufs)
                self.dma('pool', u_ap, rows(wu)[:, :, js * 256:(js + 1) * 256], [], slot.bufs)
                for jj in range(2):
                    j = js * 2 + jj
                    bb = (j % 2) * nbank
                    off = 0
                    for ti, (_, n, _) in enumerate(tiles):
                        (gb, gc), (ub, uc) = regs[ti]
                        for k in range(NCH):
                            self.mm(self.psum[bb + gb][:, gc:gc + n], g_ap[:, k, jj * 128:(jj + 1) * 128],
                                    hT.ap[:, k, off:off + n], k == 0, k == NCH - 1, slot.bufs + hT.bufs, [self.pb[bb + gb]])
                        for k in range(NCH):
                            self.mm(self.psum[bb + ub][:, uc:uc + n], u_ap[:, k, jj * 128:(jj + 1) * 128],
                                    hT.ap[:, k, off:off + n], k == 0, k == NCH - 1, slot.bufs + hT.bufs, [self.pb[bb + ub]])
                        off += n
                    off = 0
                    for ti, (_, n, _) in enumerate(tiles):
                        (gb, gc), (ub, uc) = regs[ti]
                        sgt = sg[(j * len(tiles) + ti) % 3]
                        self.act(sgt.ap[:, 0:n], self.psum[bb + gb][:, gc:gc + n], AF.Silu, [self.pb[bb + gb]], sgt.bufs)
                        self.tt('dve', aT.ap[:, j, off:off + n], sgt.ap[:, 0:n], self.psum[bb + ub][:, uc:uc + n], ALU.mult,
                                sgt.bufs + [self.pb[bb + ub]], aT.bufs)
                        off += n
            pending = self.normmod_steps(sbs[sbi + 1], hT_full, s, scr) if sbi + 1 < len(sbs) else []
            yregs = []
            cur[0], cur[1] = 0, 0
            for (_, n, _) in tiles:
                yregs.append(alloc(n))
            nby = cur[0] + 1
            for m in range(NCH):
                slot = ring[ri % 3]
                ri += 1
                w_ap = slot.ap[:, 0:NJ * 128].rearrange("p (j n) -> p j n", j=NJ)
                self.dma('pool', w_ap, rows(wd)[:, :, m * 128:(m + 1) * 128], [], slot.bufs)
                xrt, xot = xr[m % 2], xo[m % 2]
                self.dma('sp', xrt.ap[:, 0:SB].rearrange("p (b t) -> p b t", t=128), self.xtr(m, T0, SB), self.xt_bufs([m], T0, SB), xrt.bufs)
                bb = (m % 2) * nby
                off = 0
                for ti, (_, n, _) in enumerate(tiles):
                    (yb, yc) = yregs[ti]
                    for j in range(NJ):
                        self.mm(self.psum[bb + yb][:, yc:yc + n], w_ap[:, j, :], aT.ap[:, j, off:off + n],
                                j == 0, j == NJ - 1, slot.bufs + aT.bufs, [self.pb[bb + yb]])
                    off += n
                off = 0
                for ti, (_, n, cond) in enumerate(tiles):
                    (yb, yc) = yregs[ti]
                    self.stt('dve', xot.ap[:, off:off + n], self.psum[bb + yb][:, yc:yc + n], self.hgcol(s, m, cond),
                             xrt.ap[:, off:off + n], ALU.mult, ALU.add, [self.pb[bb + yb]] + self.hg.bufs + xrt.bufs, xot.bufs)
                    off += n
                self.dma('sp', self.xtr(m, T0, SB), xot.ap[:, 0:SB].rearrange("p (b t) -> p b t", t=128), xot.bufs, self.xt_bufs([m], T0, SB))
                if pending:
                    pending.pop(0)()
            while pending:
                pending.pop(0)()
        self.reset()


SB_ALL = [[(0, 256, 1), (256, 512, 0)], [(768, 512, 0), (1280, 256, 0)], [(1536, 512, 0), (2048, 256, 0)]]
SB_LAT = [[(256, 512, 0), (768, 256, 0)], [(1024, 512, 0), (1536, 256, 0)], [(1792, 512, 0)]]
SB_IN2 = [[(0, 256, 1), (256, 512, 0), (768, 256, 0)], [(1024, 512, 0), (1536, 512, 0), (2048, 256, 0)]]


def build(nlayers=2, phases=None, dbg=None):
    nc = bass.Bass("TRN2", target_bir_lowering=False)
    K = Kern6(nc, nlayers, dbg)
    K.aC = {}
    K.consts()
    if phases is None:
        phases = full_phases(nlayers)
    for ph in phases:
        if len(ph) > 1 and isinstance(ph[1], int):
            K.set_layer(ph[1])
        getattr(K, 'phase_' + ph[0])(*ph[1:])
    K.P.add('sp', None, K.outbufs, [])
    K.P.emit()
    return nc, K


def pack_sv(I, l):
    sv = np.zeros((SV_ROWS, 128), np.float32)

    def put(name, v):
        v = np.asarray(v, np.float32).reshape(-1)
        r0 = SV[name]
        n = (v.size + 127) // 128
        buf = np.zeros(n * 128, np.float32)
        buf[:v.size] = v
        sv[r0:r0 + n] = buf.reshape(n, 128)
    put('q_norm', I['mla_q_norm'][l]); put('kv_norm', I['mla_kv_norm'][l])
    put('qn_nope', I['mla_qn_nope'][l]); put('qn_rope', I['mla_qn_rope'][l])
    put('kn_nope', I['mla_kn_nope'][l]); put('kn_rope', I['mla_kn_rope'][l])
    put('out_norm', I['mla_out_norm'][l]); put('conv_w', I['hy_conv_w'][l]); put('conv_b', I['hy_conv_b'][l])
    put('skip', I['hy_skip'][l]); put('hy_norm', I['hy_out_norm'][l]); put('gn_w', I['ret_gn_w'][l])
    put('gn_b', I['ret_gn_b'][l]); put('b1', I['hy_ffn_b1'][l]); put('b2', I['hy_ffn_b2'][l])
    return sv


def hyena_consts(n):
    f32 = np.float32
    C = {}
    t = np.linspace(0.0, 1.0, n, dtype=f32)[:, None]
    w = (2 * math.pi * np.arange(n, dtype=f32)[:, None] / n).astype(f32)
    fb = np.linspace(1e-4, 15, 16, dtype=f32)[None, :]
    feats = np.concatenate([t, np.cos(fb * w), -np.sin(fb * w)], axis=-1).astype(f32)
    C[f'hy_feats{n}'] = np.ascontiguousarray(feats.T)
    deltas = np.abs(np.linspace(math.log(1e-2) / 1.5, math.log(1e-2) / 0.3, 512, dtype=f32))
    C[f'hy_window{n}'] = (np.exp(-t * deltas[None, :]) + 0.05).astype(f32)
    N2 = 2 * n
    fr = np.arange(n, dtype=np.int64)[:, None]
    tt = np.arange(n, dtype=np.int64)[None, :]
    ang = 2 * np.pi * ((fr * tt) % N2).astype(np.float64) / N2
    fwd = np.empty((2 * n, n), np.float64)
    fwd[:n] = np.cos(ang)
    fwd[n:] = -np.sin(ang)
    fwd[n] = np.cos(np.pi * np.arange(n))
    inv = np.empty((2 * n, n), np.float64)
    wf = np.full((n, 1), 2.0); wf[0] = 1.0
    inv[:n] = wf * np.cos(ang) / N2
    inv[n:] = -2.0 * np.sin(ang) / N2
    inv[n] = np.cos(np.pi * np.arange(n)) / N2
    ntc = n // 128
    nfc = 2 * ntc
    cf = fwd.reshape(nfc, 128, ntc, 128).transpose(0, 3, 2, 1)
    ci = inv.reshape(nfc, 128, ntc, 128).transpose(2, 1, 0, 3)
    C[f'hy_cfwd{n}'] = np.ascontiguousarray(cf).astype(ml_dtypes.bfloat16)
    C[f'hy_cinv{n}'] = np.ascontiguousarray(ci).astype(ml_dtypes.bfloat16)
    return C


def host_consts():
    C = {}
    inv = (10000.0 ** (-np.arange(16, dtype=np.float32) / 16)).astype(np.float32)
    t = np.arange(TL)
    ar = (t // 64).astype(np.float32)[None, :] * inv[:, None]
    ac = (t % 64).astype(np.float32)[None, :] * inv[:, None]
    C['rope_cos'] = np.concatenate([np.cos(ar), np.cos(ar), np.cos(ac), np.cos(ac)], 0).astype(np.float32)
    C['rope_sin'] = np.concatenate([np.sin(ar), np.sin(ar), np.sin(ac), np.sin(ac)], 0).astype(np.float32)
    pr = np.zeros((64, 64), np.float32)
    for base in (0, 32):
        for i in range(16):
            pr[base + 16 + i, base + i] = -1.0
            pr[base + i, base + 16 + i] = 1.0
    C['prot'] = pr
    j = np.arange(128, dtype=np.float32)[:, None]
    i = np.arange(128, dtype=np.float32)[None, :]
    dd = i - j
    C['ret_D'] = np.ascontiguousarray(np.stack([np.maximum(dd, 0), (dd >= 0).astype(np.float32),
                                                np.maximum(-dd, 0), (dd <= 0).astype(np.float32)], 1).astype(np.float32))
    eq = np.zeros((128, 128), np.float32)
    eq[0:64, :] = i + 1.0
    eq[64:128, :] = 128.0 - i
    C['ret_EQ'] = eq
    p = np.arange(128, dtype=np.float32)
    C['ret_E'] = np.ascontiguousarray(np.stack([127 - p, p, 255 - p, 128 + p], 1).astype(np.float32))
    for n in (2048, 256):
        C.update(hyena_consts(n))
    return C


def host_inputs(I, nlayers=2, ncores=8):
    f = lambda a: np.ascontiguousarray(np.asarray(a, np.float32))
    shared = {'ident': np.eye(128, dtype=np.float32)}
    shared.update(host_consts())
    for l in range(nlayers):
        shared[f'ada_w_{l}'] = f(I['ada_w'][l])
        shared[f'ada_b_{l}'] = f(I['ada_b'][l]).reshape(144, 128)
        for n in ('ffn1_gate', 'ffn1_up', 'ffn1_down', 'w_in', 'w_out', 'ffn2_gate', 'ffn2_up', 'ffn2_down'):
            shared[f'{n}_{l}'] = f(I[n][l])
        shared[f'wuq_{l}'] = f(I['mla_wuq'][l])
        shared[f'wukv_{l}'] = f(I['mla_wukv'][l])
        shared[f'hy_w1_{l}'] = f(I['hy_ffn_w1'][l])
        shared[f'hy_w2_{l}'] = f(I['hy_ffn_w2'][l])
        shared[f'hy_w3_{l}'] = f(I['hy_ffn_w3'][l])
        shared[f'sv_{l}'] = pack_sv(I, l)
        shared[f'hy_skip_{l}'] = f(I['hy_skip'][l]).reshape(1, 1024)
        shared[f'hy_convw_{l}'] = f(I['hy_conv_w'][l]).reshape(1, 3 * 1536)
        shared[f'hy_convb_{l}'] = f(I['hy_conv_b'][l]).reshape(1, 1536)
        shared[f'ret_decay_{l}'] = f(I['ret_decay'][l]).reshape(1, 8)
        shared[f'hy_normrow_{l}'] = f(I['hy_out_norm'][l]).reshape(1, 512)
    maps = []
    for b in range(ncores):
        m = dict(shared)
        m['x'] = f(I['x'][b])
        m['ctx'] = f(I['ctx'][b])
        cv = np.concatenate([f(I['c'][b]).reshape(16, 128), f(I['c_ctx']).reshape(16, 128)], axis=0)
        m['cvec'] = np.ascontiguousarray(cv)
        maps.append(m)
    return maps


_CACHE = {}


def kernel(**inputs):
    if 'nc' not in _CACHE:
        _CACHE['nc'] = build()
    nc, K = _CACHE['nc']
    maps = host_inputs(inputs)
    maps = [{k: m[k] for k in K.inputs} for m in maps]
    res = run_bass_kernel_spmd(nc, maps, core_ids=list(range(8)))
    return np.stack([np.asarray(r['out'], np.float32) for r in res.results], axis=0)


ZROW_C = 1
ZROW_L = 259
ZROWS = 2308


def tiles_all():
    return [(0, 256, 1)] + [(256 + i * 512, 512, 0) for i in range(4)]


class Kern2(Kern):
    def svcol(self, name, i=0, n=128):
        c = SV[name] + i
        return self.sv.ap[0:n, c:c + 1]

    def phase_sv(self, l):
        if not hasattr(self, 'sv'):
            self.off = self.mark
            self.sv = self.tile([128, SV_ROWS], F32, name='sv')
            self.persist()
        b = self.load_cols(None, self.W[l]['sv'], SV_ROWS)
        self.cp('dve', self.sv.ap, self.psum[b][:, 0:SV_ROWS], [self.pb[b]], self.sv.bufs)
        self.reset()

    def phase_inproj(self, l, sbs):
        W = self.W[l]
        w_in = W['w_in']
        Zkv = self.dscr("Zkv", [256, T]); Zkr = self.dscr("Zkr", [64, T]); ZrkT = self.dscr("ZrkT", [256, T], BF16)
        Zq = self.dscr("Zq", [512, T]); ZrqT = self.dscr("ZrqT", [256, T], BF16); Zg = self.dscr("Zg", [512, T])
        ZrkM = self.dscr("ZrkM", [T, 256], BF16); Zrv = self.dscr("Zrv", [T, 512], BF16)
        Zhy = self.dscr("Zhy", [ZROWS, 1536])
        fm = [(0, 128, Zkv, 0, 1.0), (128, 128, Zkv, 128, 1.0), (256, 64, Zkr, 0, 1.0),
              (320, 128, ZrkT, 0, 0.125), (448, 128, ZrkT, 128, 0.125)]
        fm += [(1088 + 128 * i, 128, Zq, 128 * i, 1.0) for i in range(4)]
        fm += [(1600, 128, ZrqT, 0, 1.0), (1728, 128, ZrqT, 128, 1.0)]
        fm += [(1856 + 128 * i, 128, Zg, 128 * i, 1.0) for i in range(4)]
        tmg = [(320, 256, 'rk'), (576, 512, 'rv'), (2368, 512, 'hy0'), (2880, 512, 'hy1'), (3392, 512, 'hy2')]
        zt = self.tile([128, 1536], F32, name='zt')
        self.memset('dve', zt.ap, 0.0, zt.bufs)
        for r in (0, 257, 258, 2307):
            self.dma('sp', Zhy[r:r + 1, :], zt.ap[0:1, :], zt.bufs, [])
        for tiles in sbs:
            SB = sum(n for (_, n, _) in tiles)
            T0 = tiles[0][0]
            hT = self.tile([128, NCH, SB], BF16, name='hT')
            scr = self.nm_scratch()
            for st_ in self.normmod_steps(tiles, hT, 1, scr):
                st_()
            wfm = [self.tile([128, NCH, 128], BF16, name=f'wfm{i}') for i in range(3)]
            stf = [self.tile([128, SB], F32, name=f'stf{i}') for i in range(2)]
            bi = 0
            for ci, (c0, M, dst, r0, sc) in enumerate(fm):
                w = wfm[ci % 3]
                self.dma('pool', w.ap[:, :, 0:M], rows(w_in)[:, :, c0:c0 + M], [], w.bufs)
                st = stf[ci % 2]
                isb = (dst.dtype == BF16)
                st_ap = st.ap.bitcast(BF16)[:, 0:SB] if isb else st.ap[:, 0:SB]
                off = 0
                for (_, n, _) in tiles:
                    b = bi % 6
                    bi += 1
                    for k in range(NCH):
                        self.mm(self.psum[b][0:M, 0:n], w.ap[:, k, 0:M], hT.ap[:, k, off:off + n], k == 0, k == NCH - 1,
                                w.bufs + hT.bufs, [self.pb[b]])
                    if bi % 2 == 0:
                        self.act(st_ap[0:M, off:off + n], self.psum[b][0:M, 0:n], AF.Copy, [self.pb[b]], st.bufs, scale=sc)
                    else:
                        self.ts('dve', st_ap[0:M, off:off + n], self.psum[b][0:M, 0:n], sc, None, ALU.mult, None,
                                [self.pb[b]], st.bufs)
                    off += n
                self.dma('sp', dst[r0:r0 + M, T0:T0 + SB], st_ap[0:M, :], st.bufs, [])
            wtm = [self.tile([128, NCH, 512], BF16, name=f'wtm{i}') for i in range(2)]
            stt_ = [self.tile([128, 512], F32, name=f'stt{i}') for i in range(3)]
            si = 0
            for gi, (c0, N, kind) in enumerate(tmg):
                w = wtm[gi % 2]
                self.dma('pool', w.ap[:, :, 0:N], rows(w_in)[:, :, c0:c0 + N], [], w.bufs)
                for tt_ in range(SB // 128):
                    tg = T0 + tt_ * 128
                    b = 6 + bi % 2
                    bi += 1
                    for k in range(NCH):
                        self.mm(self.psum[b][:, 0:N], hT.ap[:, k, tt_ * 128:(tt_ + 1) * 128], w.ap[:, k, 0:N],
                                k == 0, k == NCH - 1, w.bufs + hT.bufs, [self.pb[b]])
                    st = stt_[si % 3]
                    si += 1
                    if kind == 'rk':
                        o_ap = st.ap.bitcast(BF16)[:, 0:N]
                        self.ts('dve', o_ap, self.psum[b][:, 0:N], 0.125, None, ALU.mult, None, [self.pb[b]], st.bufs)
                        self.dma('sp', ZrkM[tg:tg + 128, :], o_ap, st.bufs, [])
                    elif kind == 'rv':
                        o_ap = st.ap.bitcast(BF16)[:, 0:N]
                        self.act(o_ap, self.psum[b][:, 0:N], AF.Copy, [self.pb[b]], st.bufs)
                        self.dma('sp', Zrv[tg:tg + 128, :], o_ap, st.bufs, [])
                    else:
                        hc = int(kind[2]) * 512
                        if si % 2 == 0:
                            self.act(st.ap, self.psum[b][:, 0:N], AF.Copy, [self.pb[b]], st.bufs)
                        else:
                            self.cp('dve', st.ap, self.psum[b][:, 0:N], [self.pb[b]], st.bufs)
                        zr = (ZROW_C + tg) if tg < 256 else (ZROW_L + tg - 256)
                        self.dma('sp', Zhy[zr:zr + 128, hc:hc + 512], st.ap, st.bufs, [])
            self.reset()


class Kern3(Kern2):
    def headnorm_item(self, proj, src, src_bufs, M, n, gaincol, rope_tok0, dst, ring):
        i = self.hn_i
        self.hn_i += 1
        sq, r, o, t1, t2 = ring[i % len(ring)]

        def A():
            if proj is not None:
                proj()
            self.act(sq.ap[0:M, 0:n], src, AF.Square, src_bufs, sq.bufs)

        def B():
            self.mm(self.psum[3][0:M, 0:n], self.onesb.ap[0:M, 0:M], sq.ap[0:M, 0:n], True, True,
                    self.onesb.bufs + sq.bufs, [self.pb[3]])
            self.sqrt_ms(r.ap[0:M, 0:n], self.psum[3][0:M, 0:n], 1.0 / M, [self.pb[3]], r.bufs)
            if rope_tok0 is None:
                self.stt('dve', o.ap[0:M, 0:n], src, gaincol, r.ap[0:M, 0:n], ALU.mult, ALU.mult,
                         src_bufs + r.bufs + self.sv.bufs, o.bufs)
            else:
                self.stt('dve', t1.ap[0:M, 0:n], src, gaincol, r.ap[0:M, 0:n], ALU.mult, ALU.mult,
                         src_bufs + r.bufs + self.sv.bufs, t1.bufs)

        def C():
            if rope_tok0 is not None:
                self.mm(self.psum[4][0:M, 0:n], self.prot.ap, t1.ap[0:M, 0:n], True, True, self.prot.bufs + t1.bufs, [self.pb[4]])

        def D_():
            if rope_tok0 is not None:
                self.tt('dve', t2.ap[0:M, 0:n], self.psum[4][0:M, 0:n], self.sin.ap[:, rope_tok0:rope_tok0 + n], ALU.mult,
                        [self.pb[4]] + self.sin.bufs, t2.bufs)
                self.tt('pool', t1.ap[0:M, 0:n], t1.ap[0:M, 0:n], self.cos.ap[:, rope_tok0:rope_tok0 + n], ALU.mult,
                        t1.bufs + self.cos.bufs, t1.bufs)
                self.tt('pool', o.ap[0:M, 0:n], t1.ap[0:M, 0:n], t2.ap[0:M, 0:n], ALU.add, t1.bufs + t2.bufs, o.bufs)
            self.dma('sp', dst, o.ap[0:M, 0:n], o.bufs, [])
        return (A, B, C, D_)

    def phase_mlaprep(self, l, with_ctx_q):
        W = self.W[l]
        Zkv = self.dscr("Zkv", [256, T]); Zkr = self.dscr("Zkr", [64, T]); Zq = self.dscr("Zq", [512, T])
        KN = self.dscr("KN", [8, 128, T], BF16); KR = self.dscr("KR", [64, T], BF16)
        VT = self.dscr("VT", [T, 1024], BF16)
        QN = self.dscr("QN", [8, 128, T], BF16); QR = self.dscr("QR", [8, 64, T], BF16)
        self.hn_i = 0
        wukv = self.tile([128, 2, 2048], BF16, name='wukv')
        wuq = self.tile([128, 4, 1536], BF16, name='wuq')
        self.dma('pool', wukv.ap, rows(W['wukv']), [], wukv.bufs)
        self.dma('pool', wuq.ap, rows(W['wuq']), [], wuq.bufs)
        self.cos = self.tile([64, 2048], F32, name='cos')
        self.sin = self.tile([64, 2048], F32, name='sin')
        self.prot = self.tile([64, 64], F32, name='prot')
        self.dma('sp', self.cos.ap, self.din("rope_cos", [64, 2048]), [], self.cos.bufs)
        self.dma('sp', self.sin.ap, self.din("rope_sin", [64, 2048]), [], self.sin.bufs)
        self.dma('sp', self.prot.ap, self.din("prot", [64, 64]), [], self.prot.bufs)
        ring = []
        for i in range(5):
            ring.append((self.tile([128, 512], BF16, name='hn_sq'),
                         self.tile([128, 512], F32, name='hn_r'), self.tile([128, 512], BF16, name='hn_o'),
                         self.tile([128, 512], F32, name='hn_t1'), self.tile([128, 512], F32, name='hn_t2')))
        kvs = [self.tile([128, 4, 512], F32, name=f'kv{i}') for i in range(2)]
        sqs = [self.tile([128, 4, 512], BF16, name=f'kvsq{i}') for i in range(2)]
        rrs = [self.tile([128, 512], F32, name=f'kvr{i}') for i in range(2)]
        kvns = [self.tile([128, 4, 512], BF16, name=f'kvn{i}') for i in range(2)]
        krs = [self.tile([64, 512], F32, name=f'kr{i}') for i in range(2)]
        vst = [self.tile([128, 1024], BF16, name=f'vst{i}') for i in range(2)]
        wv = wukv.ap.rearrange("p c (h e) -> p c h e", e=256)
        units = []
        for (t0, n, cond) in tiles_all():
            for side in ('k', 'q'):
                if side == 'q' and cond == 1 and not with_ctx_q:
                    continue
                units.append((t0, n, cond, side))
        self._pi = 0
        self._vi = 0

        def make_unit(ui, t0, n, cond, side):
            lat0 = None if cond == 1 else t0 - 256
            nc_ = 2 if side == 'k' else 4
            src = Zkv if side == 'k' else Zq
            kv, sq, rr, kvn = kvs[ui % 2], sqs[ui % 2], rrs[ui % 2], kvns[ui % 2]
            gname = 'kv_norm' if side == 'k' else 'q_norm'

            def P1():
                self.dma('sp', kv.ap[:, 0:nc_, 0:n], rows(src)[:, :, t0:t0 + n], [], kv.bufs)
                self.act(sq.ap[:, 0:nc_, 0:n], kv.ap[:, 0:nc_, 0:n], AF.Square, kv.bufs, sq.bufs)
                if side == 'k':
                    kr = krs[ui % 2]
                    self.dma('sp', kr.ap[:, 0:n], Zkr[:, t0:t0 + n], [], kr.bufs)

            def P2():
                for c in range(nc_):
                    self.mm(self.psum[5][:, 0:n], self.onesb.ap, sq.ap[:, c, 0:n], c == 0, c == nc_ - 1,
                            self.onesb.bufs + sq.bufs, [self.pb[5]])
                self.sqrt_ms(rr.ap[:, 0:n], self.psum[5][:, 0:n], 1.0 / (128 * nc_), [self.pb[5]], rr.bufs)
                for c in range(nc_):
                    self.stt('dve', kvn.ap[:, c, 0:n], kv.ap[:, c, 0:n], self.svcol(gname, c), rr.ap[:, 0:n], ALU.mult, ALU.mult,
                             kv.bufs + rr.bufs + self.sv.bufs, kvn.bufs)
            items = []

            def proj_fn(b, M, w_t, nc2, col0):
                def f():
                    for c in range(nc2):
                        self.mm(self.psum[b][0:M, 0:n], w_t.ap[:, c, col0:col0 + M], kvn.ap[:, c, 0:n], c == 0, c == nc2 - 1,
                                w_t.bufs + kvn.bufs, [self.pb[b]])
                return f
            if side == 'k':
                for h in range(8):
                    b = self._pi % 3
                    self._pi += 1
                    items.append(self.headnorm_item(proj_fn(b, 128, wukv, 2, h * 256), self.psum[b][:, 0:n], [self.pb[b]], 128, n,
                                                    self.svcol('kn_nope'), None, KN[h, :, t0:t0 + n], ring))
                kr = krs[ui % 2]
                items.append(self.headnorm_item(None, kr.ap[:, 0:n], kr.bufs, 64, n, self.svcol('kn_rope', 0, 64), lat0,
                                                KR[:, t0:t0 + n], ring))

                def vfn(u):
                    def A():
                        st = vst[self._vi % 2]
                        self._vi += 1
                        for half in range(2):
                            b = 6 + half
                            for c in range(2):
                                self.mm(self.psum[b][:, :].rearrange("p (h e) -> p h e", e=128),
                                        kvn.ap[:, c, u * 128:(u + 1) * 128], wv[:, c, half * 4:(half + 1) * 4, 128:256],
                                        c == 0, c == 1, wukv.bufs + kvn.bufs, [self.pb[b]])
                            if half == 0:
                                self.cp('act', st.ap[:, 0:512], self.psum[b][:, :], [self.pb[b]], st.bufs)
                            else:
                                self.cp('dve', st.ap[:, 512:1024], self.psum[b][:, :], [self.pb[b]], st.bufs)
                        self.dma('sp', VT[t0 + u * 128:t0 + (u + 1) * 128, :], st.ap, st.bufs, [])
                    return (A, lambda: None, lambda: None, lambda: None)
                for u in range(n // 128):
                    items.append(vfn(u))
            else:
                for h in range(8):
                    b = self._pi % 3
                    self._pi += 1
                    items.append(self.headnorm_item(proj_fn(b, 128, wuq, 4, h * 192), self.psum[b][:, 0:n], [self.pb[b]], 128, n,
                                                    self.svcol('qn_nope'), None, QN[h, :, t0:t0 + n], ring))
                    b = self._pi % 3
                    self._pi += 1
                    items.append(self.headnorm_item(proj_fn(b, 64, wuq, 4, h * 192 + 128), self.psum[b][0:64, 0:n], [self.pb[b]], 64, n,
                                                    self.svcol('qn_rope', 0, 64), lat0, QR[h, :, t0:t0 + n], ring))
            return P1, P2, items

        U = [make_unit(ui, *u) for ui, u in enumerate(units)]
        U[0][0]()
        U[0][1]()
        for ui, (P1, P2, items) in enumerate(U):
            nxt = U[ui + 1] if ui + 1 < len(U) else None
            if nxt:
                nxt[0]()
            ni = len(items)
            for sidx in range(ni + 3):
                if sidx < ni:
                    items[sidx][0]()
                if 0 <= sidx - 1 < ni:
                    items[sidx - 1][1]()
                if 0 <= sidx - 2 < ni:
                    items[sidx - 2][2]()
                if 0 <= sidx - 3 < ni:
                    items[sidx - 3][3]()
                if nxt and sidx == ni // 2:
                    nxt[1]()
        self.reset()

    def phase_attn(self, l, with_ctx_q):
        KN = self.dscr("KN", [8, 128, T], BF16); KR = self.dscr("KR", [64, T], BF16)
        VT = self.dscr("VT", [T, 1024], BF16)
        QN = self.dscr("QN", [8, 128, T], BF16); QR = self.dscr("QR", [8, 64, T], BF16)
        MG = self.dscr("MG", [D, T], BF16)
        kn = self.tile([128, 8, T], BF16, name='kn')
        kr = self.tile([64, T], BF16, name='kr')
        v = self.tile([128, 18, 1024], BF16, name='v')
        self.dma('sp', kn.ap, KN.rearrange("h p t -> p h t"), [], kn.bufs)
        self.dma('sp', kr.ap, KR, [], kr.bufs)
        self.dma('sp', v.ap, rows(VT), [], v.bufs)
        qn = [self.tile([128, 512], BF16, name=f'qn{i}') for i in range(2)]
        qr = [self.tile([64, 512], BF16, name=f'qr{i}') for i in range(2)]
        pt = [self.tile([128, 512], BF16, name=f'pt{i}') for i in range(4)]
        oall = [self.tile([128, 8, 512], F32, name=f'oall{i}') for i in range(2)]
        osq = self.tile([128, 8, 512], BF16, name='osq')
        rinv = [self.tile([128, 512], F32, name=f'rinv{i}') for i in range(2)]
        rr = self.tile([128, 512], F32, name='arr')
        mst = [self.tile([128, 8, 512], BF16, name=f'mst{i}') for i in range(2)]
        scale = 192 ** -0.5
        qi = 0
        pi = 0
        qtiles = tiles_all() if with_ctx_q else tiles_all()[1:]
        LA = 3
        hq = 0
        for ti, (t0, n, cond) in enumerate(qtiles):
            nkc = 2 if cond == 1 else 18
            oa = oall[ti % 2]
            steps = [(h, kc) for h in range(8) for kc in range(nkc)]
            hq0 = hq

            def ld_q(h_, slot):
                if h_ < 8:
                    tt0, nn = t0, n
                elif ti + 1 < len(qtiles):
                    tt0, nn, h_ = qtiles[ti + 1][0], qtiles[ti + 1][1], 0
                else:
                    return
                self.dma('sp', qn[slot % 2].ap[:, 0:nn], QN[h_, :, tt0:tt0 + nn], [], qn[slot % 2].bufs)
                self.dma('sp', qr[slot % 2].ap[:, 0:nn], QR[h_, :, tt0:tt0 + nn], [], qr[slot % 2].bufs)

            def emit_S(i):
                h, kc = steps[i]
                q1, q2 = qn[(hq0 + h) % 2], qr[(hq0 + h) % 2]
                if ti == 0 and h == 0 and kc == 0:
                    ld_q(0, hq0)
                if kc == nkc // 2:
                    ld_q(h + 1, hq0 + h + 1)
                sb_ = (pi0 + i) % 4
                self.mm(self.psum[sb_][:, 0:n], kn.ap[:, h, kc * 128:(kc + 1) * 128], q1.ap[:, 0:n], True, False,
                        kn.bufs + q1.bufs, [self.pb[sb_]])
                self.mm(self.psum[sb_][:, 0:n], kr.ap[:, kc * 128:(kc + 1) * 128], q2.ap[:, 0:n], False, True,
                        kr.bufs + q2.bufs, [self.pb[sb_]])
                p_ = pt[(pi0 + i) % 4]
                self.act(p_.ap[:, 0:n], self.psum[sb_][:, 0:n], AF.Exp, [self.pb[sb_]], p_.bufs, scale=scale)

            def emit_PV(i):
                h, kc = steps[i]
                p_ = pt[(pi0 + i) % 4]
                ob, lb = 4 + ((hq0 + h) % 2), 6 + ((hq0 + h) % 2)
                self.mm(self.psum[ob][:, 0:n], v.ap[:, kc, h * 128:(h + 1) * 128], p_.ap[:, 0:n], kc == 0, kc == nkc - 1,
                        v.bufs + p_.bufs, [self.pb[ob]])
                self.mm(self.psum[lb][:, 0:n], self.onesb.ap, p_.ap[:, 0:n], kc == 0, kc == nkc - 1,
                        self.onesb.bufs + p_.bufs, [self.pb[lb]])
                if kc == nkc - 1:
                    ri = rinv[(hq0 + h) % 2]
                    self.P.add('dve', (lambda o_, i_: (lambda e: e.reciprocal(o_, i_)))(ri.ap[:, 0:n], self.psum[lb][:, 0:n]),
                               [self.pb[lb]], ri.bufs)
                    self.tt('dve', oa.ap[:, h, 0:n], self.psum[ob][:, 0:n], ri.ap[:, 0:n], ALU.mult, [self.pb[ob]] + ri.bufs, oa.bufs)

            pi0 = pi
            ns = len(steps)
            for i in range(min(LA, ns)):
                emit_S(i)
            for i in range(ns):
                emit_PV(i)
                if i + LA < ns:
                    emit_S(i + LA)
            pi += ns
            hq += 8
            qi = hq
            self.act(osq.ap[:, :, 0:n], oa.ap[:, :, 0:n], AF.Square, oa.bufs, osq.bufs)
            nb_ = 4 + (qi % 2)
            for h in range(8):
                self.mm(self.psum[nb_][:, 0:n], self.onesb.ap, osq.ap[:, h, 0:n], h == 0, h == 7,
                        self.onesb.bufs + osq.bufs, [self.pb[nb_]])
            self.rstd(rr.ap[:, 0:n], self.psum[nb_][:, 0:n], 1.0 / 1024, [self.pb[nb_]], rr.bufs)
            ms = mst[ti % 2]
            for h in range(8):
                self.stt('dve' if h % 2 == 0 else 'pool', ms.ap[:, h, 0:n], oa.ap[:, h, 0:n], self.svcol('out_norm', h), rr.ap[:, 0:n],
                         ALU.mult, ALU.mult, oa.bufs + rr.bufs + self.sv.bufs, ms.bufs)
            self.dma('sp', rows(MG)[:, 0:8, t0:t0 + n], ms.ap[:, :, 0:n], ms.bufs, [])
        self.reset()


class Kern4(Kern3):
    def phase_ret(self, l, with_ctx):
        W = self.W[l]
        ZrqT = self.dscr("ZrqT", [256, T], BF16); ZrkT = self.dscr("ZrkT", [256, T], BF16)
        ZrkM = self.dscr("ZrkM", [T, 256], BF16); Zrv = self.dscr("Zrv", [T, 512], BF16)
        Zg = self.dscr("Zg", [512, T]); MG = self.dscr("MG", [D, T], BF16)
        dtab = self.tile([128, 4, 128], F32, name='ret_D')
        eq = self.tile([128, 128], F32, name='ret_EQ')
        ecol = self.tile([128, 4], F32, name='ret_E')
        self.dma('sp', dtab.ap, self.din("ret_D", [128, 4, 128]), [], dtab.bufs)
        self.dma('sp', eq.ap, self.din("ret_EQ", [128, 128]), [], eq.bufs)
        self.dma('sp', ecol.ap, self.din("ret_E", [128, 4]), [], ecol.bufs)
        rd = self.tile([128, 8], F32, name='rd')
        self.dma('sp', rd.ap, W['ret_decay'].partition_broadcast(128).rearrange("p a b -> p (a b)"), [], rd.bufs)
        lg = self.tile([128, 8], F32, name='lg')
        self.act(lg.ap, rd.ap, AF.Exp, rd.bufs, lg.bufs, scale=-1.0)
        self.act(lg.ap, lg.ap, AF.Ln, lg.bufs + self.onesf.bufs, lg.bufs, bias=self.onesf.ap[:, 0:1])
        self.ts('dve', lg.ap, lg.ap, -1.0, None, ALU.mult, None, lg.bufs, lg.bufs)
        lgc = self.tile([128, 4], F32, name='lgc')
        self.cp('dve', lgc.ap[0:64, :], lg.ap[0:64, 0:4], lg.bufs, lgc.bufs)
        self.cp('dve', lgc.ap[64:128, :], lg.ap[64:128, 4:8], lg.bufs, lgc.bufs)
        Mt = self.tile([128, 4, 128], F32, name='Mt')
        mtmp = self.tile([128, 128], F32, name='mtmp')
        qdec = self.tile([128, 4, 128], F32, name='qdec')
        kdec = self.tile([128, 4, 4], F32, name='kdec')
        cdec = self.tile([128, 4], F32, name='cdec')
        for h in range(4):
            self.act(Mt.ap[:, h, :], dtab.ap[:, 0, :], AF.Exp, dtab.bufs + lg.bufs, Mt.bufs, scale=lg.ap[:, h:h + 1])
            self.tt('dve', Mt.ap[:, h, :], Mt.ap[:, h, :], dtab.ap[:, 1, :], ALU.mult, Mt.bufs + dtab.bufs, Mt.bufs)
            self.act(mtmp.ap, dtab.ap[:, 2, :], AF.Exp, dtab.bufs + lg.bufs, mtmp.bufs, scale=lg.ap[:, 4 + h:5 + h])
            self.tt('dve', mtmp.ap, mtmp.ap, dtab.ap[:, 3, :], ALU.mult, mtmp.bufs + dtab.bufs, mtmp.bufs)
            self.tt('dve', Mt.ap[:, h, :], Mt.ap[:, h, :], mtmp.ap, ALU.add, Mt.bufs + mtmp.bufs, Mt.bufs)
            self.act(qdec.ap[:, h, :], eq.ap, AF.Exp, eq.bufs + lgc.bufs, qdec.bufs, scale=lgc.ap[:, h:h + 1])
            for j, (ec, d) in enumerate([(0, 0), (1, 1), (2, 0), (3, 1)]):
                self.act(kdec.ap[:, h, j:j + 1], ecol.ap[:, ec:ec + 1], AF.Exp, ecol.bufs + lg.bufs, kdec.bufs,
                         scale=lg.ap[:, d * 4 + h:d * 4 + h + 1])
        self.act(cdec.ap, lgc.ap, AF.Exp, lgc.bufs, cdec.bufs, scale=128.0)
        kMc = self.tile([128, 2, 256], BF16, name='kMc')
        vc = self.tile([128, 2, 512], BF16, name='vc')
        self.dma('sp', kMc.ap, rows(ZrkM)[:, 0:2, :], [], kMc.bufs)
        self.dma('sp', vc.ap, rows(Zrv)[:, 0:2, :], [], vc.bufs)
        sinit = self.tile([128, 4, 128], F32, name='sinit')
        kdc = self.tile([128, 2, 128], BF16, name='kdc')
        for h in range(4):
            for c, (cf, cb) in enumerate([(2, 1), (0, 3)]):
                self.ts('dve', kdc.ap[:, c, 0:64], kMc.ap[:, c, h * 64:(h + 1) * 64], kdec.ap[:, h, cf:cf + 1], None, ALU.mult, None,
                        kMc.bufs + kdec.bufs, kdc.bufs)
                self.ts('dve', kdc.ap[:, c, 64:128], kMc.ap[:, c, h * 64:(h + 1) * 64], kdec.ap[:, h, cb:cb + 1], None, ALU.mult, None,
                        kMc.bufs + kdec.bufs, kdc.bufs)
            b = h % 2
            for c in range(2):
                self.mm(self.psum[b][:, 0:128], kdc.ap[:, c, :], vc.ap[:, c, h * 128:(h + 1) * 128], c == 0, c == 1,
                        kdc.bufs + vc.bufs, [self.pb[b]])
            self.cp('act', sinit.ap[:, h, :], self.psum[b][:, 0:128], [self.pb[b]], sinit.bufs)
        modq = []
        mod_epi = None
        if l + 1 < self.nl and not self.dbg.get('no_modov'):
            pro, modq, mod_epi = self.mod_steps(l + 1, 3)
            pro()
        base = self.off
        seqs = [(256, 16, True)] + ([(0, 2, False)] if with_ctx else [])
        bi = 0
        for (t0, nch, use_init) in seqs:
            self.off = base
            n = nch * 128
            kM = self.tile([128, nch, 256], BF16, name='kM')
            v = self.tile([128, nch, 512], BF16, name='v')
            self.dma('sp', kM.ap, rows(ZrkM)[:, t0 // 128:t0 // 128 + nch, :], [], kM.bufs)
            self.dma('sp', v.ap, rows(Zrv)[:, t0 // 128:t0 // 128 + nch, :], [], v.bufs)
            qq = [self.tile([128, n], BF16, name=f'qq{i}') for i in range(2)]
            kk = [self.tile([64, n], BF16, name=f'kk{i}') for i in range(2)]
            qd = [self.tile([128, n], BF16, name=f'qd{i}') for i in range(2)]
            kd = [self.tile([128, nch, 128], BF16, name=f'kd{i}') for i in range(2)]
            kvs = [self.tile([128, nch, 128], F32, name=f'kvs{i}') for i in range(2)]
            sst = [self.tile([128, nch, 128], F32, name=f'sst{i}') for i in range(2)]
            sstb = [self.tile([128, nch, 128], BF16, name=f'sstb{i}') for i in range(2)]
            sm = [self.tile([128, 4, 128], BF16, name=f'sm{i}') for i in range(2)]
            oT = [self.tile([128, 512], F32, name=f'oT{i}') for i in range(2)]
            osq = [self.tile([128, 512], F32, name=f'osq{i}') for i in range(2)]
            mu = [self.tile([128, 512], F32, name=f'mu{i}') for i in range(2)]
            var = [self.tile([128, 512], F32, name=f'var{i}') for i in range(2)]
            gt = [self.tile([128, 512], F32, name=f'gt{i}') for i in range(2)]
            og = [self.tile([128, 512], BF16, name=f'og{i}') for i in range(2)]
            gi = 0
            for h in range(4):
                q_, k_, qd_, kd_, kvs_, sst_, sstb_ = qq[h % 2], kk[h % 2], qd[h % 2], kd[h % 2], kvs[h % 2], sst[h % 2], sstb[h % 2]
                self.dma('sp', q_.ap[0:64, :], ZrqT[h * 64:(h + 1) * 64, t0:t0 + n], [], q_.bufs)
                self.dma('sp', q_.ap[64:128, :], ZrqT[h * 64:(h + 1) * 64, t0:t0 + n], [], q_.bufs)
                self.dma('sp', k_.ap, ZrkT[h * 64:(h + 1) * 64, t0:t0 + n], [], k_.bufs)
                self.tt('dve', qd_.ap.rearrange("p (c i) -> p c i", i=128), q_.ap.rearrange("p (c i) -> p c i", i=128),
                        qdec.ap[:, h, :].unsqueeze(1).to_broadcast([128, nch, 128]), ALU.mult, q_.bufs + qdec.bufs, qd_.bufs)
                self.ts('dve', kd_.ap[:, :, 0:64], kM.ap[:, :, h * 64:(h + 1) * 64], kdec.ap[:, h, 0:1], None, ALU.mult, None,
                        kM.bufs + kdec.bufs, kd_.bufs)
                self.ts('dve', kd_.ap[:, :, 64:128], kM.ap[:, :, h * 64:(h + 1) * 64], kdec.ap[:, h, 1:2], None, ALU.mult, None,
                        kM.bufs + kdec.bufs, kd_.bufs)
                for g in range((nch + 3) // 4):
                    b = bi % 2
                    bi += 1
                    ng = min(4, nch - g * 4)
                    for c4 in range(ng):
                        c = g * 4 + c4
                        self.mm(self.psum[b][:, c4 * 128:(c4 + 1) * 128], kd_.ap[:, c, :], v.ap[:, c, h * 128:(h + 1) * 128], True, True,
                                kd_.bufs + v.bufs, [self.pb[b]])
                    self.cp('act' if g % 2 == 0 else 'dve', kvs_.ap[:, g * 4:g * 4 + ng, :],
                            self.psum[b][:, 0:ng * 128].rearrange("p (c d) -> p c d", d=128), [self.pb[b]], kvs_.bufs)
                if use_init:
                    self.cp('dve', sst_.ap[0:64, 0, :], sinit.ap[0:64, h, :], sinit.bufs, sst_.bufs)
                    self.cp('dve', sst_.ap[64:128, nch - 1, :], sinit.ap[64:128, h, :], sinit.bufs, sst_.bufs)
                else:
                    self.memset('dve', sst_.ap[0:64, 0, :], 0.0, sst_.bufs)
                    self.memset('dve', sst_.ap[64:128, nch - 1, :], 0.0, sst_.bufs)
                for c in range(nch - 1):
                    self.stt('dve', sst_.ap[0:64, c + 1, :], sst_.ap[0:64, c, :], cdec.ap[0:64, h:h + 1], kvs_.ap[0:64, c, :],
                             ALU.mult, ALU.add, sst_.bufs + cdec.bufs + kvs_.bufs, sst_.bufs)
                    cb = nch - 1 - c
                    self.stt('dve', sst_.ap[64:128, cb - 1, :], sst_.ap[64:128, cb, :], cdec.ap[64:128, h:h + 1], kvs_.ap[64:128, cb, :],
                             ALU.mult, ALU.add, sst_.bufs + cdec.bufs + kvs_.bufs, sst_.bufs)
                self.cp('act', sstb_.ap, sst_.ap, sst_.bufs, sstb_.bufs)
                for g in range((nch + 3) // 4):
                    ng = min(4, nch - g * 4)
                    nt = ng * 128
                    tg = t0 + g * 512
                    sb_ = 2 if modq or mod_epi else 2 + gi % 2
                    ob = 4 + gi % 2
                    for _ in range(2):
                        if modq:
                            modq.pop(0)()
                    sm_, oT_, osq_, mu_, var_, gt_, og_ = sm[gi % 2], oT[gi % 2], osq[gi % 2], mu[gi % 2], var[gi % 2], gt[gi % 2], og[gi % 2]
                    gi += 1
                    self.dma('sp', gt_.ap[:, 0:nt], Zg[h * 128:(h + 1) * 128, tg:tg + nt], [], gt_.bufs)
                    for c4 in range(ng):
                        c = g * 4 + c4
                        self.mm(self.psum[sb_][:, c4 * 128:(c4 + 1) * 128], k_.ap[0:64, c * 128:(c + 1) * 128],
                                q_.ap[0:64, c * 128:(c + 1) * 128], True, True, k_.bufs + q_.bufs, [self.pb[sb_]])
                    self.tt('dve', sm_.ap[:, 0:ng, :], self.psum[sb_][:, 0:nt].rearrange("p (c i) -> p c i", i=128),
                            Mt.ap[:, h, :].unsqueeze(1).to_broadcast([128, ng, 128]), ALU.mult, [self.pb[sb_]] + Mt.bufs, sm_.bufs)
                    for c4 in range(ng):
                        c = g * 4 + c4
                        self.mm(self.psum[ob][:, c4 * 128:(c4 + 1) * 128], v.ap[:, c, h * 128:(h + 1) * 128], sm_.ap[:, c4, :], True, False,
                                v.bufs + sm_.bufs, [self.pb[ob]])
                        self.mm(self.psum[ob][:, c4 * 128:(c4 + 1) * 128], sstb_.ap[:, c, :], qd_.ap[:, c * 128:(c + 1) * 128], False, True,
                                sstb_.bufs + qd_.bufs, [self.pb[ob]])
                    self.cp('act', oT_.ap[:, 0:nt], self.psum[ob][:, 0:nt], [self.pb[ob]], oT_.bufs)
                    self.act(osq_.ap[:, 0:nt], oT_.ap[:, 0:nt], AF.Square, oT_.bufs, osq_.bufs)
                    self.mm(self.psum[6][:, 0:nt], self.onesf.ap, oT_.ap[:, 0:nt], True, True, self.onesf.bufs + oT_.bufs, [self.pb[6]])
                    self.mm(self.psum[7][:, 0:nt], self.onesf.ap, osq_.ap[:, 0:nt], True, True, self.onesf.bufs + osq_.bufs, [self.pb[7]])
                    self.ts('dve', mu_.ap[:, 0:nt], self.psum[6][:, 0:nt], 1.0 / 128, None, ALU.mult, None, [self.pb[6]], mu_.bufs)
                    self.tt('dve', var_.ap[:, 0:nt], mu_.ap[:, 0:nt], mu_.ap[:, 0:nt], ALU.mult, mu_.bufs, var_.bufs)
                    self.stt('dve', var_.ap[:, 0:nt], self.psum[7][:, 0:nt], 1.0 / 128, var_.ap[:, 0:nt], ALU.mult, ALU.subtract,
                             [self.pb[7]] + var_.bufs, var_.bufs)
                    self.rstd(var_.ap[:, 0:nt], var_.ap[:, 0:nt], 1.0, var_.bufs, var_.bufs)
                    self.tt('dve', oT_.ap[:, 0:nt], oT_.ap[:, 0:nt], mu_.ap[:, 0:nt], ALU.subtract, oT_.bufs + mu_.bufs, oT_.bufs)
                    self.tt('pool', oT_.ap[:, 0:nt], oT_.ap[:, 0:nt], var_.ap[:, 0:nt], ALU.mult, oT_.bufs + var_.bufs, oT_.bufs)
                    self.act(oT_.ap[:, 0:nt], oT_.ap[:, 0:nt], AF.Identity, oT_.bufs + self.sv.bufs, oT_.bufs,
                             scale=self.svcol('gn_w', h), bias=self.svcol('gn_b', h))
                    self.act(gt_.ap[:, 0:nt], gt_.ap[:, 0:nt], AF.Silu, gt_.bufs, gt_.bufs)
                    self.tt('pool', og_.ap[:, 0:nt], oT_.ap[:, 0:nt], gt_.ap[:, 0:nt], ALU.mult, oT_.bufs + gt_.bufs, og_.bufs)
                    self.dma('sp', MG[1536 + h * 128:1536 + (h + 1) * 128, tg:tg + nt], og_.ap[:, 0:nt], og_.bufs, [])
        while modq:
            modq.pop(0)()
        if mod_epi:
            mod_epi()
        self.reset()


TWO_PI = 2.0 * math.pi


class Kern5(Kern4):
    def wrap_sin(self, x, M, n, bufs):
        m1 = self.hy_m1
        self.ts('dve', m1.ap[0:M, 0:n], x, math.pi, -TWO_PI, ALU.is_gt, ALU.mult, bufs, m1.bufs)
        self.tt('dve', x, x, m1.ap[0:M, 0:n], ALU.add, bufs + m1.bufs, bufs)
        self.ts('dve', m1.ap[0:M, 0:n], x, -math.pi, TWO_PI, ALU.is_lt, ALU.mult, bufs, m1.bufs)
        self.tt('dve', x, x, m1.ap[0:M, 0:n], ALU.add, bufs + m1.bufs, bufs)
        self.act(x, x, AF.Sin, bufs, bufs)

    def phase_hyena(self, l, seq):
        W = self.W[l]
        if seq == 'lat':
            n, tg0, zr0 = 2048, 256, ZROW_L
        else:
            n, tg0, zr0 = 256, 0, ZROW_C
        ntc = n // 128
        nfc = 2 * ntc
        npair = ntc
        sfx = str(n)
        Zhy = self.dscr("Zhy", [ZROWS, 1536]); MG = self.dscr("MG", [D, T], BF16)
        HS = self.dscr("HS" + sfx, [2, 2, n, 512])
        HC = self.dscr("HC" + sfx, [2, 128, 512])
        HX = self.dscr("HX" + sfx, [2, n, 512])
        featsT = self.din("hy_feats" + sfx, [33, n])
        window = self.din("hy_window" + sfx, [n, 512])
        Cfwd = self.din("hy_cfwd" + sfx, [nfc, 128, ntc, 128], BF16)
        Cinv = self.din("hy_cinv" + sfx, [ntc, 128, nfc, 128], BF16)
        base0 = self.off
        HV = self.dscr("HV" + sfx, [n, 512])
        u16 = self.tile([128, ntc, 512], BF16, name='u16')
        base_u16 = self.off
        hfilt = self.tile([128, ntc, 2048], BF16, name='hfilt')
        base_tmp = self.off
        w1s = self.tile([33, 64], F32, name='w1s'); w2s = self.tile([64, 64], F32, name='w2s')
        w3s = self.tile([64, 2048], F32, name='w3s'); ft = self.tile([33, n], F32, name='ft')
        self.dma('sp', w1s.ap, W['hy_w1'], [], w1s.bufs); self.dma('sp', w2s.ap, W['hy_w2'], [], w2s.bufs)
        self.dma('sp', w3s.ap, W['hy_w3'], [], w3s.bufs); self.dma('sp', ft.ap, featsT, [], ft.bufs)
        h2T = self.tile([64, n], F32, name='h2T')
        h1 = self.tile([64, 512], F32, name='h1')
        self.hy_m1 = self.tile([64, 512], F32, name='hy_m1')
        for c0 in range(0, n, 512):
            nt = min(512, n - c0)
            self.mm(self.psum[0][0:64, 0:nt], w1s.ap, ft.ap[:, c0:c0 + nt], True, True, w1s.bufs + ft.bufs, [self.pb[0]])
            self.ts('dve', h1.ap[:, 0:nt], self.psum[0][0:64, 0:nt], self.svcol('b1', 0, 64), None, ALU.add, None,
                    [self.pb[0]] + self.sv.bufs, h1.bufs)
            self.wrap_sin(h1.ap[:, 0:nt], 64, nt, h1.bufs)
            self.mm(self.psum[1][0:64, 0:nt], w2s.ap, h1.ap[:, 0:nt], True, True, w2s.bufs + h1.bufs, [self.pb[1]])
            self.ts('dve', h2T.ap[:, c0:c0 + nt], self.psum[1][0:64, 0:nt], self.svcol('b2', 0, 64), None, ALU.add, None,
                    [self.pb[1]] + self.sv.bufs, h2T.bufs)
            self.wrap_sin(h2T.ap[:, c0:c0 + nt], 64, nt, h2T.bufs)
        h2b = self.tile([64, n], BF16, name='h2b')
        w3b = self.tile([64, 2048], BF16, name='w3b')
        self.cp('act', h2b.ap, h2T.ap, h2T.bufs, h2b.bufs)
        self.dma('pool', w3b.ap, W['hy_w3'], [], w3b.bufs)
        win = [self.tile([128, 512], F32, name=f'win{i}') for i in range(2)]
        ftmp = [[self.tile([128, 512], F32, name=f'ftmp{i}{g}') for g in range(4)] for i in range(2)]
        bi = 0
        for tc in range(ntc):
            wt = win[tc % 2]
            ft4 = ftmp[tc % 2]
            self.dma('sp', wt.ap, window[tc * 128:(tc + 1) * 128, :], [], wt.bufs)
            for g in range(4):
                b = bi % 4
                bi += 1
                self.mm(self.psum[b][:, :], h2b.ap[:, tc * 128:(tc + 1) * 128], w3b.ap[:, g * 512:(g + 1) * 512], True, True,
                        h2b.bufs + w3b.bufs, [self.pb[b]])
                self.tt('dve', ft4[g].ap, self.psum[b][:, :], wt.ap, ALU.mult, [self.pb[b]] + wt.bufs, ft4[g].bufs)
            for o in range(2):
                self.tt('dve', hfilt.ap[:, tc, o * 512:(o + 1) * 512], ft4[o].ap, ft4[2 + o].ap, ALU.add,
                        ft4[o].bufs + ft4[2 + o].bufs, hfilt.bufs)
                self.tt('dve', hfilt.ap[:, tc, (2 + o) * 512:(3 + o) * 512], ft4[o].ap, ft4[2 + o].ap, ALU.subtract,
                        ft4[o].bufs + ft4[2 + o].bufs, hfilt.bufs)
        if self.dbg.get('hy_stop') == 'A':
            self.reset()
            return
        self.reset()
        self.off = base_tmp
        slab = [self.tile([128, ntc, 128], BF16, name=f'fslab{i}') for i in range(3)]
        ob_ = [self.tile([128, 512], F32, name=f'hso{i}') for i in range(6)]
        cw = self.tile([128, 3, 1536], F32, name='cw')
        cb = self.tile([128, 1536], F32, name='cb')
        self.dma('sp', cw.ap.rearrange("p a b -> p (a b)"), W['hy_convw'].partition_broadcast(128).rearrange("p a b -> p (a b)"), [], cw.bufs)
        self.dma('sp', cb.ap, W['hy_convb'].partition_broadcast(128).rearrange("p a b -> p (a b)"), [], cb.bufs)
        zz = [[self.tile([128, 1536], F32, name=f'zz{i}{j}') for j in range(3)] for i in range(2)]
        acc = [self.tile([128, 1536], F32, name=f'acc{i}') for i in range(2)]

        def ld_z(tc_):
            z3_ = zz[tc_ % 2]
            r0_ = zr0 + tc_ * 128
            for j in range(3):
                self.dma('sp', z3_[j].ap, Zhy[r0_ - 1 + j:r0_ - 1 + j + 128, :], [], z3_[j].bufs)

        def sc_step(tc):
            z3 = zz[tc % 2]
            a = acc[tc % 2]
            if tc + 1 < ntc:
                ld_z(tc + 1)
            self.tt('pool', z3[2].ap[:, 0:768], z3[2].ap[:, 0:768], cw.ap[:, 2, 0:768], ALU.mult, z3[2].bufs + cw.bufs, z3[2].bufs)
            self.tt('dve', a.ap, z3[1].ap, cw.ap[:, 1, :], ALU.mult, z3[1].bufs + cw.bufs, a.bufs)
            self.tt('dve', z3[0].ap, z3[0].ap, cw.ap[:, 0, :], ALU.mult, z3[0].bufs + cw.bufs, z3[0].bufs)
            self.tt('dve', z3[2].ap[:, 768:1536], z3[2].ap[:, 768:1536], cw.ap[:, 2, 768:1536], ALU.mult, z3[2].bufs + cw.bufs, z3[2].bufs)
            self.tt('dve', a.ap, a.ap, z3[0].ap, ALU.add, a.bufs + z3[0].bufs, a.bufs)
            self.tt('dve', a.ap, a.ap, cb.ap, ALU.add, a.bufs + cb.bufs, a.bufs)
            self.tt('dve', a.ap, a.ap, z3[2].ap, ALU.add, a.bufs + z3[2].bufs, a.bufs)
            self.dma('sp', HX[0, tc * 128:(tc + 1) * 128, :], a.ap[:, 0:512], a.bufs, [])
            self.dma('sp', HX[1, tc * 128:(tc + 1) * 128, :], a.ap[:, 512:1024], a.bufs, [])
            self.dma('sp', HV[tc * 128:(tc + 1) * 128, :], a.ap[:, 1024:1536], a.bufs, [])
            self.cp('act', u16.ap[:, tc, :], a.ap[:, 1024:1536], a.bufs, u16.bufs)
        ld_z(0)
        scq = [(lambda tc=tc: sc_step(tc)) for tc in range(ntc)]
        si = 0
        oi = 0
        chunks = [(i, part) for i in range(npair) for part in range(2)]

        def ld_slab(ci):
            i_, part_ = chunks[ci]
            sl_ = slab[ci % 3]
            self.dma('sp', sl_.ap, Cfwd[i_ + part_ * npair], [], sl_.bufs)
        ld_slab(0)
        if len(chunks) > 1:
            ld_slab(1)
        for i in range(npair):
            for part in range(2):
                sl = slab[si % 3]
                if si + 2 < len(chunks):
                    ld_slab(si + 2)
                for o in range(2):
                    b = (2 * si + o) % 4
                    col = (o if part == 0 else 2 + o) * 512
                    for tc in range(ntc):
                        self.mm(self.psum[b][:, :], sl.ap[:, tc, :], hfilt.ap[:, tc, col:col + 512], tc == 0, tc == ntc - 1,
                                sl.bufs + hfilt.bufs, [self.pb[b]])
                    if i == 0 and part == 1:
                        for tc in range(ntc):
                            self.mm(self.psum[4 + o][0:1, :], sl.ap[:, tc, 0:1], hfilt.ap[:, tc, o * 512:(o + 1) * 512],
                                    tc == 0, tc == ntc - 1, sl.bufs + hfilt.bufs, [self.pb[4 + o]])
                    ot = ob_[oi % 6]
                    oi += 1
                    self.cp('act' if o == 0 else 'dve', ot.ap, self.psum[b][:, :], [self.pb[b]], ot.bufs)
                    if part == 0:
                        self.dma('sp', HS[o, 0, i * 128:(i + 1) * 128, :], ot.ap, ot.bufs, [])
                        if i == 0:
                            self.aC[o] = ot
                    else:
                        if i == 0:
                            ct = self.tile([128, 512], F32, name=f'ct{o}')
                            a0 = self.aC[o]
                            self.cp('pool', ct.ap, a0.ap, a0.bufs, ct.bufs)
                            self.cp('dve', ct.ap[0:1, :], self.psum[4 + o][0:1, :], [self.pb[4 + o]], ct.bufs)
                            self.dma('sp', HC[o], ct.ap, ct.bufs, [])
                            self.memset('dve', ot.ap[0:1, :], 0.0, ot.bufs)
                        self.dma('sp', HS[o, 1, i * 128:(i + 1) * 128, :], ot.ap, ot.bufs, [])
                si += 1
                if scq and si % 2 == 0:
                    scq.pop(0)()
        while scq:
            scq.pop(0)()
        self.reset()
        self.off = base_u16
        if self.dbg.get('hy_stop') in ('B', 'C'):
            return
        u32 = self.tile([128, ntc, 512], F32, name='u32')
        skb = self.tile([128, 1024], F32, name='skb')
        nrm = self.tile([128, 512], F32, name='nrm')
        self.dma('sp', u32.ap, rows(HV), [], u32.bufs)
        self.dma('sp', skb.ap, W['hy_skip'].partition_broadcast(128).rearrange("p a b -> p (a b)"), [], skb.bufs)
        self.dma('sp', nrm.ap, W['hy_normrow'].partition_broadcast(128).rearrange("p a b -> p (a b)"), [], nrm.bufs)
        Y = self.tile([128, nfc, 512], BF16, name='Y')
        fsl = [self.tile([128, ntc, 128], BF16, name=f'fsl{i}') for i in range(3)]
        isl = [self.tile([128, nfc, 128], BF16, name=f'isl{i}') for i in range(2)]
        hA = [self.tile([128, 512], F32, name=f'hA{i}') for i in range(2)]
        hB = [self.tile([128, 512], F32, name=f'hB{i}') for i in range(2)]
        hCt = self.tile([128, 512], F32, name='hCt')
        ur = [self.tile([128, 512], F32, name=f'ur{i}') for i in range(2)]
        ui = [self.tile([128, 512], F32, name=f'ui{i}') for i in range(2)]
        t1 = [self.tile([128, 512], F32, name=f't1{i}') for i in range(2)]
        t2 = [self.tile([128, 512], F32, name=f't2{i}') for i in range(2)]
        xg = [self.tile([128, 512], F32, name=f'xg{i}') for i in range(2)]
        yo = [self.tile([128, 512], F32, name=f'yo{i}') for i in range(2)]
        yb = [self.tile([128, 512], BF16, name=f'yb{i}') for i in range(2)]
        ssq = [self.tile([128, 8], F32, name=f'ssq{i}') for i in range(2)]
        fmo = [self.tile([128, 4, 128], BF16, name=f'fmo{i}') for i in range(2)]
        si = 0
        fchunks = [(i, part) for i in range(npair) for part in range(2)]
        for o in range(2):
            self.dma('sp', hCt.ap, HC[o], [], hCt.bufs)

            def ld_f(ci, base):
                i_, part_ = fchunks[ci]
                sl_ = fsl[(base + ci) % 3]
                self.dma('sp', sl_.ap, Cfwd[i_ + part_ * npair], [], sl_.bufs)

            def ld_i(tc_):
                sl_ = isl[tc_ % 2]
                self.dma('sp', sl_.ap, Cinv[tc_], [], sl_.bufs)
            base_si = si
            ld_f(0, base_si)
            ld_f(1, base_si)
            for i in range(npair):
                pb0 = (i % 2) * 2
                for part in range(2):
                    fc = i + part * npair
                    sl = fsl[si % 3]
                    ci = si - base_si
                    if ci + 2 < len(fchunks):
                        ld_f(ci + 2, base_si)
                    elif ci + 2 == len(fchunks):
                        ld_i(0)
                    si += 1
                    for tc in range(ntc):
                        self.mm(self.psum[pb0 + part][:, :], sl.ap[:, tc, :], u16.ap[:, tc, :], tc == 0, tc == ntc - 1,
                                sl.bufs + u16.bufs, [self.pb[pb0 + part]])
                A, B = hA[i % 2], hB[i % 2]
                self.dma('sp', A.ap, HS[o, 0, i * 128:(i + 1) * 128, :], [], A.bufs)
                self.dma('sp', B.ap, HS[o, 1, i * 128:(i + 1) * 128, :], [], B.bufs)
                Cc = hCt if i == 0 else A
                ur_, ui_, t1_, t2_ = ur[i % 2], ui[i % 2], t1[i % 2], t2[i % 2]
                self.cp('act', ur_.ap, self.psum[pb0][:, :], [self.pb[pb0]], ur_.bufs)
                self.cp('act', ui_.ap, self.psum[pb0 + 1][:, :], [self.pb[pb0 + 1]], ui_.bufs)
                self.tt('dve', t1_.ap, ur_.ap, A.ap, ALU.mult, ur_.bufs + A.bufs, t1_.bufs)
                self.tt('pool', t2_.ap, ui_.ap, B.ap, ALU.mult, ui_.bufs + B.bufs, t2_.bufs)
                self.tt('dve', Y.ap[:, i, :], t1_.ap, t2_.ap, ALU.subtract, t1_.bufs + t2_.bufs, Y.bufs)
                self.tt('pool', t1_.ap, ur_.ap, B.ap, ALU.mult, ur_.bufs + B.bufs, t1_.bufs)
                self.tt('dve', t2_.ap, ui_.ap, Cc.ap, ALU.mult, ui_.bufs + Cc.bufs, t2_.bufs)
                self.tt('pool', Y.ap[:, npair + i, :], t1_.ap, t2_.ap, ALU.add, t1_.bufs + t2_.bufs, Y.bufs)
            for tc in range(ntc):
                sl = isl[tc % 2]
                if tc + 1 < ntc:
                    ld_i(tc + 1)
                b = 4 + tc % 2
                for fc in range(nfc):
                    self.mm(self.psum[b][:, :], sl.ap[:, fc, :], Y.ap[:, fc, :], fc == 0, fc == nfc - 1, sl.bufs + Y.bufs, [self.pb[b]])
                x_, y_ = xg[tc % 2], yo[tc % 2]
                self.dma('sp', x_.ap, HX[o, tc * 128:(tc + 1) * 128, :], [], x_.bufs)
                self.tt('pool', y_.ap, u32.ap[:, tc, :], skb.ap[:, o * 512:(o + 1) * 512], ALU.mult, u32.bufs + skb.bufs, y_.bufs)
                self.tt('dve', y_.ap, y_.ap, self.psum[b][:, :], ALU.add, y_.bufs + [self.pb[b]], y_.bufs)
                if o == 0:
                    self.tt('dve', u32.ap[:, tc, :], y_.ap, x_.ap, ALU.mult, y_.bufs + x_.bufs, u32.bufs)
                    self.cp('act', u16.ap[:, tc, :], u32.ap[:, tc, :], u32.bufs, u16.bufs)
                else:
                    self.tt('dve', y_.ap, y_.ap, x_.ap, ALU.mult, y_.bufs + x_.bufs, y_.bufs)
                    sq_, yb_, fo = ssq[tc % 2], yb[tc % 2], fmo[tc % 2]
                    self.act(x_.ap, y_.ap, AF.Square, y_.bufs, x_.bufs + sq_.bufs, accum_out=sq_.ap[:, 0:1])
                    self.rstd(sq_.ap[:, 1:2], sq_.ap[:, 0:1], 1.0 / 512, sq_.bufs, sq_.bufs)
                    self.stt('dve', yb_.ap, y_.ap, sq_.ap[:, 1:2], nrm.ap, ALU.mult, ALU.mult, y_.bufs + sq_.bufs + nrm.bufs, yb_.bufs)
                    pt_ = self.psum[6 + tc % 2][:, :].bitcast(BF16)
                    for cc in range(4):
                        self.tr(pt_[:, cc * 128:(cc + 1) * 128], yb_.ap[:, cc * 128:(cc + 1) * 128], self.identb.ap,
                                yb_.bufs + self.identb.bufs, [self.pb[6 + tc % 2]])
                    self.cp('act', fo.ap, pt_[:, 0:512].rearrange("p (c t) -> p c t", t=128), [self.pb[6 + tc % 2]], fo.bufs)
                    tg = tg0 + tc * 128
                    self.dma('sp', rows(MG)[:, 8:12, tg:tg + 128], fo.ap, fo.bufs, [])
            if o == 0:
                pass
        self.reset()
        self.off = base0


class Kern6(Kern5):
    def phase_outproj(self, l, sbs):
        W = self.W[l]
        w_out = W['w_out']
        MG = self.dscr("MG", [D, T], BF16)
        tiles = [t for sb in sbs for t in sb]
        T0 = tiles[0][0]
        SB = sum(n for (_, n, _) in tiles)
        mg = [self.tile([128, SB], BF16, name=f'mg{k}') for k in range(NCH)]
        for k in range(NCH):
            self.dma('sp', mg[k].ap, MG[k * 128:(k + 1) * 128, T0:T0 + SB], [], mg[k].bufs)
        ws = [self.tile([128, NCH, 128], BF16, name=f'wo{i}') for i in range(3)]
        xr = [self.tile([128, SB], F32, name=f'oxr{i}') for i in range(2)]
        xo = [self.tile([128, SB], F32, name=f'oxo{i}') for i in range(2)]
        bi = 0

        def ld(m_):
            self.dma('pool', ws[m_ % 3].ap, rows(w_out)[:, :, m_ * 128:(m_ + 1) * 128], [], ws[m_ % 3].bufs)
            self.dma('sp', xr[m_ % 2].ap.rearrange("p (b t) -> p b t", t=128), self.xtr(m_, T0, SB), self.xt_bufs([m_], T0, SB), xr[m_ % 2].bufs)
        ld(0)
        for m in range(NCH):
            if m + 1 < NCH:
                ld(m + 1)
            w = ws[m % 3]
            xrt, xot = xr[m % 2], xo[m % 2]
            off = 0
            for ti, (_, n, cond) in enumerate(tiles):
                b = bi % 8
                bi += 1
                for k in range(NCH):
                    self.mm(self.psum[b][:, 0:n], w.ap[:, k, :], mg[k].ap[:, off:off + n], k == 0, k == NCH - 1,
                            w.bufs + mg[k].bufs, [self.pb[b]])
                self.stt('dve', xot.ap[:, off:off + n], self.psum[b][:, 0:n], self.hgcol(1, m, cond),
                         xrt.ap[:, off:off + n], ALU.mult, ALU.add, [self.pb[b]] + self.hg.bufs + xrt.bufs, xot.bufs)
                off += n
            self.dma('sp', self.xtr(m, T0, SB), xot.ap.rearrange("p (b t) -> p b t", t=128), xot.bufs, self.xt_bufs([m], T0, SB))
        self.reset()


def full_phases(nlayers=2):
    ph = [('init',)]
    for l in range(nlayers):
        last = (l == nlayers - 1)
        sb_mix = SB_ALL
        if l == 0:
            ph += [('mod', l)]
        ph += [('sv', l), ('ffn', l, 1, SB_ALL), ('inproj', l, SB_IN2), ('mlaprep', l, not last),
               ('attn', l, not last), ('ret', l, not last), ('hyena', l, 'lat')]
        if not last:
            ph += [('hyena', l, 'ctx')]
        ph += [('outproj', l, SB_LAT if last else SB_ALL), ('ffn', l, 2, SB_LAT if last else SB_ALL)]
    ph += [('final',)]
    return ph
```

```python
import math
import numpy as np
import ml_dtypes
import concourse.bass as bass
import concourse.mybir as mybir
from concourse.bass_utils import run_bass_kernel_spmd
from contextlib import ExitStack

F32 = mybir.dt.float32
BF16 = mybir.dt.bfloat16
AF = mybir.ActivationFunctionType
ALU = mybir.AluOpType
AX = mybir.AxisListType

ENGS = ('pe', 'act', 'dve', 'pool', 'sp')
CAP = 30000
NDSEM = 56

D = 2048
NCH = 16
T = 2304
TC = 256
TL = 2048
FH = 5632
NJ = 44
EPS = 1e-6
NIN = 3904


class Buf:
    __slots__ = ('name', 'last_w', 'readers', 'psum', 'lw_real')
    registry = []

    def __init__(self, name='', psum=False):
        self.name = name
        self.last_w = None
        self.readers = []
        self.psum = psum
        self.lw_real = True
        Buf.registry.append(self)


class Op:
    __slots__ = ('idx', 'eng', 'fn', 'deps', 'is_dma', 'ticket', 'dsem', 'dval',
                 'needs_inc', 'waits', 'ndma', 'eidx')


class Prog:
    def __init__(self, nc):
        self.nc = nc
        self.ops = []
        self.es = ExitStack()
        self.last_eng = {}
        self.dmas_since = []
        Buf.registry = []

    def sb(self, name, shape, dtype):
        return self.es.enter_context(self.nc.sbuf_tensor(name, list(shape), dtype))

    def ps(self, name, shape, dtype=F32):
        return self.es.enter_context(self.nc.psum_tensor(name, list(shape), dtype))

    def add(self, eng, fn, reads=(), writes=(), dma=False, ndma=1, extra=()):
        op = Op()
        op.idx = len(self.ops)
        op.eng = eng
        op.fn = fn
        op.is_dma = dma
        op.ndma = ndma
        op.needs_inc = False
        op.ticket = None
        op.dsem = None
        op.dval = None
        op.waits = None
        deps = {}
        for b in reads:
            if b.last_w is not None:
                raw = b.lw_real
                if b.last_w.idx in deps:
                    raw = raw or deps[b.last_w.idx][1]
                deps[b.last_w.idx] = (b.last_w, raw)
        for b in writes:
            if b.last_w is not None and b.last_w.idx not in deps:
                deps[b.last_w.idx] = (b.last_w, False)
            for r in b.readers:
                if r.idx not in deps:
                    deps[r.idx] = (r, False)
        for d in extra:
            if d.idx not in deps:
                deps[d.idx] = (d, True)
        op.deps = list(deps.values())
        for b in writes:
            b.last_w = op
            b.lw_real = True
            b.readers = []
        for b in reads:
            if b.psum:
                if b.last_w is not op:
                    b.last_w = op
                    b.lw_real = False
                    b.readers = []
            else:
                b.readers.append(op)
        self.ops.append(op)
        if fn is not None:
            if dma:
                self.dmas_since.append(op)
            else:
                self.last_eng[eng] = op
        return op

    def dma(self, q, out, in_, reads, writes, **kw):
        return self.add(q, lambda e: e.dma_start(out=out, in_=in_, **kw), reads, writes, dma=True)

    def barrier(self):
        ex = list(self.last_eng.values()) + list(self.dmas_since)
        for e in ENGS:
            self.add(e, None, extra=ex)
        self.dmas_since = []
        for b in Buf.registry:
            b.last_w = None
            b.readers = []
            b.lw_real = True

    def emit(self):
        nc = self.nc
        ops = self.ops
        for op in ops:
            need = []
            for (d, raw) in op.deps:
                if d.fn is None and not d.is_dma:
                    continue
                if d.eng == op.eng and not d.is_dma and not op.is_dma and op.fn is not None:
                    if not raw or op.eng == 'pe':
                        continue
                if d.eng == op.eng and not d.is_dma and op.fn is None:
                    continue
                need.append(d)
                if not d.is_dma:
                    d.needs_inc = True
            op.deps = need
        cnt = {e: 0 for e in ENGS}
        eidx = {e: 0 for e in ENGS}
        for op in ops:
            op.eidx = eidx[op.eng]
            eidx[op.eng] += 1
            if op.needs_inc:
                cnt[op.eng] += 1
                op.ticket = cnt[op.eng]
        nep = {e: (cnt[e] + CAP - 1) // CAP for e in ENGS}
        sems = {e: [self.es.enter_context(nc.semaphore(f"s_{e}{i}")) for i in range(nep[e])]
                for e in ENGS}
        dsems = [self.es.enter_context(nc.semaphore(f"d{i}")) for i in range(NDSEM)]
        dval = [0] * NDSEM
        known = {f: {e: -1 for e in ENGS} for f in ENGS}
        dknown = {f: [0] * NDSEM for f in ENGS}
        ndma = 0
        for op in ops:
            w = []
            f = op.eng
            if op.is_dma:
                si = ndma % NDSEM
                ndma += 1
                if dval[si] > dknown[f][si]:
                    w.append((dsems[si], dval[si]))
                    dknown[f][si] = dval[si]
                dval[si] += 16 * op.ndma
                op.dsem = si
                op.dval = dval[si]
            best = {}
            for d in op.deps:
                if d.is_dma:
                    if d.dval > dknown[f][d.dsem]:
                        w.append((dsems[d.dsem], d.dval))
                        dknown[f][d.dsem] = d.dval
                else:
                    if d.eidx > known[f][d.eng]:
                        if d.eng not in best or d.eidx > best[d.eng].eidx:
                            best[d.eng] = d
            for e, d in best.items():
                known[f][e] = d.eidx
                t = d.ticket - 1
                w.append((sems[e][t // CAP], t % CAP + 1))
            op.waits = w
        per = {e: [op for op in ops if op.eng == e] for e in ENGS}
        self.stats = {e: len(per[e]) for e in ENGS}
        self.stats['waits'] = sum(len(op.waits) for op in ops)
        self.stats['ndma'] = ndma

        def run(e, engobj):
            for op in per[e]:
                for (s, v) in op.waits:
                    engobj.wait_ge(s, v)
                if op.fn is None:
                    continue
                ins = op.fn(engobj)
                if op.is_dma:
                    if not isinstance(ins, (list, tuple)):
                        ins = [ins]
                    assert len(ins) == op.ndma
                    for i_ in ins:
                        i_.then_inc(dsems[op.dsem], 16)
                elif op.needs_inc:
                    t = op.ticket - 1
                    ins.then_inc(sems[e][t // CAP], 1)

        with nc.Block() as block:
            @block.tensor
            def _(eng):
                run('pe', eng)

            @block.scalar
            def _(eng):
                run('act', eng)

            @block.vector
            def _(eng):
                run('dve', eng)

            @block.gpsimd
            def _(eng):
                run('pool', eng)

            @block.sync
            def _(eng):
                run('sp', eng)
        self.es.close()


class Tl:
    __slots__ = ('ap', 'bufs')

    def __init__(self, ap, bufs):
        self.ap = ap
        self.bufs = bufs

    def __getitem__(self, k):
        return self.ap[k]


AW = 50688


class KB:
    def __init__(self, nc):
        self.nc = nc
        self.P = Prog(nc)
        P = self.P
        self.arena = P.sb("arena", [128, AW], F32)
        self.off = 0
        self.mark = 0
        self.psum = [P.ps(f"ps{i}", [128, 512], F32) for i in range(8)]
        self.pb = [Buf(f"ps{i}", psum=True) for i in range(8)]
        self.qi = 0

    def tile(self, shape, dtype, bufs=None, at=None, name=''):
        free = 1
        for s in shape[1:]:
            free *= s
        words = free if dtype == F32 else (free + 1) // 2
        words = (words + 7) // 8 * 8
        if at is None:
            o = self.off
            self.off += words
            assert self.off <= AW, f"arena overflow {self.off} ({name})"
        else:
            o = at
            assert o + words <= AW
        ap = self.arena[0:shape[0], o:o + words]
        if dtype != F32:
            ap = ap.bitcast(dtype)
        ap = ap[:, 0:free]
        if len(shape) == 3:
            ap = ap.rearrange("p (a b) -> p a b", a=shape[1])
        elif len(shape) == 4:
            ap = ap.rearrange("p (a b c) -> p a b c", a=shape[1], b=shape[2])
        t = Tl(ap, bufs if bufs is not None else [Buf(name)])
        return t

    def persist(self):
        self.mark = self.off

    def reset(self):
        self.P.barrier()
        self.off = self.mark

    def mm(self, out, lhsT, rhs, start, stop, reads, writes):
        self.P.add('pe', lambda e: e.matmul(out, lhsT, rhs, start=start, stop=stop), reads, writes)

    def tr(self, out, in_, ident, reads, writes):
        self.P.add('pe', lambda e: e.transpose(out, in_, ident), reads, writes)

    def act(self, out, in_, func, reads, writes, **kw):
        self.P.add('act', lambda e: e.activation(out, in_, func, **kw), reads, writes)

    def tt(self, eng, out, in0, in1, op, reads, writes):
        self.P.add(eng, lambda e: e.tensor_tensor(out, in0, in1, op), reads, writes)

    def ts(self, eng, out, in0, s1, s2, op0, op1, reads, writes):
        if op1 is None:
            self.P.add(eng, lambda e: e.tensor_scalar(out, in0, s1, None, op0), reads, writes)
        else:
            self.P.add(eng, lambda e: e.tensor_scalar(out, in0, s1, s2, op0, op1), reads, writes)

    def stt(self, eng, out, in0, scalar, in1, op0, op1, reads, writes):
        eng = 'dve'
        self.P.add(eng, lambda e: e.scalar_tensor_tensor(out, in0, scalar, in1, op0, op1), reads, writes)

    def cp(self, eng, out, in_, reads, writes):
        if eng == 'act':
            self.P.add('act', lambda e: e.copy(out, in_), reads, writes)
        else:
            self.P.add(eng, lambda e: e.tensor_copy(out, in_), reads, writes)

    def memset(self, eng, out, val, writes):
        self.P.add(eng, lambda e: e.memset(out, val), [], writes)

    def dma(self, q, out, in_, reads, writes):
        self.P.dma(q, out, in_, reads, writes)

    def rstd(self, out, ss, scale, reads, writes, eps=EPS):
        self.P.add('act', lambda e: e.activation(out, ss, AF.Ln, bias=self.epsc(eps, out.shape[0]), scale=scale),
                   reads + self.epst.bufs, writes)
        self.P.add('act', lambda e: e.activation(out, out, AF.Exp, scale=-0.5), writes, writes)

    def sqrt_ms(self, out, ss, scale, reads, writes, eps=EPS):
        self.P.add('act', lambda e: e.activation(out, ss, AF.Ln, bias=self.epsc(eps, out.shape[0]), scale=scale),
                   reads + self.epst.bufs, writes)
        self.P.add('act', lambda e: e.activation(out, out, AF.Exp, scale=-0.5), writes, writes)

    def epsc(self, eps, n):
        assert eps == EPS
        return self.epst.ap[0:n, 0:1]


def rows(ap, p=128):
    return ap.rearrange("(k p) n -> p k n", p=p)


SV_ROWS = 88
SV = dict(q_norm=0, kv_norm=4, qn_nope=6, qn_rope=7, kn_nope=8, kn_rope=9, out_norm=10,
          conv_w=18, conv_b=54, skip=66, hy_norm=74, gn_w=78, gn_b=82, b1=86, b2=87)

LAYER_W = [('ada_w', [D, 9 * D]), ('ada_b', [144, 128]), ('ffn1_gate', [D, FH]), ('ffn1_up', [D, FH]),
           ('ffn1_down', [FH, D]), ('w_in', [D, NIN]), ('wuq', [512, 1536]), ('wukv', [256, 2048]),
           ('w_out', [D, D]), ('ffn2_gate', [D, FH]), ('ffn2_up', [D, FH]), ('ffn2_down', [FH, D]),
           ('hy_w1', [33, 64]), ('hy_w2', [64, 64]), ('hy_w3', [64, 2048]), ('sv', [SV_ROWS, 128]),
           ('hy_skip', [1, 1024]), ('hy_convw', [1, 3 * 1536]), ('hy_convb', [1, 1536]),
           ('ret_decay', [1, 8]), ('hy_normrow', [1, 512])]


class LazyW(dict):
    def __init__(self, kern, l):
        super().__init__()
        self.kern, self.l = kern, l
        self.shapes = dict(LAYER_W)

    def __missing__(self, n):
        ap = self.kern.din(f"{n}_{self.l}", self.shapes[n])
        self[n] = ap
        return ap


class Kern(KB):
    def __init__(self, nc, nlayers=2, dbg=None):
        super().__init__(nc)
        self.nl = nlayers
        self.dbg = dbg or {}
        self.feeds = self.dbg.get('feeds', {})
        self.dumps = self.dbg.get('dumps', [])
        self.inputs = {}
        self.scratch = {}
        self.W = [LazyW(self, l) for l in range(nlayers)]
        self.ident_d = self.din("ident", [128, 128])
        self.XT = self.dscr("XT", [T // 128, 128, NCH, 128], F32)
        self.bXT = [[Buf(f"XT{k}_{i}") for i in range(9)] for k in range(NCH)]
        self.outbufs = []

    def din(self, name, shape, dt=F32):
        if name not in self.inputs:
            self.inputs[name] = self.nc.dram_tensor(name, list(shape), dt, kind="ExternalInput").ap()
        return self.inputs[name]

    def dscr(self, name, shape, dt=F32):
        if name not in self.scratch:
            kind = "ExternalOutput" if (name in self.dumps or name in self.feeds) else "Internal"
            self.scratch[name] = self.nc.dram_tensor(name, list(shape), dt, kind=kind).ap()
        return self.scratch[name]

    def phase_feed(self):
        for name, arr in self.feeds.items():
            dt_ = F32 if arr.dtype == np.float32 else BF16
            dst = self.dscr(name, list(arr.shape), dt_)
            src = self.din("feed_" + name, list(arr.shape), dt_)
            ob = Buf('feed')
            self.outbufs.append(ob)
            self.dma('sp', dst, src, [], [ob])
        if 'modtab' in self.dbg:
            src = self.din("modtab", [128, 288 + 96 + 96])
            self.dma('sp', self.mod.ap, src[:, 0:288], [], self.mod.bufs)
            self.dma('sp', self.s1p.ap, src[:, 288:384], [], self.s1p.bufs)
            self.dma('sp', self.hg.ap, src[:, 384:480], [], self.hg.bufs)
        self.reset()

    def xtr(self, m, T0, SB):
        return self.XT[T0 // 128:(T0 + SB) // 128, :, m, :].rearrange("b p t -> p b t")

    def xt_bufs(self, ks, t0, n):
        out = []
        for k in ks:
            for i in range(t0 // 256, (t0 + n + 255) // 256):
                out.append(self.bXT[k][i])
        return out

    def consts(self):
        P = self.P
        self.ident = self.tile([128, 128], F32, name='ident')
        self.identb = self.tile([128, 128], BF16, name='identb')
        self.onesb = self.tile([128, 128], BF16, name='onesb')
        self.onesf = self.tile([128, 128], F32, name='onesf')
        self.dma('sp', self.ident.ap, self.ident_d, [], self.ident.bufs)
        self.cp('dve', self.identb.ap, self.ident.ap, self.ident.bufs, self.identb.bufs)
        self.memset('dve', self.onesb.ap, 1.0, self.onesb.bufs)
        self.memset('dve', self.onesf.ap, 1.0, self.onesf.bufs)
        self.epst = self.tile([128, 8], F32, name='epst')
        self.memset('dve', self.epst.ap, EPS, self.epst.bufs)
        self.modT = []
        for i in range(2):
            self.modT.append((self.tile([128, 9 * 16 * 2], F32, name=f'mod{i}'),
                              self.tile([128, 3 * 16 * 2], F32, name=f's1p{i}'),
                              self.tile([128, 3 * 16 * 2], F32, name=f'hg{i}')))
        self.set_layer(0)
        self.persist()

    def set_layer(self, l):
        self.mod, self.s1p, self.hg = self.modT[l % 2]

    def modcol(self, v, k, cond):
        i = (v * 16 + k) * 2 + cond
        return self.mod.ap[:, i:i + 1]

    def s1pcol(self, s, k, cond):
        i = (s * 16 + k) * 2 + cond
        return self.s1p.ap[:, i:i + 1]

    def hgcol(self, s, k, cond):
        i = (s * 16 + k) * 2 + cond
        return self.hg.ap[:, i:i + 1]

    def phase_init(self):
        self.x = self.din("x", [TL, D])
        self.ctx = self.din("ctx", [TC, D])
        groups = [(self.ctx, 0, 0, 256)] + [(self.x, 256, i * 512, 512) for i in range(4)]
        xs = [self.tile([128, D], F32, name=f'xin{i}') for i in range(2)]
        xo = [self.tile([128, NCH, 128], F32, name=f'xo{i}') for i in range(3)]
        cnt = 0
        bi = 0
        for gi, (src, tg, s0, n) in enumerate(groups):
            for tt_ in range(n // 128):
                xi = xs[cnt % 2]
                o = xo[cnt % 3]
                cnt += 1
                self.dma('sp', xi.ap, src[s0 + tt_ * 128:s0 + (tt_ + 1) * 128, :], [], xi.bufs)
                for kg in range(4):
                    b = bi % 8
                    bi += 1
                    for kk in range(4):
                        k = kg * 4 + kk
                        self.tr(self.psum[b][:, kk * 128:(kk + 1) * 128], xi.ap[:, k * 128:(k + 1) * 128],
                                self.ident.ap, xi.bufs + self.ident.bufs, [self.pb[b]])
                    src_ap = self.psum[b][:, :].rearrange("p (a b) -> p a b", a=4)
                    self.cp('dve' if kg % 2 == 0 else 'act', o.ap[:, kg * 4:(kg + 1) * 4, :], src_ap, [self.pb[b]], o.bufs)
                tok = tg + s0 + tt_ * 128
                self.dma('sp', self.XT[tok // 128], o.ap, o.bufs, self.xt_bufs(range(NCH), tok, 128))
        self.reset()

    def phase_dumpmod(self, l=0):
        dst = self.dscr("modout", [128, 480], F32)
        ob = Buf('mo')
        self.outbufs.append(ob)
        self.dma('sp', dst[:, 0:288], self.mod.ap, self.mod.bufs, [ob])
        ob = Buf('mo')
        self.outbufs.append(ob)
        self.dma('sp', dst[:, 288:384], self.s1p.ap, self.s1p.bufs, [ob])
        ob = Buf('mo')
        self.outbufs.append(ob)
        self.dma('sp', dst[:, 384:480], self.hg.ap, self.hg.bufs, [ob])
        self.reset()

    def phase_final(self):
        self.out = self.nc.dram_tensor("out", [TL, D], F32, kind="ExternalOutput").ap()
        xi = [self.tile([128, NCH, 128], F32, name=f'fi{i}') for i in range(3)]
        xo = [self.tile([128, D], F32, name=f'fo{i}') for i in range(2)]
        cnt = 0
        bi = 0
        for g in range(4):
            for tt_ in range(4):
                t0 = 256 + g * 512 + tt_ * 128
                i_ = xi[cnt % 3]
                self.dma('sp', i_.ap, self.XT[t0 // 128], self.xt_bufs(range(NCH), t0, 128), i_.bufs)
                o = xo[cnt % 2]
                cnt += 1
                for kg in range(4):
                    b = bi % 8
                    bi += 1
                    for kk in range(4):
                        k = kg * 4 + kk
                        self.tr(self.psum[b][:, kk * 128:(kk + 1) * 128], i_.ap[:, k, :],
                                self.ident.ap, i_.bufs + self.ident.bufs, [self.pb[b]])
                    self.cp('dve' if kg % 2 == 0 else 'act', o.ap[:, kg * 512:(kg + 1) * 512], self.psum[b][:, :],
                            [self.pb[b]], o.bufs)
                r0 = g * 512 + tt_ * 128
                ob = Buf('outrow')
                self.outbufs.append(ob)
                self.dma('sp', self.out[r0:r0 + 128, :], o.ap, o.bufs, [ob])
        self.reset()

    def load_cols(self, dst, src2d, nrows):
        tmp = self.tile([128, 128], F32, name='lc_tmp')
        self.dma('sp', tmp.ap[0:nrows, :], src2d, [], tmp.bufs)
        b = 7
        self.tr(self.psum[b][:, 0:nrows], tmp.ap[0:nrows, :], self.ident.ap[0:nrows, 0:nrows],
                tmp.bufs + self.ident.bufs, [self.pb[b]])
        return b

    def mod_steps(self, l, accbank):
        W = self.W[l]
        mod_t, s1p_t, hg_t = self.modT[l % 2]
        st = {}
        NSL = 3

        def prologue():
            self.cvec = self.din("cvec", [32, 128])
            b = self.load_cols(None, self.cvec, 32)
            sc = self.tile([128, 16, 2], BF16, name='sc')
            src = self.psum[b][:, 0:32].rearrange("p (c k) -> p k c", c=2)
            self.act(sc.ap, src, AF.Silu, [self.pb[b]], sc.bufs)
            abT = self.tile([128, 144], F32, name='abT')
            b = self.load_cols(None, W['ada_b'][0:128, :], 128)
            self.cp('dve', abT.ap[:, 0:128], self.psum[b][:, 0:128], [self.pb[b]], abT.bufs)
            b = self.load_cols(None, W['ada_b'][128:144, :], 16)
            self.cp('dve', abT.ap[:, 128:144], self.psum[b][:, 0:16], [self.pb[b]], abT.bufs)
            st['sc'], st['abT'] = sc, abT
            st['slabs'] = [self.tile([128, 16, 512], BF16, name=f'adaw{i}') for i in range(NSL)]
            for i in range(NSL - 1):
                ld(i)

        def ld(s_):
            sl = st['slabs'][s_ % NSL]
            self.dma('pool', sl.ap, rows(W['ada_w'])[:, :, s_ * 512:(s_ + 1) * 512], [], sl.bufs)

        def slab(s_):
            sc, abT = st['sc'], st['abT']
            if s_ + NSL - 1 < 36:
                ld(s_ + NSL - 1)
            sl = st['slabs'][s_ % NSL]
            acc = self.psum[accbank][:, 0:8].rearrange("p (j c) -> p j c", c=2)
            for jj in range(4):
                for k in range(16):
                    self.mm(acc[:, jj, :], sl.ap[:, k, jj * 128:(jj + 1) * 128], sc.ap[:, k, :],
                            k == 0, k == 15, sl.bufs + sc.bufs, [self.pb[accbank]])
            modv = mod_t.ap.rearrange("p (j c) -> p j c", c=2)
            self.tt('dve', modv[:, s_ * 4:(s_ + 1) * 4, :], acc, abT.ap[:, s_ * 4:(s_ + 1) * 4].unsqueeze(2).to_broadcast([128, 4, 2]),
                    ALU.add, [self.pb[accbank]] + abT.bufs, mod_t.bufs)

        def epilogue():
            for s_ in range(3):
                self.ts('dve', s1p_t.ap[:, s_ * 32:(s_ + 1) * 32], mod_t.ap[:, (3 * s_ + 1) * 32:(3 * s_ + 2) * 32],
                        1.0, None, ALU.add, None, mod_t.bufs, s1p_t.bufs)
                self.ts('dve', hg_t.ap[:, s_ * 32:(s_ + 1) * 32], mod_t.ap[:, (3 * s_ + 2) * 32:(3 * s_ + 3) * 32],
                        1.0 if s_ == 1 else 0.5, None, ALU.mult, None, mod_t.bufs, hg_t.bufs)
        return prologue, [(lambda s_=s_: slab(s_)) for s_ in range(36)], epilogue

    def phase_mod(self, l):
        pro, slabs, epi = self.mod_steps(l, 0)
        pro()
        for f in slabs:
            f()
        epi()
        self.reset()

    def normmod(self, tiles, hT, s, work_at):
        o = work_at
        xts, sqs = [], []
        for i in range(2):
            xts.append(self.tile([128, NCH, 256], F32, at=o, name=f'nm_x{i}'))
            o += NCH * 256
            sqs.append(self.tile([128, NCH, 256], BF16, at=o, name=f'nm_s{i}'))
            o += NCH * 128
        rs = [self.tile([128, 256], F32, at=o + i * 256, name=f'nm_r{i}') for i in range(2)]
        allb = []
        for t_ in xts + sqs + rs:
            allb += t_.bufs
        off = 0
        it = 0
        for (t0, n, cond) in tiles:
            for u0 in range(0, n, 256):
                xt, sq, r = xts[it % 2], sqs[it % 2], rs[it % 2]
                pbk = 6 + it % 2
                it += 1
                ta = t0 + u0
                self.dma('sp', xt.ap, rows(self.XT)[:, :, ta:ta + 256], self.xt_bufs(range(NCH), ta, 256), xt.bufs)
                self.act(sq.ap, xt.ap, AF.Square, xt.bufs, sq.bufs)
                for k in range(NCH):
                    self.mm(self.psum[pbk][:, 0:256], self.onesb.ap, sq.ap[:, k, :], k == 0, k == NCH - 1,
                            self.onesb.bufs + sq.bufs, [self.pb[pbk]])
                self.rstd(r.ap, self.psum[pbk][:, 0:256], 1.0 / D, [self.pb[pbk]], r.bufs)
                self.tt('dve', xt.ap, xt.ap, r.ap.unsqueeze(1).to_broadcast([128, NCH, 256]), ALU.mult,
                        xt.bufs + r.bufs, xt.bufs)
                for k in range(NCH):
                    dst = hT.ap[:, k, off + u0:off + u0 + 256]
                    if k % 2 == 0:
                        self.act(dst, xt.ap[:, k, :], AF.Identity, xt.bufs + self.s1p.bufs + self.mod.bufs, hT.bufs,
                                 scale=self.s1pcol(s, k, cond), bias=self.modcol(3 * s, k, cond))
                    else:
                        self.ts('dve', dst, xt.ap[:, k, :], self.s1pcol(s, k, cond), self.modcol(3 * s, k, cond),
                                ALU.mult, ALU.add, xt.bufs + self.s1p.bufs + self.mod.bufs, hT.bufs)
            off += n
        return allb

    def nm_scratch(self):
        xts = [self.tile([128, NCH, 128], F32, name=f'nmx{i}') for i in range(3)]
        sqs = [self.tile([128, NCH, 128], BF16, name=f'nms{i}') for i in range(2)]
        rs = [self.tile([128, 128], F32, name=f'nmr{i}') for i in range(2)]
        return (xts, sqs, rs)

    def normmod_steps(self, tiles, hT, s, scr):
        xts, sqs, rs = scr
        subs = []
        off = 0
        for (t0, n, cond) in tiles:
            for u0 in range(0, n, 128):
                subs.append((t0 + u0, off + u0, cond))
            off += n

        def ld(k):
            ta, o_, cond = subs[k]
            xt = xts[k % 3]
            self.dma('sp', xt.ap, self.XT[ta // 128], self.xt_bufs(range(NCH), ta, 128), xt.bufs)

        def sqr(k):
            xt, sq = xts[k % 3], sqs[k % 2]
            self.act(sq.ap, xt.ap, AF.Square, xt.bufs, sq.bufs)

        def b1(k):
            sq, r = sqs[k % 2], rs[k % 2]
            pbk = 6 + k % 2
            for kk in range(NCH):
                self.mm(self.psum[pbk][:, 0:128], self.onesb.ap, sq.ap[:, kk, :], kk == 0, kk == NCH - 1,
                        self.onesb.bufs + sq.bufs, [self.pb[pbk]])
            self.rstd(r.ap, self.psum[pbk][:, 0:128], 1.0 / D, [self.pb[pbk]], r.bufs)

        def b2(k):
            ta, o_, cond = subs[k]
            xt, r = xts[k % 3], rs[k % 2]
            self.tt('dve', xt.ap, xt.ap, r.ap.unsqueeze(1).to_broadcast([128, NCH, 128]), ALU.mult, xt.bufs + r.bufs, xt.bufs)
            for kk in range(NCH):
                dst = hT.ap[:, kk, o_:o_ + 128]
                if kk % 2 == 0:
                    self.act(dst, xt.ap[:, kk, :], AF.Identity, xt.bufs + self.s1p.bufs + self.mod.bufs, hT.bufs,
                             scale=self.s1pcol(s, kk, cond), bias=self.modcol(3 * s, kk, cond))
                else:
                    self.ts('dve', dst, xt.ap[:, kk, :], self.s1pcol(s, kk, cond), self.modcol(3 * s, kk, cond),
                            ALU.mult, ALU.add, xt.bufs + self.s1p.bufs + self.mod.bufs, hT.bufs)
        ns = len(subs)

        def first():
            ld(0)
            if ns > 1:
                ld(1)
            sqr(0)
            b1(0)
        steps = [first]
        for k in range(ns):
            def st(k=k):
                if k + 2 < ns:
                    ld(k + 2)
                if k + 1 < ns:
                    sqr(k + 1)
                b2(k)
                if k + 1 < ns:
                    b1(k + 1)
            steps.append(st)
        return steps

    def phase_ffn(self, l, which, sbs):
        W = self.W[l]
        wg, wu, wd = W[f'ffn{which}_gate'], W[f'ffn{which}_up'], W[f'ffn{which}_down']
        s = 0 if which == 1 else 2
        maxSB = max(sum(n for (_, n, _) in tiles) for tiles in sbs)
        hT_full = self.tile([128, NCH, maxSB], BF16, name='hT')
        aT_full = self.tile([128, NJ, maxSB], BF16, name='aT')
        scr = self.nm_scratch()
        ring = [self.tile([128, 4096 * 2], BF16, name=f'wr{i}') for i in range(3)]
        sg = [self.tile([128, 512], F32, name=f'sg{i}') for i in range(3)]
        xr = [self.tile([128, maxSB], F32, name=f'xr{i}') for i in range(2)]
        xo = [self.tile([128, maxSB], F32, name=f'xo{i}') for i in range(2)]
        ri = 0
        for st_ in self.normmod_steps(sbs[0], hT_full, s, scr):
            st_()
        for sbi, tiles in enumerate(sbs):
            SB = sum(n for (_, n, _) in tiles)
            T0 = tiles[0][0]
            hT, aT = hT_full, aT_full
            regs = []
            cur = [0, 0]

            def alloc(n):
                if cur[1] + n > 512:
                    cur[0] += 1
                    cur[1] = 0
                r = (cur[0], cur[1])
                cur[1] += n
                return r
            for (_, n, _) in tiles:
                regs.append((alloc(n), alloc(n)))
            nbank = cur[0] + 1
            assert nbank * 2 <= 6
            for js in range(NJ // 2):
                slot = ring[ri % 3]
                ri += 1
                g_ap = slot.ap[:, 0:4096].rearrange("p (k n) -> p k n", k=16)
                u_ap = slot.ap[:, 4096:8192].rearrange("p (k n) -> p k n", k=16)
                self.dma('pool', g_ap, rows(wg)[:, :, js * 256:(js + 1) * 256], [], slot.bufs)
                self.dma('pool', u_ap, rows(wu)[:, :, js * 256:(js + 1) * 256], [], slot.bufs)
                for jj in range(2):
                    j = js * 2 + jj
                    bb = (j % 2) * nbank
                    off = 0
                    for ti, (_, n, _) in enumerate(tiles):
                        (gb, gc), (ub, uc) = regs[ti]
                        for k in range(NCH):
                            self.mm(self.psum[bb + gb][:, gc:gc + n], g_ap[:, k, jj * 128:(jj + 1) * 128],
                                    hT.ap[:, k, off:off + n], k == 0, k == NCH - 1, slot.bufs + hT.bufs, [self.pb[bb + gb]])
                        for k in range(NCH):
                            self.mm(self.psum[bb + ub][:, uc:uc + n], u_ap[:, k, jj * 128:(jj + 1) * 128],
                                    hT.ap[:, k, off:off + n], k == 0, k == NCH - 1, slot.bufs + hT.bufs, [self.pb[bb + ub]])
                        off += n
                    off = 0
                    for ti, (_, n, _) in enumerate(tiles):
                        (gb, gc), (ub, uc) = regs[ti]
                        sgt = sg[(j * len(tiles) + ti) % 3]
                        self.act(sgt.ap[:, 0:n], self.psum[bb + gb][:, gc:gc + n], AF.Silu, [self.pb[bb + gb]], sgt.bufs)
                        self.tt('dve', aT.ap[:, j, off:off + n], sgt.ap[:, 0:n], self.psum[bb + ub][:, uc:uc + n], ALU.mult,
                                sgt.bufs + [self.pb[bb + ub]], aT.bufs)
                        off += n
            pending = self.normmod_steps(sbs[sbi + 1], hT_full, s, scr) if sbi + 1 < len(sbs) else []
            yregs = []
            cur[0], cur[1] = 0, 0
            for (_, n, _) in tiles:
                yregs.append(alloc(n))
            nby = cur[0] + 1
            for m in range(NCH):
                slot = ring[ri % 3]
                ri += 1
                w_ap = slot.ap[:, 0:NJ * 128].rearrange("p (j n) -> p j n", j=NJ)
                self.dma('pool', w_ap, rows(wd)[:, :, m * 128:(m + 1) * 128], [], slot.bufs)
                xrt, xot = xr[m % 2], xo[m % 2]
                self.dma('sp', xrt.ap[:, 0:SB].rearrange("p (b t) -> p b t", t=128), self.xtr(m, T0, SB), self.xt_bufs([m], T0, SB), xrt.bufs)
                bb = (m % 2) * nby
                off = 0
                for ti, (_, n, _) in enumerate(tiles):
                    (yb, yc) = yregs[ti]
                    for j in range(NJ):
                        self.mm(self.psum[bb + yb][:, yc:yc + n], w_ap[:, j, :], aT.ap[:, j, off:off + n],
                                j == 0, j == NJ - 1, slot.bufs + aT.bufs, [self.pb[bb + yb]])
                    off += n
                off = 0
                for ti, (_, n, cond) in enumerate(tiles):
                    (yb, yc) = yregs[ti]
                    self.stt('dve', xot.ap[:, off:off + n], self.psum[bb + yb][:, yc:yc + n], self.hgcol(s, m, cond),
                             xrt.ap[:, off:off + n], ALU.mult, ALU.add, [self.pb[bb + yb]] + self.hg.bufs + xrt.bufs, xot.bufs)
                    off += n
                self.dma('sp', self.xtr(m, T0, SB), xot.ap[:, 0:SB].rearrange("p (b t) -> p b t", t=128), xot.bufs, self.xt_bufs([m], T0, SB))
                if pending:
                    pending.pop(0)()
            while pending:
                pending.pop(0)()
        self.reset()


SB_ALL = [[(0, 256, 1), (256, 512, 0)], [(768, 512, 0), (1280, 256, 0)], [(1536, 512, 0), (2048, 256, 0)]]
SB_LAT = [[(256, 512, 0), (768, 256, 0)], [(1024, 512, 0), (1536, 256, 0)], [(1792, 512, 0)]]
SB_IN2 = [[(0, 256, 1), (256, 512, 0), (768, 256, 0)], [(1024, 512, 0), (1536, 512, 0), (2048, 256, 0)]]


def build(nlayers=2, phases=None, dbg=None):
    nc = bass.Bass("TRN2", target_bir_lowering=False)
    K = Kern6(nc, nlayers, dbg)
    K.aC = {}
    K.consts()
    if phases is None:
        phases = full_phases(nlayers)
    for ph in phases:
        if len(ph) > 1 and isinstance(ph[1], int):
            K.set_layer(ph[1])
        getattr(K, 'phase_' + ph[0])(*ph[1:])
    K.P.add('sp', None, K.outbufs, [])
    K.P.emit()
    return nc, K


def pack_sv(I, l):
    sv = np.zeros((SV_ROWS, 128), np.float32)

    def put(name, v):
        v = np.asarray(v, np.float32).reshape(-1)
        r0 = SV[name]
        n = (v.size + 127) // 128
        buf = np.zeros(n * 128, np.float32)
        buf[:v.size] = v
        sv[r0:r0 + n] = buf.reshape(n, 128)
    put('q_norm', I['mla_q_norm'][l]); put('kv_norm', I['mla_kv_norm'][l])
    put('qn_nope', I['mla_qn_nope'][l]); put('qn_rope', I['mla_qn_rope'][l])
    put('kn_nope', I['mla_kn_nope'][l]); put('kn_rope', I['mla_kn_rope'][l])
    put('out_norm', I['mla_out_norm'][l]); put('conv_w', I['hy_conv_w'][l]); put('conv_b', I['hy_conv_b'][l])
    put('skip', I['hy_skip'][l]); put('hy_norm', I['hy_out_norm'][l]); put('gn_w', I['ret_gn_w'][l])
    put('gn_b', I['ret_gn_b'][l]); put('b1', I['hy_ffn_b1'][l]); put('b2', I['hy_ffn_b2'][l])
    return sv


def hyena_consts(n):
    f32 = np.float32
    C = {}
    t = np.linspace(0.0, 1.0, n, dtype=f32)[:, None]
    w = (2 * math.pi * np.arange(n, dtype=f32)[:, None] / n).astype(f32)
    fb = np.linspace(1e-4, 15, 16, dtype=f32)[None, :]
    feats = np.concatenate([t, np.cos(fb * w), -np.sin(fb * w)], axis=-1).astype(f32)
    C[f'hy_feats{n}'] = np.ascontiguousarray(feats.T)
    deltas = np.abs(np.linspace(math.log(1e-2) / 1.5, math.log(1e-2) / 0.3, 512, dtype=f32))
    C[f'hy_window{n}'] = (np.exp(-t * deltas[None, :]) + 0.05).astype(f32)
    N2 = 2 * n
    fr = np.arange(n, dtype=np.int64)[:, None]
    tt = np.arange(n, dtype=np.int64)[None, :]
    ang = 2 * np.pi * ((fr * tt) % N2).astype(np.float64) / N2
    fwd = np.empty((2 * n, n), np.float64)
    fwd[:n] = np.cos(ang)
    fwd[n:] = -np.sin(ang)
    fwd[n] = np.cos(np.pi * np.arange(n))
    inv = np.empty((2 * n, n), np.float64)
    wf = np.full((n, 1), 2.0); wf[0] = 1.0
    inv[:n] = wf * np.cos(ang) / N2
    inv[n:] = -2.0 * np.sin(ang) / N2
    inv[n] = np.cos(np.pi * np.arange(n)) / N2
    ntc = n // 128
    nfc = 2 * ntc
    cf = fwd.reshape(nfc, 128, ntc, 128).transpose(0, 3, 2, 1)
    ci = inv.reshape(nfc, 128, ntc, 128).transpose(2, 1, 0, 3)
    C[f'hy_cfwd{n}'] = np.ascontiguousarray(cf).astype(ml_dtypes.bfloat16)
    C[f'hy_cinv{n}'] = np.ascontiguousarray(ci).astype(ml_dtypes.bfloat16)
    return C


def host_consts():
    C = {}
    inv = (10000.0 ** (-np.arange(16, dtype=np.float32) / 16)).astype(np.float32)
    t = np.arange(TL)
    ar = (t // 64).astype(np.float32)[None, :] * inv[:, None]
    ac = (t % 64).astype(np.float32)[None, :] * inv[:, None]
    C['rope_cos'] = np.concatenate([np.cos(ar), np.cos(ar), np.cos(ac), np.cos(ac)], 0).astype(np.float32)
    C['rope_sin'] = np.concatenate([np.sin(ar), np.sin(ar), np.sin(ac), np.sin(ac)], 0).astype(np.float32)
    pr = np.zeros((64, 64), np.float32)
    for base in (0, 32):
        for i in range(16):
            pr[base + 16 + i, base + i] = -1.0
            pr[base + i, base + 16 + i] = 1.0
    C['prot'] = pr
    j = np.arange(128, dtype=np.float32)[:, None]
    i = np.arange(128, dtype=np.float32)[None, :]
    dd = i - j
    C['ret_D'] = np.ascontiguousarray(np.stack([np.maximum(dd, 0), (dd >= 0).astype(np.float32),
                                                np.maximum(-dd, 0), (dd <= 0).astype(np.float32)], 1).astype(np.float32))
    eq = np.zeros((128, 128), np.float32)
    eq[0:64, :] = i + 1.0
    eq[64:128, :] = 128.0 - i
    C['ret_EQ'] = eq
    p = np.arange(128, dtype=np.float32)
    C['ret_E'] = np.ascontiguousarray(np.stack([127 - p, p, 255 - p, 128 + p], 1).astype(np.float32))
    for n in (2048, 256):
        C.update(hyena_consts(n))
    return C


def host_inputs(I, nlayers=2, ncores=8):
    f = lambda a: np.ascontiguousarray(np.asarray(a, np.float32))
    shared = {'ident': np.eye(128, dtype=np.float32)}
    shared.update(host_consts())
    for l in range(nlayers):
        shared[f'ada_w_{l}'] = f(I['ada_w'][l])
        shared[f'ada_b_{l}'] = f(I['ada_b'][l]).reshape(144, 128)
        for n in ('ffn1_gate', 'ffn1_up', 'ffn1_down', 'w_in', 'w_out', 'ffn2_gate', 'ffn2_up', 'ffn2_down'):
            shared[f'{n}_{l}'] = f(I[n][l])
        shared[f'wuq_{l}'] = f(I['mla_wuq'][l])
        shared[f'wukv_{l}'] = f(I['mla_wukv'][l])
        shared[f'hy_w1_{l}'] = f(I['hy_ffn_w1'][l])
        shared[f'hy_w2_{l}'] = f(I['hy_ffn_w2'][l])
        shared[f'hy_w3_{l}'] = f(I['hy_ffn_w3'][l])
        shared[f'sv_{l}'] = pack_sv(I, l)
        shared[f'hy_skip_{l}'] = f(I['hy_skip'][l]).reshape(1, 1024)
        shared[f'hy_convw_{l}'] = f(I['hy_conv_w'][l]).reshape(1, 3 * 1536)
        shared[f'hy_convb_{l}'] = f(I['hy_conv_b'][l]).reshape(1, 1536)
        shared[f'ret_decay_{l}'] = f(I['ret_decay'][l]).reshape(1, 8)
        shared[f'hy_normrow_{l}'] = f(I['hy_out_norm'][l]).reshape(1, 512)
    maps = []
    for b in range(ncores):
        m = dict(shared)
        m['x'] = f(I['x'][b])
        m['ctx'] = f(I['ctx'][b])
        cv = np.concatenate([f(I['c'][b]).reshape(16, 128), f(I['c_ctx']).reshape(16, 128)], axis=0)
        m['cvec'] = np.ascontiguousarray(cv)
        maps.append(m)
    return maps


_CACHE = {}


def kernel(**inputs):
    if 'nc' not in _CACHE:
        _CACHE['nc'] = build()
    nc, K = _CACHE['nc']
    maps = host_inputs(inputs)
    maps = [{k: m[k] for k in K.inputs} for m in maps]
    res = run_bass_kernel_spmd(nc, maps, core_ids=list(range(8)))
    return np.stack([np.asarray(r['out'], np.float32) for r in res.results], axis=0)


ZROW_C = 1
ZROW_L = 259
ZROWS = 2308


def tiles_all():
    return [(0, 256, 1)] + [(256 + i * 512, 512, 0) for i in range(4)]


class Kern2(Kern):
    def svcol(self, name, i=0, n=128):
        c = SV[name] + i
        return self.sv.ap[0:n, c:c + 1]

    def phase_sv(self, l):
        if not hasattr(self, 'sv'):
            self.off = self.mark
            self.sv = self.tile([128, SV_ROWS], F32, name='sv')
            self.persist()
        b = self.load_cols(None, self.W[l]['sv'], SV_ROWS)
        self.cp('dve', self.sv.ap, self.psum[b][:, 0:SV_ROWS], [self.pb[b]], self.sv.bufs)
        self.reset()

    def phase_inproj(self, l, sbs):
        W = self.W[l]
        w_in = W['w_in']
        Zkv = self.dscr("Zkv", [256, T]); Zkr = self.dscr("Zkr", [64, T]); ZrkT = self.dscr("ZrkT", [256, T], BF16)
        Zq = self.dscr("Zq", [512, T]); ZrqT = self.dscr("ZrqT", [256, T], BF16); Zg = self.dscr("Zg", [512, T])
        ZrkM = self.dscr("ZrkM", [T, 256], BF16); Zrv = self.dscr("Zrv", [T, 512], BF16)
        Zhy = self.dscr("Zhy", [ZROWS, 1536])
        fm = [(0, 128, Zkv, 0, 1.0), (128, 128, Zkv, 128, 1.0), (256, 64, Zkr, 0, 1.0),
              (320, 128, ZrkT, 0, 0.125), (448, 128, ZrkT, 128, 0.125)]
        fm += [(1088 + 128 * i, 128, Zq, 128 * i, 1.0) for i in range(4)]
        fm += [(1600, 128, ZrqT, 0, 1.0), (1728, 128, ZrqT, 128, 1.0)]
        fm += [(1856 + 128 * i, 128, Zg, 128 * i, 1.0) for i in range(4)]
        tmg = [(320, 256, 'rk'), (576, 512, 'rv'), (2368, 512, 'hy0'), (2880, 512, 'hy1'), (3392, 512, 'hy2')]
        zt = self.tile([128, 1536], F32, name='zt')
        self.memset('dve', zt.ap, 0.0, zt.bufs)
        for r in (0, 257, 258, 2307):
            self.dma('sp', Zhy[r:r + 1, :], zt.ap[0:1, :], zt.bufs, [])
        for tiles in sbs:
            SB = sum(n for (_, n, _) in tiles)
            T0 = tiles[0][0]
            hT = self.tile([128, NCH, SB], BF16, name='hT')
            scr = self.nm_scratch()
            for st_ in self.normmod_steps(tiles, hT, 1, scr):
                st_()
            wfm = [self.tile([128, NCH, 128], BF16, name=f'wfm{i}') for i in range(3)]
            stf = [self.tile([128, SB], F32, name=f'stf{i}') for i in range(2)]
            bi = 0
            for ci, (c0, M, dst, r0, sc) in enumerate(fm):
                w = wfm[ci % 3]
                self.dma('pool', w.ap[:, :, 0:M], rows(w_in)[:, :, c0:c0 + M], [], w.bufs)
                st = stf[ci % 2]
                isb = (dst.dtype == BF16)
                st_ap = st.ap.bitcast(BF16)[:, 0:SB] if isb else st.ap[:, 0:SB]
                off = 0
                for (_, n, _) in tiles:
                    b = bi % 6
                    bi += 1
                    for k in range(NCH):
                        self.mm(self.psum[b][0:M, 0:n], w.ap[:, k, 0:M], hT.ap[:, k, off:off + n], k == 0, k == NCH - 1,
                                w.bufs + hT.bufs, [self.pb[b]])
                    if bi % 2 == 0:
                        self.act(st_ap[0:M, off:off + n], self.psum[b][0:M, 0:n], AF.Copy, [self.pb[b]], st.bufs, scale=sc)
                    else:
                        self.ts('dve', st_ap[0:M, off:off + n], self.psum[b][0:M, 0:n], sc, None, ALU.mult, None,
                                [self.pb[b]], st.bufs)
                    off += n
                self.dma('sp', dst[r0:r0 + M, T0:T0 + SB], st_ap[0:M, :], st.bufs, [])
            wtm = [self.tile([128, NCH, 512], BF16, name=f'wtm{i}') for i in range(2)]
            stt_ = [self.tile([128, 512], F32, name=f'stt{i}') for i in range(3)]
            si = 0
            for gi, (c0, N, kind) in enumerate(tmg):
                w = wtm[gi % 2]
                self.dma('pool', w.ap[:, :, 0:N], rows(w_in)[:, :, c0:c0 + N], [], w.bufs)
                for tt_ in range(SB // 128):
                    tg = T0 + tt_ * 128
                    b = 6 + bi % 2
                    bi += 1
                    for k in range(NCH):
                        self.mm(self.psum[b][:, 0:N], hT.ap[:, k, tt_ * 128:(tt_ + 1) * 128], w.ap[:, k, 0:N],
                                k == 0, k == NCH - 1, w.bufs + hT.bufs, [self.pb[b]])
                    st = stt_[si % 3]
                    si += 1
                    if kind == 'rk':
                        o_ap = st.ap.bitcast(BF16)[:, 0:N]
                        self.ts('dve', o_ap, self.psum[b][:, 0:N], 0.125, None, ALU.mult, None, [self.pb[b]], st.bufs)
                        self.dma('sp', ZrkM[tg:tg + 128, :], o_ap, st.bufs, [])
                    elif kind == 'rv':
                        o_ap = st.ap.bitcast(BF16)[:, 0:N]
                        self.act(o_ap, self.psum[b][:, 0:N], AF.Copy, [self.pb[b]], st.bufs)
                        self.dma('sp', Zrv[tg:tg + 128, :], o_ap, st.bufs, [])
                    else:
                        hc = int(kind[2]) * 512
                        if si % 2 == 0:
                            self.act(st.ap, self.psum[b][:, 0:N], AF.Copy, [self.pb[b]], st.bufs)
                        else:
                            self.cp('dve', st.ap, self.psum[b][:, 0:N], [self.pb[b]], st.bufs)
                        zr = (ZROW_C + tg) if tg < 256 else (ZROW_L + tg - 256)
                        self.dma('sp', Zhy[zr:zr + 128, hc:hc + 512], st.ap, st.bufs, [])
            self.reset()


class Kern3(Kern2):
    def headnorm_item(self, proj, src, src_bufs, M, n, gaincol, rope_tok0, dst, ring):
        i = self.hn_i
        self.hn_i += 1
        sq, r, o, t1, t2 = ring[i % len(ring)]

        def A():
            if proj is not None:
                proj()
            self.act(sq.ap[0:M, 0:n], src, AF.Square, src_bufs, sq.bufs)

        def B():
            self.mm(self.psum[3][0:M, 0:n], self.onesb.ap[0:M, 0:M], sq.ap[0:M, 0:n], True, True,
                    self.onesb.bufs + sq.bufs, [self.pb[3]])
            self.sqrt_ms(r.ap[0:M, 0:n], self.psum[3][0:M, 0:n], 1.0 / M, [self.pb[3]], r.bufs)
            if rope_tok0 is None:
                self.stt('dve', o.ap[0:M, 0:n], src, gaincol, r.ap[0:M, 0:n], ALU.mult, ALU.mult,
                         src_bufs + r.bufs + self.sv.bufs, o.bufs)
            else:
                self.stt('dve', t1.ap[0:M, 0:n], src, gaincol, r.ap[0:M, 0:n], ALU.mult, ALU.mult,
                         src_bufs + r.bufs + self.sv.bufs, t1.bufs)

        def C():
            if rope_tok0 is not None:
                self.mm(self.psum[4][0:M, 0:n], self.prot.ap, t1.ap[0:M, 0:n], True, True, self.prot.bufs + t1.bufs, [self.pb[4]])

        def D_():
            if rope_tok0 is not None:
                self.tt('dve', t2.ap[0:M, 0:n], self.psum[4][0:M, 0:n], self.sin.ap[:, rope_tok0:rope_tok0 + n], ALU.mult,
                        [self.pb[4]] + self.sin.bufs, t2.bufs)
                self.tt('pool', t1.ap[0:M, 0:n], t1.ap[0:M, 0:n], self.cos.ap[:, rope_tok0:rope_tok0 + n], ALU.mult,
                        t1.bufs + self.cos.bufs, t1.bufs)
                self.tt('pool', o.ap[0:M, 0:n], t1.ap[0:M, 0:n], t2.ap[0:M, 0:n], ALU.add, t1.bufs + t2.bufs, o.bufs)
            self.dma('sp', dst, o.ap[0:M, 0:n], o.bufs, [])
        return (A, B, C, D_)

    def phase_mlaprep(self, l, with_ctx_q):
        W = self.W[l]
        Zkv = self.dscr("Zkv", [256, T]); Zkr = self.dscr("Zkr", [64, T]); Zq = self.dscr("Zq", [512, T])
        KN = self.dscr("KN", [8, 128, T], BF16); KR = self.dscr("KR", [64, T], BF16)
        VT = self.dscr("VT", [T, 1024], BF16)
        QN = self.dscr("QN", [8, 128, T], BF16); QR = self.dscr("QR", [8, 64, T], BF16)
        self.hn_i = 0
        wukv = self.tile([128, 2, 2048], BF16, name='wukv')
        wuq = self.tile([128, 4, 1536], BF16, name='wuq')
        self.dma('pool', wukv.ap, rows(W['wukv']), [], wukv.bufs)
        self.dma('pool', wuq.ap, rows(W['wuq']), [], wuq.bufs)
        self.cos = self.tile([64, 2048], F32, name='cos')
        self.sin = self.tile([64, 2048], F32, name='sin')
        self.prot = self.tile([64, 64], F32, name='prot')
        self.dma('sp', self.cos.ap, self.din("rope_cos", [64, 2048]), [], self.cos.bufs)
        self.dma('sp', self.sin.ap, self.din("rope_sin", [64, 2048]), [], self.sin.bufs)
        self.dma('sp', self.prot.ap, self.din("prot", [64, 64]), [], self.prot.bufs)
        ring = []
        for i in range(5):
            ring.append((self.tile([128, 512], BF16, name='hn_sq'),
                         self.tile([128, 512], F32, name='hn_r'), self.tile([128, 512], BF16, name='hn_o'),
                         self.tile([128, 512], F32, name='hn_t1'), self.tile([128, 512], F32, name='hn_t2')))
        kvs = [self.tile([128, 4, 512], F32, name=f'kv{i}') for i in range(2)]
        sqs = [self.tile([128, 4, 512], BF16, name=f'kvsq{i}') for i in range(2)]
        rrs = [self.tile([128, 512], F32, name=f'kvr{i}') for i in range(2)]
        kvns = [self.tile([128, 4, 512], BF16, name=f'kvn{i}') for i in range(2)]
        krs = [self.tile([64, 512], F32, name=f'kr{i}') for i in range(2)]
        vst = [self.tile([128, 1024], BF16, name=f'vst{i}') for i in range(2)]
        wv = wukv.ap.rearrange("p c (h e) -> p c h e", e=256)
        units = []
        for (t0, n, cond) in tiles_all():
            for side in ('k', 'q'):
                if side == 'q' and cond == 1 and not with_ctx_q:
                    continue
                units.append((t0, n, cond, side))
        self._pi = 0
        self._vi = 0

        def make_unit(ui, t0, n, cond, side):
            lat0 = None if cond == 1 else t0 - 256
            nc_ = 2 if side == 'k' else 4
            src = Zkv if side == 'k' else Zq
            kv, sq, rr, kvn = kvs[ui % 2], sqs[ui % 2], rrs[ui % 2], kvns[ui % 2]
            gname = 'kv_norm' if side == 'k' else 'q_norm'

            def P1():
                self.dma('sp', kv.ap[:, 0:nc_, 0:n], rows(src)[:, :, t0:t0 + n], [], kv.bufs)
                self.act(sq.ap[:, 0:nc_, 0:n], kv.ap[:, 0:nc_, 0:n], AF.Square, kv.bufs, sq.bufs)
                if side == 'k':
                    kr = krs[ui % 2]
                    self.dma('sp', kr.ap[:, 0:n], Zkr[:, t0:t0 + n], [], kr.bufs)

            def P2():
                for c in range(nc_):
                    self.mm(self.psum[5][:, 0:n], self.onesb.ap, sq.ap[:, c, 0:n], c == 0, c == nc_ - 1,
                            self.onesb.bufs + sq.bufs, [self.pb[5]])
                self.sqrt_ms(rr.ap[:, 0:n], self.psum[5][:, 0:n], 1.0 / (128 * nc_), [self.pb[5]], rr.bufs)
                for c in range(nc_):
                    self.stt('dve', kvn.ap[:, c, 0:n], kv.ap[:, c, 0:n], self.svcol(gname, c), rr.ap[:, 0:n], ALU.mult, ALU.mult,
                             kv.bufs + rr.bufs + self.sv.bufs, kvn.bufs)
            items = []

            def proj_fn(b, M, w_t, nc2, col0):
                def f():
                    for c in range(nc2):
                        self.mm(self.psum[b][0:M, 0:n], w_t.ap[:, c, col0:col0 + M], kvn.ap[:, c, 0:n], c == 0, c == nc2 - 1,
                                w_t.bufs + kvn.bufs, [self.pb[b]])
                return f
            if side == 'k':
                for h in range(8):
                    b = self._pi % 3
                    self._pi += 1
                    items.append(self.headnorm_item(proj_fn(b, 128, wukv, 2, h * 256), self.psum[b][:, 0:n], [self.pb[b]], 128, n,
                                                    self.svcol('kn_nope'), None, KN[h, :, t0:t0 + n], ring))
                kr = krs[ui % 2]
                items.append(self.headnorm_item(None, kr.ap[:, 0:n], kr.bufs, 64, n, self.svcol('kn_rope', 0, 64), lat0,
                                                KR[:, t0:t0 + n], ring))

                def vfn(u):
                    def A():
                        st = vst[self._vi % 2]
                        self._vi += 1
                        for half in range(2):
                            b = 6 + half
                            for c in range(2):
                                self.mm(self.psum[b][:, :].rearrange("p (h e) -> p h e", e=128),
                                        kvn.ap[:, c, u * 128:(u + 1) * 128], wv[:, c, half * 4:(half + 1) * 4, 128:256],
                                        c == 0, c == 1, wukv.bufs + kvn.bufs, [self.pb[b]])
                            if half == 0:
                                self.cp('act', st.ap[:, 0:512], self.psum[b][:, :], [self.pb[b]], st.bufs)
                            else:
                                self.cp('dve', st.ap[:, 512:1024], self.psum[b][:, :], [self.pb[b]], st.bufs)
                        self.dma('sp', VT[t0 + u * 128:t0 + (u + 1) * 128, :], st.ap, st.bufs, [])
                    return (A, lambda: None, lambda: None, lambda: None)
                for u in range(n // 128):
                    items.append(vfn(u))
            else:
                for h in range(8):
                    b = self._pi % 3
                    self._pi += 1
                    items.append(self.headnorm_item(proj_fn(b, 128, wuq, 4, h * 192), self.psum[b][:, 0:n], [self.pb[b]], 128, n,
                                                    self.svcol('qn_nope'), None, QN[h, :, t0:t0 + n], ring))
                    b = self._pi % 3
                    self._pi += 1
                    items.append(self.headnorm_item(proj_fn(b, 64, wuq, 4, h * 192 + 128), self.psum[b][0:64, 0:n], [self.pb[b]], 64, n,
                                                    self.svcol('qn_rope', 0, 64), lat0, QR[h, :, t0:t0 + n], ring))
            return P1, P2, items

        U = [make_unit(ui, *u) for ui, u in enumerate(units)]
        U[0][0]()
        U[0][1]()
        for ui, (P1, P2, items) in enumerate(U):
            nxt = U[ui + 1] if ui + 1 < len(U) else None
            if nxt:
                nxt[0]()
            ni = len(items)
            for sidx in range(ni + 3):
                if sidx < ni:
                    items[sidx][0]()
                if 0 <= sidx - 1 < ni:
                    items[sidx - 1][1]()
                if 0 <= sidx - 2 < ni:
                    items[sidx - 2][2]()
                if 0 <= sidx - 3 < ni:
                    items[sidx - 3][3]()
                if nxt and sidx == ni // 2:
                    nxt[1]()
        self.reset()

    def phase_attn(self, l, with_ctx_q):
        KN = self.dscr("KN", [8, 128, T], BF16); KR = self.dscr("KR", [64, T], BF16)
        VT = self.dscr("VT", [T, 1024], BF16)
        QN = self.dscr("QN", [8, 128, T], BF16); QR = self.dscr("QR", [8, 64, T], BF16)
        MG = self.dscr("MG", [D, T], BF16)
        kn = self.tile([128, 8, T], BF16, name='kn')
        kr = self.tile([64, T], BF16, name='kr')
        v = self.tile([128, 18, 1024], BF16, name='v')
        self.dma('sp', kn.ap, KN.rearrange("h p t -> p h t"), [], kn.bufs)
        self.dma('sp', kr.ap, KR, [], kr.bufs)
        self.dma('sp', v.ap, rows(VT), [], v.bufs)
        qn = [self.tile([128, 512], BF16, name=f'qn{i}') for i in range(2)]
        qr = [self.tile([64, 512], BF16, name=f'qr{i}') for i in range(2)]
        pt = [self.tile([128, 512], BF16, name=f'pt{i}') for i in range(4)]
        oall = [self.tile([128, 8, 512], F32, name=f'oall{i}') for i in range(2)]
        osq = self.tile([128, 8, 512], BF16, name='osq')
        rinv = [self.tile([128, 512], F32, name=f'rinv{i}') for i in range(2)]
        rr = self.tile([128, 512], F32, name='arr')
        mst = [self.tile([128, 8, 512], BF16, name=f'mst{i}') for i in range(2)]
        scale = 192 ** -0.5
        qi = 0
        pi = 0
        qtiles = tiles_all() if with_ctx_q else tiles_all()[1:]
        LA = 3
        hq = 0
        for ti, (t0, n, cond) in enumerate(qtiles):
            nkc = 2 if cond == 1 else 18
            oa = oall[ti % 2]
            steps = [(h, kc) for h in range(8) for kc in range(nkc)]
            hq0 = hq

            def ld_q(h_, slot):
                if h_ < 8:
                    tt0, nn = t0, n
                elif ti + 1 < len(qtiles):
                    tt0, nn, h_ = qtiles[ti + 1][0], qtiles[ti + 1][1], 0
                else:
                    return
                self.dma('sp', qn[slot % 2].ap[:, 0:nn], QN[h_, :, tt0:tt0 + nn], [], qn[slot % 2].bufs)
                self.dma('sp', qr[slot % 2].ap[:, 0:nn], QR[h_, :, tt0:tt0 + nn], [], qr[slot % 2].bufs)

            def emit_S(i):
                h, kc = steps[i]
                q1, q2 = qn[(hq0 + h) % 2], qr[(hq0 + h) % 2]
                if ti == 0 and h == 0 and kc == 0:
                    ld_q(0, hq0)
                if kc == nkc // 2:
                    ld_q(h + 1, hq0 + h + 1)
                sb_ = (pi0 + i) % 4
                self.mm(self.psum[sb_][:, 0:n], kn.ap[:, h, kc * 128:(kc + 1) * 128], q1.ap[:, 0:n], True, False,
                        kn.bufs + q1.bufs, [self.pb[sb_]])
                self.mm(self.psum[sb_][:, 0:n], kr.ap[:, kc * 128:(kc + 1) * 128], q2.ap[:, 0:n], False, True,
                        kr.bufs + q2.bufs, [self.pb[sb_]])
                p_ = pt[(pi0 + i) % 4]
                self.act(p_.ap[:, 0:n], self.psum[sb_][:, 0:n], AF.Exp, [self.pb[sb_]], p_.bufs, scale=scale)

            def emit_PV(i):
                h, kc = steps[i]
                p_ = pt[(pi0 + i) % 4]
                ob, lb = 4 + ((hq0 + h) % 2), 6 + ((hq0 + h) % 2)
                self.mm(self.psum[ob][:, 0:n], v.ap[:, kc, h * 128:(h + 1) * 128], p_.ap[:, 0:n], kc == 0, kc == nkc - 1,
                        v.bufs + p_.bufs, [self.pb[ob]])
                self.mm(self.psum[lb][:, 0:n], self.onesb.ap, p_.ap[:, 0:n], kc == 0, kc == nkc - 1,
                        self.onesb.bufs + p_.bufs, [self.pb[lb]])
                if kc == nkc - 1:
                    ri = rinv[(hq0 + h) % 2]
                    self.P.add('dve', (lambda o_, i_: (lambda e: e.reciprocal(o_, i_)))(ri.ap[:, 0:n], self.psum[lb][:, 0:n]),
                               [self.pb[lb]], ri.bufs)
                    self.tt('dve', oa.ap[:, h, 0:n], self.psum[ob][:, 0:n], ri.ap[:, 0:n], ALU.mult, [self.pb[ob]] + ri.bufs, oa.bufs)

            pi0 = pi
            ns = len(steps)
            for i in range(min(LA, ns)):
                emit_S(i)
            for i in range(ns):
                emit_PV(i)
                if i + LA < ns:
                    emit_S(i + LA)
            pi += ns
            hq += 8
            qi = hq
            self.act(osq.ap[:, :, 0:n], oa.ap[:, :, 0:n], AF.Square, oa.bufs, osq.bufs)
            nb_ = 4 + (qi % 2)
            for h in range(8):
                self.mm(self.psum[nb_][:, 0:n], self.onesb.ap, osq.ap[:, h, 0:n], h == 0, h == 7,
                        self.onesb.bufs + osq.bufs, [self.pb[nb_]])
            self.rstd(rr.ap[:, 0:n], self.psum[nb_][:, 0:n], 1.0 / 1024, [self.pb[nb_]], rr.bufs)
            ms = mst[ti % 2]
            for h in range(8):
                self.stt('dve' if h % 2 == 0 else 'pool', ms.ap[:, h, 0:n], oa.ap[:, h, 0:n], self.svcol('out_norm', h), rr.ap[:, 0:n],
                         ALU.mult, ALU.mult, oa.bufs + rr.bufs + self.sv.bufs, ms.bufs)
            self.dma('sp', rows(MG)[:, 0:8, t0:t0 + n], ms.ap[:, :, 0:n], ms.bufs, [])
        self.reset()


class Kern4(Kern3):
    def phase_ret(self, l, with_ctx):
        W = self.W[l]
        ZrqT = self.dscr("ZrqT", [256, T], BF16); ZrkT = self.dscr("ZrkT", [256, T], BF16)
        ZrkM = self.dscr("ZrkM", [T, 256], BF16); Zrv = self.dscr("Zrv", [T, 512], BF16)
        Zg = self.dscr("Zg", [512, T]); MG = self.dscr("MG", [D, T], BF16)
        dtab = self.tile([128, 4, 128], F32, name='ret_D')
        eq = self.tile([128, 128], F32, name='ret_EQ')
        ecol = self.tile([128, 4], F32, name='ret_E')
        self.dma('sp', dtab.ap, self.din("ret_D", [128, 4, 128]), [], dtab.bufs)
        self.dma('sp', eq.ap, self.din("ret_EQ", [128, 128]), [], eq.bufs)
        self.dma('sp', ecol.ap, self.din("ret_E", [128, 4]), [], ecol.bufs)
        rd = self.tile([128, 8], F32, name='rd')
        self.dma('sp', rd.ap, W['ret_decay'].partition_broadcast(128).rearrange("p a b -> p (a b)"), [], rd.bufs)
        lg = self.tile([128, 8], F32, name='lg')
        self.act(lg.ap, rd.ap, AF.Exp, rd.bufs, lg.bufs, scale=-1.0)
        self.act(lg.ap, lg.ap, AF.Ln, lg.bufs + self.onesf.bufs, lg.bufs, bias=self.onesf.ap[:, 0:1])
        self.ts('dve', lg.ap, lg.ap, -1.0, None, ALU.mult, None, lg.bufs, lg.bufs)
        lgc = self.tile([128, 4], F32, name='lgc')
        self.cp('dve', lgc.ap[0:64, :], lg.ap[0:64, 0:4], lg.bufs, lgc.bufs)
        self.cp('dve', lgc.ap[64:128, :], lg.ap[64:128, 4:8], lg.bufs, lgc.bufs)
        Mt = self.tile([128, 4, 128], F32, name='Mt')
        mtmp = self.tile([128, 128], F32, name='mtmp')
        qdec = self.tile([128, 4, 128], F32, name='qdec')
        kdec = self.tile([128, 4, 4], F32, name='kdec')
        cdec = self.tile([128, 4], F32, name='cdec')
        for h in range(4):
            self.act(Mt.ap[:, h, :], dtab.ap[:, 0, :], AF.Exp, dtab.bufs + lg.bufs, Mt.bufs, scale=lg.ap[:, h:h + 1])
            self.tt('dve', Mt.ap[:, h, :], Mt.ap[:, h, :], dtab.ap[:, 1, :], ALU.mult, Mt.bufs + dtab.bufs, Mt.bufs)
            self.act(mtmp.ap, dtab.ap[:, 2, :], AF.Exp, dtab.bufs + lg.bufs, mtmp.bufs, scale=lg.ap[:, 4 + h:5 + h])
            self.tt('dve', mtmp.ap, mtmp.ap, dtab.ap[:, 3, :], ALU.mult, mtmp.bufs + dtab.bufs, mtmp.bufs)
            self.tt('dve', Mt.ap[:, h, :], Mt.ap[:, h, :], mtmp.ap, ALU.add, Mt.bufs + mtmp.bufs, Mt.bufs)
            self.act(qdec.ap[:, h, :], eq.ap, AF.Exp, eq.bufs + lgc.bufs, qdec.bufs, scale=lgc.ap[:, h:h + 1])
            for j, (ec, d) in enumerate([(0, 0), (1, 1), (2, 0), (3, 1)]):
                self.act(kdec.ap[:, h, j:j + 1], ecol.ap[:, ec:ec + 1], AF.Exp, ecol.bufs + lg.bufs, kdec.bufs,
                         scale=lg.ap[:, d * 4 + h:d * 4 + h + 1])
        self.act(cdec.ap, lgc.ap, AF.Exp, lgc.bufs, cdec.bufs, scale=128.0)
        kMc = self.tile([128, 2, 256], BF16, name='kMc')
        vc = self.tile([128, 2, 512], BF16, name='vc')
        self.dma('sp', kMc.ap, rows(ZrkM)[:, 0:2, :], [], kMc.bufs)
        self.dma('sp', vc.ap, rows(Zrv)[:, 0:2, :], [], vc.bufs)
        sinit = self.tile([128, 4, 128], F32, name='sinit')
        kdc = self.tile([128, 2, 128], BF16, name='kdc')
        for h in range(4):
            for c, (cf, cb) in enumerate([(2, 1), (0, 3)]):
                self.ts('dve', kdc.ap[:, c, 0:64], kMc.ap[:, c, h * 64:(h + 1) * 64], kdec.ap[:, h, cf:cf + 1], None, ALU.mult, None,
                        kMc.bufs + kdec.bufs, kdc.bufs)
                self.ts('dve', kdc.ap[:, c, 64:128], kMc.ap[:, c, h * 64:(h + 1) * 64], kdec.ap[:, h, cb:cb + 1], None, ALU.mult, None,
                        kMc.bufs + kdec.bufs, kdc.bufs)
            b = h % 2
            for c in range(2):
                self.mm(self.psum[b][:, 0:128], kdc.ap[:, c, :], vc.ap[:, c, h * 128:(h + 1) * 128], c == 0, c == 1,
                        kdc.bufs + vc.bufs, [self.pb[b]])
            self.cp('act', sinit.ap[:, h, :], self.psum[b][:, 0:128], [self.pb[b]], sinit.bufs)
        modq = []
        mod_epi = None
        if l + 1 < self.nl and not self.dbg.get('no_modov'):
            pro, modq, mod_epi = self.mod_steps(l + 1, 3)
            pro()
        base = self.off
        seqs = [(256, 16, True)] + ([(0, 2, False)] if with_ctx else [])
        bi = 0
        for (t0, nch, use_init) in seqs:
            self.off = base
            n = nch * 128
            kM = self.tile([128, nch, 256], BF16, name='kM')
            v = self.tile([128, nch, 512], BF16, name='v')
            self.dma('sp', kM.ap, rows(ZrkM)[:, t0 // 128:t0 // 128 + nch, :], [], kM.bufs)
            self.dma('sp', v.ap, rows(Zrv)[:, t0 // 128:t0 // 128 + nch, :], [], v.bufs)
            qq = [self.tile([128, n], BF16, name=f'qq{i}') for i in range(2)]
            kk = [self.tile([64, n], BF16, name=f'kk{i}') for i in range(2)]
            qd = [self.tile([128, n], BF16, name=f'qd{i}') for i in range(2)]
            kd = [self.tile([128, nch, 128], BF16, name=f'kd{i}') for i in range(2)]
            kvs = [self.tile([128, nch, 128], F32, name=f'kvs{i}') for i in range(2)]
            sst = [self.tile([128, nch, 128], F32, name=f'sst{i}') for i in range(2)]
            sstb = [self.tile([128, nch, 128], BF16, name=f'sstb{i}') for i in range(2)]
            sm = [self.tile([128, 4, 128], BF16, name=f'sm{i}') for i in range(2)]
            oT = [self.tile([128, 512], F32, name=f'oT{i}') for i in range(2)]
            osq = [self.tile([128, 512], F32, name=f'osq{i}') for i in range(2)]
            mu = [self.tile([128, 512], F32, name=f'mu{i}') for i in range(2)]
            var = [self.tile([128, 512], F32, name=f'var{i}') for i in range(2)]
            gt = [self.tile([128, 512], F32, name=f'gt{i}') for i in range(2)]
            og = [self.tile([128, 512], BF16, name=f'og{i}') for i in range(2)]
            gi = 0
            for h in range(4):
                q_, k_, qd_, kd_, kvs_, sst_, sstb_ = qq[h % 2], kk[h % 2], qd[h % 2], kd[h % 2], kvs[h % 2], sst[h % 2], sstb[h % 2]
                self.dma('sp', q_.ap[0:64, :], ZrqT[h * 64:(h + 1) * 64, t0:t0 + n], [], q_.bufs)
                self.dma('sp', q_.ap[64:128, :], ZrqT[h * 64:(h + 1) * 64, t0:t0 + n], [], q_.bufs)
                self.dma('sp', k_.ap, ZrkT[h * 64:(h + 1) * 64, t0:t0 + n], [], k_.bufs)
                self.tt('dve', qd_.ap.rearrange("p (c i) -> p c i", i=128), q_.ap.rearrange("p (c i) -> p c i", i=128),
                        qdec.ap[:, h, :].unsqueeze(1).to_broadcast([128, nch, 128]), ALU.mult, q_.bufs + qdec.bufs, qd_.bufs)
                self.ts('dve', kd_.ap[:, :, 0:64], kM.ap[:, :, h * 64:(h + 1) * 64], kdec.ap[:, h, 0:1], None, ALU.mult, None,
                        kM.bufs + kdec.bufs, kd_.bufs)
                self.ts('dve', kd_.ap[:, :, 64:128], kM.ap[:, :, h * 64:(h + 1) * 64], kdec.ap[:, h, 1:2], None, ALU.mult, None,
                        kM.bufs + kdec.bufs, kd_.bufs)
                for g in range((nch + 3) // 4):
                    b = bi % 2
                    bi += 1
                    ng = min(4, nch - g * 4)
                    for c4 in range(ng):
                        c = g * 4 + c4
                        self.mm(self.psum[b][:, c4 * 128:(c4 + 1) * 128], kd_.ap[:, c, :], v.ap[:, c, h * 128:(h + 1) * 128], True, True,
                                kd_.bufs + v.bufs, [self.pb[b]])
                    self.cp('act' if g % 2 == 0 else 'dve', kvs_.ap[:, g * 4:g * 4 + ng, :],
                            self.psum[b][:, 0:ng * 128].rearrange("p (c d) -> p c d", d=128), [self.pb[b]], kvs_.bufs)
                if use_init:
                    self.cp('dve', sst_.ap[0:64, 0, :], sinit.ap[0:64, h, :], sinit.bufs, sst_.bufs)
                    self.cp('dve', sst_.ap[64:128, nch - 1, :], sinit.ap[64:128, h, :], sinit.bufs, sst_.bufs)
                else:
                    self.memset('dve', sst_.ap[0:64, 0, :], 0.0, sst_.bufs)
                    self.memset('dve', sst_.ap[64:128, nch - 1, :], 0.0, sst_.bufs)
                for c in range(nch - 1):
                    self.stt('dve', sst_.ap[0:64, c + 1, :], sst_.ap[0:64, c, :], cdec.ap[0:64, h:h + 1], kvs_.ap[0:64, c, :],
                             ALU.mult, ALU.add, sst_.bufs + cdec.bufs + kvs_.bufs, sst_.bufs)
                    cb = nch - 1 - c
                    self.stt('dve', sst_.ap[64:128, cb - 1, :], sst_.ap[64:128, cb, :], cdec.ap[64:128, h:h + 1], kvs_.ap[64:128, cb, :],
                             ALU.mult, ALU.add, sst_.bufs + cdec.bufs + kvs_.bufs, sst_.bufs)
                self.cp('act', sstb_.ap, sst_.ap, sst_.bufs, sstb_.bufs)
                for g in range((nch + 3) // 4):
                    ng = min(4, nch - g * 4)
                    nt = ng * 128
                    tg = t0 + g * 512
                    sb_ = 2 if modq or mod_epi else 2 + gi % 2
                    ob = 4 + gi % 2
                    for _ in range(2):
                        if modq:
                            modq.pop(0)()
                    sm_, oT_, osq_, mu_, var_, gt_, og_ = sm[gi % 2], oT[gi % 2], osq[gi % 2], mu[gi % 2], var[gi % 2], gt[gi % 2], og[gi % 2]
                    gi += 1
                    self.dma('sp', gt_.ap[:, 0:nt], Zg[h * 128:(h + 1) * 128, tg:tg + nt], [], gt_.bufs)
                    for c4 in range(ng):
                        c = g * 4 + c4
                        self.mm(self.psum[sb_][:, c4 * 128:(c4 + 1) * 128], k_.ap[0:64, c * 128:(c + 1) * 128],
                                q_.ap[0:64, c * 128:(c + 1) * 128], True, True, k_.bufs + q_.bufs, [self.pb[sb_]])
                    self.tt('dve', sm_.ap[:, 0:ng, :], self.psum[sb_][:, 0:nt].rearrange("p (c i) -> p c i", i=128),
                            Mt.ap[:, h, :].unsqueeze(1).to_broadcast([128, ng, 128]), ALU.mult, [self.pb[sb_]] + Mt.bufs, sm_.bufs)
                    for c4 in range(ng):
                        c = g * 4 + c4
                        self.mm(self.psum[ob][:, c4 * 128:(c4 + 1) * 128], v.ap[:, c, h * 128:(h + 1) * 128], sm_.ap[:, c4, :], True, False,
                                v.bufs + sm_.bufs, [self.pb[ob]])
                        self.mm(self.psum[ob][:, c4 * 128:(c4 + 1) * 128], sstb_.ap[:, c, :], qd_.ap[:, c * 128:(c + 1) * 128], False, True,
                                sstb_.bufs + qd_.bufs, [self.pb[ob]])
                    self.cp('act', oT_.ap[:, 0:nt], self.psum[ob][:, 0:nt], [self.pb[ob]], oT_.bufs)
                    self.act(osq_.ap[:, 0:nt], oT_.ap[:, 0:nt], AF.Square, oT_.bufs, osq_.bufs)
                    self.mm(self.psum[6][:, 0:nt], self.onesf.ap, oT_.ap[:, 0:nt], True, True, self.onesf.bufs + oT_.bufs, [self.pb[6]])
                    self.mm(self.psum[7][:, 0:nt], self.onesf.ap, osq_.ap[:, 0:nt], True, True, self.onesf.bufs + osq_.bufs, [self.pb[7]])
                    self.ts('dve', mu_.ap[:, 0:nt], self.psum[6][:, 0:nt], 1.0 / 128, None, ALU.mult, None, [self.pb[6]], mu_.bufs)
                    self.tt('dve', var_.ap[:, 0:nt], mu_.ap[:, 0:nt], mu_.ap[:, 0:nt], ALU.mult, mu_.bufs, var_.bufs)
                    self.stt('dve', var_.ap[:, 0:nt], self.psum[7][:, 0:nt], 1.0 / 128, var_.ap[:, 0:nt], ALU.mult, ALU.subtract,
                             [self.pb[7]] + var_.bufs, var_.bufs)
                    self.rstd(var_.ap[:, 0:nt], var_.ap[:, 0:nt], 1.0, var_.bufs, var_.bufs)
                    self.tt('dve', oT_.ap[:, 0:nt], oT_.ap[:, 0:nt], mu_.ap[:, 0:nt], ALU.subtract, oT_.bufs + mu_.bufs, oT_.bufs)
                    self.tt('pool', oT_.ap[:, 0:nt], oT_.ap[:, 0:nt], var_.ap[:, 0:nt], ALU.mult, oT_.bufs + var_.bufs, oT_.bufs)
                    self.act(oT_.ap[:, 0:nt], oT_.ap[:, 0:nt], AF.Identity, oT_.bufs + self.sv.bufs, oT_.bufs,
                             scale=self.svcol('gn_w', h), bias=self.svcol('gn_b', h))
                    self.act(gt_.ap[:, 0:nt], gt_.ap[:, 0:nt], AF.Silu, gt_.bufs, gt_.bufs)
                    self.tt('pool', og_.ap[:, 0:nt], oT_.ap[:, 0:nt], gt_.ap[:, 0:nt], ALU.mult, oT_.bufs + gt_.bufs, og_.bufs)
                    self.dma('sp', MG[1536 + h * 128:1536 + (h + 1) * 128, tg:tg + nt], og_.ap[:, 0:nt], og_.bufs, [])
        while modq:
            modq.pop(0)()
        if mod_epi:
            mod_epi()
        self.reset()


TWO_PI = 2.0 * math.pi


class Kern5(Kern4):
    def wrap_sin(self, x, M, n, bufs):
        m1 = self.hy_m1
        self.ts('dve', m1.ap[0:M, 0:n], x, math.pi, -TWO_PI, ALU.is_gt, ALU.mult, bufs, m1.bufs)
        self.tt('dve', x, x, m1.ap[0:M, 0:n], ALU.add, bufs + m1.bufs, bufs)
        self.ts('dve', m1.ap[0:M, 0:n], x, -math.pi, TWO_PI, ALU.is_lt, ALU.mult, bufs, m1.bufs)
        self.tt('dve', x, x, m1.ap[0:M, 0:n], ALU.add, bufs + m1.bufs, bufs)
        self.act(x, x, AF.Sin, bufs, bufs)

    def phase_hyena(self, l, seq):
        W = self.W[l]
        if seq == 'lat':
            n, tg0, zr0 = 2048, 256, ZROW_L
        else:
            n, tg0, zr0 = 256, 0, ZROW_C
        ntc = n // 128
        nfc = 2 * ntc
        npair = ntc
        sfx = str(n)
        Zhy = self.dscr("Zhy", [ZROWS, 1536]); MG = self.dscr("MG", [D, T], BF16)
        HS = self.dscr("HS" + sfx, [2, 2, n, 512])
        HC = self.dscr("HC" + sfx, [2, 128, 512])
        HX = self.dscr("HX" + sfx, [2, n, 512])
        featsT = self.din("hy_feats" + sfx, [33, n])
        window = self.din("hy_window" + sfx, [n, 512])
        Cfwd = self.din("hy_cfwd" + sfx, [nfc, 128, ntc, 128], BF16)
        Cinv = self.din("hy_cinv" + sfx, [ntc, 128, nfc, 128], BF16)
        base0 = self.off
        HV = self.dscr("HV" + sfx, [n, 512])
        u16 = self.tile([128, ntc, 512], BF16, name='u16')
        base_u16 = self.off
        hfilt = self.tile([128, ntc, 2048], BF16, name='hfilt')
        base_tmp = self.off
        w1s = self.tile([33, 64], F32, name='w1s'); w2s = self.tile([64, 64], F32, name='w2s')
        w3s = self.tile([64, 2048], F32, name='w3s'); ft = self.tile([33, n], F32, name='ft')
        self.dma('sp', w1s.ap, W['hy_w1'], [], w1s.bufs); self.dma('sp', w2s.ap, W['hy_w2'], [], w2s.bufs)
        self.dma('sp', w3s.ap, W['hy_w3'], [], w3s.bufs); self.dma('sp', ft.ap, featsT, [], ft.bufs)
        h2T = self.tile([64, n], F32, name='h2T')
        h1 = self.tile([64, 512], F32, name='h1')
        self.hy_m1 = self.tile([64, 512], F32, name='hy_m1')
        for c0 in range(0, n, 512):
            nt = min(512, n - c0)
            self.mm(self.psum[0][0:64, 0:nt], w1s.ap, ft.ap[:, c0:c0 + nt], True, True, w1s.bufs + ft.bufs, [self.pb[0]])
            self.ts('dve', h1.ap[:, 0:nt], self.psum[0][0:64, 0:nt], self.svcol('b1', 0, 64), None, ALU.add, None,
                    [self.pb[0]] + self.sv.bufs, h1.bufs)
            self.wrap_sin(h1.ap[:, 0:nt], 64, nt, h1.bufs)
            self.mm(self.psum[1][0:64, 0:nt], w2s.ap, h1.ap[:, 0:nt], True, True, w2s.bufs + h1.bufs, [self.pb[1]])
            self.ts('dve', h2T.ap[:, c0:c0 + nt], self.psum[1][0:64, 0:nt], self.svcol('b2', 0, 64), None, ALU.add, None,
                    [self.pb[1]] + self.sv.bufs, h2T.bufs)
            self.wrap_sin(h2T.ap[:, c0:c0 + nt], 64, nt, h2T.bufs)
        h2b = self.tile([64, n], BF16, name='h2b')
        w3b = self.tile([64, 2048], BF16, name='w3b')
        self.cp('act', h2b.ap, h2T.ap, h2T.bufs, h2b.bufs)
        self.dma('pool', w3b.ap, W['hy_w3'], [], w3b.bufs)
        win = [self.tile([128, 512], F32, name=f'win{i}') for i in range(2)]
        ftmp = [[self.tile([128, 512], F32, name=f'ftmp{i}{g}') for g in range(4)] for i in range(2)]
        bi = 0
        for tc in range(ntc):
            wt = win[tc % 2]
            ft4 = ftmp[tc % 2]
            self.dma('sp', wt.ap, window[tc * 128:(tc + 1) * 128, :], [], wt.bufs)
            for g in range(4):
                b = bi % 4
                bi += 1
                self.mm(self.psum[b][:, :], h2b.ap[:, tc * 128:(tc + 1) * 128], w3b.ap[:, g * 512:(g + 1) * 512], True, True,
                        h2b.bufs + w3b.bufs, [self.pb[b]])
                self.tt('dve', ft4[g].ap, self.psum[b][:, :], wt.ap, ALU.mult, [self.pb[b]] + wt.bufs, ft4[g].bufs)
            for o in range(2):
                self.tt('dve', hfilt.ap[:, tc, o * 512:(o + 1) * 512], ft4[o].ap, ft4[2 + o].ap, ALU.add,
                        ft4[o].bufs + ft4[2 + o].bufs, hfilt.bufs)
                self.tt('dve', hfilt.ap[:, tc, (2 + o) * 512:(3 + o) * 512], ft4[o].ap, ft4[2 + o].ap, ALU.subtract,
                        ft4[o].bufs + ft4[2 + o].bufs, hfilt.bufs)
        if self.dbg.get('hy_stop') == 'A':
            self.reset()
            return
        self.reset()
        self.off = base_tmp
        slab = [self.tile([128, ntc, 128], BF16, name=f'fslab{i}') for i in range(3)]
        ob_ = [self.tile([128, 512], F32, name=f'hso{i}') for i in range(6)]
        cw = self.tile([128, 3, 1536], F32, name='cw')
        cb = self.tile([128, 1536], F32, name='cb')
        self.dma('sp', cw.ap.rearrange("p a b -> p (a b)"), W['hy_convw'].partition_broadcast(128).rearrange("p a b -> p (a b)"), [], cw.bufs)
        self.dma('sp', cb.ap, W['hy_convb'].partition_broadcast(128).rearrange("p a b -> p (a b)"), [], cb.bufs)
        zz = [[self.tile([128, 1536], F32, name=f'zz{i}{j}') for j in range(3)] for i in range(2)]
        acc = [self.tile([128, 1536], F32, name=f'acc{i}') for i in range(2)]

        def ld_z(tc_):
            z3_ = zz[tc_ % 2]
            r0_ = zr0 + tc_ * 128
            for j in range(3):
                self.dma('sp', z3_[j].ap, Zhy[r0_ - 1 + j:r0_ - 1 + j + 128, :], [], z3_[j].bufs)

        def sc_step(tc):
            z3 = zz[tc % 2]
            a = acc[tc % 2]
            if tc + 1 < ntc:
                ld_z(tc + 1)
            self.tt('pool', z3[2].ap[:, 0:768], z3[2].ap[:, 0:768], cw.ap[:, 2, 0:768], ALU.mult, z3[2].bufs + cw.bufs, z3[2].bufs)
            self.tt('dve', a.ap, z3[1].ap, cw.ap[:, 1, :], ALU.mult, z3[1].bufs + cw.bufs, a.bufs)
            self.tt('dve', z3[0].ap, z3[0].ap, cw.ap[:, 0, :], ALU.mult, z3[0].bufs + cw.bufs, z3[0].bufs)
            self.tt('dve', z3[2].ap[:, 768:1536], z3[2].ap[:, 768:1536], cw.ap[:, 2, 768:1536], ALU.mult, z3[2].bufs + cw.bufs, z3[2].bufs)
            self.tt('dve', a.ap, a.ap, z3[0].ap, ALU.add, a.bufs + z3[0].bufs, a.bufs)
            self.tt('dve', a.ap, a.ap, cb.ap, ALU.add, a.bufs + cb.bufs, a.bufs)
            self.tt('dve', a.ap, a.ap, z3[2].ap, ALU.add, a.bufs + z3[2].bufs, a.bufs)
            self.dma('sp', HX[0, tc * 128:(tc + 1) * 128, :], a.ap[:, 0:512], a.bufs, [])
            self.dma('sp', HX[1, tc * 128:(tc + 1) * 128, :], a.ap[:, 512:1024], a.bufs, [])
            self.dma('sp', HV[tc * 128:(tc + 1) * 128, :], a.ap[:, 1024:1536], a.bufs, [])
            self.cp('act', u16.ap[:, tc, :], a.ap[:, 1024:1536], a.bufs, u16.bufs)
        ld_z(0)
        scq = [(lambda tc=tc: sc_step(tc)) for tc in range(ntc)]
        si = 0
        oi = 0
        chunks = [(i, part) for i in range(npair) for part in range(2)]

        def ld_slab(ci):
            i_, part_ = chunks[ci]
            sl_ = slab[ci % 3]
            self.dma('sp', sl_.ap, Cfwd[i_ + part_ * npair], [], sl_.bufs)
        ld_slab(0)
        if len(chunks) > 1:
            ld_slab(1)
        for i in range(npair):
            for part in range(2):
                sl = slab[si % 3]
                if si + 2 < len(chunks):
                    ld_slab(si + 2)
                for o in range(2):
                    b = (2 * si + o) % 4
                    col = (o if part == 0 else 2 + o) * 512
                    for tc in range(ntc):
                        self.mm(self.psum[b][:, :], sl.ap[:, tc, :], hfilt.ap[:, tc, col:col + 512], tc == 0, tc == ntc - 1,
                                sl.bufs + hfilt.bufs, [self.pb[b]])
                    if i == 0 and part == 1:
                        for tc in range(ntc):
                            self.mm(self.psum[4 + o][0:1, :], sl.ap[:, tc, 0:1], hfilt.ap[:, tc, o * 512:(o + 1) * 512],
                                    tc == 0, tc == ntc - 1, sl.bufs + hfilt.bufs, [self.pb[4 + o]])
                    ot = ob_[oi % 6]
                    oi += 1
                    self.cp('act' if o == 0 else 'dve', ot.ap, self.psum[b][:, :], [self.pb[b]], ot.bufs)
                    if part == 0:
                        self.dma('sp', HS[o, 0, i * 128:(i + 1) * 128, :], ot.ap, ot.bufs, [])
                        if i == 0:
                            self.aC[o] = ot
                    else:
                        if i == 0:
                            ct = self.tile([128, 512], F32, name=f'ct{o}')
                            a0 = self.aC[o]
                            self.cp('pool', ct.ap, a0.ap, a0.bufs, ct.bufs)
                            self.cp('dve', ct.ap[0:1, :], self.psum[4 + o][0:1, :], [self.pb[4 + o]], ct.bufs)
                            self.dma('sp', HC[o], ct.ap, ct.bufs, [])
                            self.memset('dve', ot.ap[0:1, :], 0.0, ot.bufs)
                        self.dma('sp', HS[o, 1, i * 128:(i + 1) * 128, :], ot.ap, ot.bufs, [])
                si += 1
                if scq and si % 2 == 0:
                    scq.pop(0)()
        while scq:
            scq.pop(0)()
        self.reset()
        self.off = base_u16
        if self.dbg.get('hy_stop') in ('B', 'C'):
            return
        u32 = self.tile([128, ntc, 512], F32, name='u32')
        skb = self.tile([128, 1024], F32, name='skb')
        nrm = self.tile([128, 512], F32, name='nrm')
        self.dma('sp', u32.ap, rows(HV), [], u32.bufs)
        self.dma('sp', skb.ap, W['hy_skip'].partition_broadcast(128).rearrange("p a b -> p (a b)"), [], skb.bufs)
        self.dma('sp', nrm.ap, W['hy_normrow'].partition_broadcast(128).rearrange("p a b -> p (a b)"), [], nrm.bufs)
        Y = self.tile([128, nfc, 512], BF16, name='Y')
        fsl = [self.tile([128, ntc, 128], BF16, name=f'fsl{i}') for i in range(3)]
        isl = [self.tile([128, nfc, 128], BF16, name=f'isl{i}') for i in range(2)]
        hA = [self.tile([128, 512], F32, name=f'hA{i}') for i in range(2)]
        hB = [self.tile([128, 512], F32, name=f'hB{i}') for i in range(2)]
        hCt = self.tile([128, 512], F32, name='hCt')
        ur = [self.tile([128, 512], F32, name=f'ur{i}') for i in range(2)]
        ui = [self.tile([128, 512], F32, name=f'ui{i}') for i in range(2)]
        t1 = [self.tile([128, 512], F32, name=f't1{i}') for i in range(2)]
        t2 = [self.tile([128, 512], F32, name=f't2{i}') for i in range(2)]
        xg = [self.tile([128, 512], F32, name=f'xg{i}') for i in range(2)]
        yo = [self.tile([128, 512], F32, name=f'yo{i}') for i in range(2)]
        yb = [self.tile([128, 512], BF16, name=f'yb{i}') for i in range(2)]
        ssq = [self.tile([128, 8], F32, name=f'ssq{i}') for i in range(2)]
        fmo = [self.tile([128, 4, 128], BF16, name=f'fmo{i}') for i in range(2)]
        si = 0
        fchunks = [(i, part) for i in range(npair) for part in range(2)]
        for o in range(2):
            self.dma('sp', hCt.ap, HC[o], [], hCt.bufs)

            def ld_f(ci, base):
                i_, part_ = fchunks[ci]
                sl_ = fsl[(base + ci) % 3]
                self.dma('sp', sl_.ap, Cfwd[i_ + part_ * npair], [], sl_.bufs)

            def ld_i(tc_):
                sl_ = isl[tc_ % 2]
                self.dma('sp', sl_.ap, Cinv[tc_], [], sl_.bufs)
            base_si = si
            ld_f(0, base_si)
            ld_f(1, base_si)
            for i in range(npair):
                pb0 = (i % 2) * 2
                for part in range(2):
                    fc = i + part * npair
                    sl = fsl[si % 3]
                    ci = si - base_si
                    if ci + 2 < len(fchunks):
                        ld_f(ci + 2, base_si)
                    elif ci + 2 == len(fchunks):
                        ld_i(0)
                    si += 1
                    for tc in range(ntc):
                        self.mm(self.psum[pb0 + part][:, :], sl.ap[:, tc, :], u16.ap[:, tc, :], tc == 0, tc == ntc - 1,
                                sl.bufs + u16.bufs, [self.pb[pb0 + part]])
                A, B = hA[i % 2], hB[i % 2]
                self.dma('sp', A.ap, HS[o, 0, i * 128:(i + 1) * 128, :], [], A.bufs)
                self.dma('sp', B.ap, HS[o, 1, i * 128:(i + 1) * 128, :], [], B.bufs)
                Cc = hCt if i == 0 else A
                ur_, ui_, t1_, t2_ = ur[i % 2], ui[i % 2], t1[i % 2], t2[i % 2]
                self.cp('act', ur_.ap, self.psum[pb0][:, :], [self.pb[pb0]], ur_.bufs)
                self.cp('act', ui_.ap, self.psum[pb0 + 1][:, :], [self.pb[pb0 + 1]], ui_.bufs)
                self.tt('dve', t1_.ap, ur_.ap, A.ap, ALU.mult, ur_.bufs + A.bufs, t1_.bufs)
                self.tt('pool', t2_.ap, ui_.ap, B.ap, ALU.mult, ui_.bufs + B.bufs, t2_.bufs)
                self.tt('dve', Y.ap[:, i, :], t1_.ap, t2_.ap, ALU.subtract, t1_.bufs + t2_.bufs, Y.bufs)
                self.tt('pool', t1_.ap, ur_.ap, B.ap, ALU.mult, ur_.bufs + B.bufs, t1_.bufs)
                self.tt('dve', t2_.ap, ui_.ap, Cc.ap, ALU.mult, ui_.bufs + Cc.bufs, t2_.bufs)
                self.tt('pool', Y.ap[:, npair + i, :], t1_.ap, t2_.ap, ALU.add, t1_.bufs + t2_.bufs, Y.bufs)
            for tc in range(ntc):
                sl = isl[tc % 2]
                if tc + 1 < ntc:
                    ld_i(tc + 1)
                b = 4 + tc % 2
                for fc in range(nfc):
                    self.mm(self.psum[b][:, :], sl.ap[:, fc, :], Y.ap[:, fc, :], fc == 0, fc == nfc - 1, sl.bufs + Y.bufs, [self.pb[b]])
                x_, y_ = xg[tc % 2], yo[tc % 2]
                self.dma('sp', x_.ap, HX[o, tc * 128:(tc + 1) * 128, :], [], x_.bufs)
                self.tt('pool', y_.ap, u32.ap[:, tc, :], skb.ap[:, o * 512:(o + 1) * 512], ALU.mult, u32.bufs + skb.bufs, y_.bufs)
                self.tt('dve', y_.ap, y_.ap, self.psum[b][:, :], ALU.add, y_.bufs + [self.pb[b]], y_.bufs)
                if o == 0:
                    self.tt('dve', u32.ap[:, tc, :], y_.ap, x_.ap, ALU.mult, y_.bufs + x_.bufs, u32.bufs)
                    self.cp('act', u16.ap[:, tc, :], u32.ap[:, tc, :], u32.bufs, u16.bufs)
                else:
                    self.tt('dve', y_.ap, y_.ap, x_.ap, ALU.mult, y_.bufs + x_.bufs, y_.bufs)
                    sq_, yb_, fo = ssq[tc % 2], yb[tc % 2], fmo[tc % 2]
                    self.act(x_.ap, y_.ap, AF.Square, y_.bufs, x_.bufs + sq_.bufs, accum_out=sq_.ap[:, 0:1])
                    self.rstd(sq_.ap[:, 1:2], sq_.ap[:, 0:1], 1.0 / 512, sq_.bufs, sq_.bufs)
                    self.stt('dve', yb_.ap, y_.ap, sq_.ap[:, 1:2], nrm.ap, ALU.mult, ALU.mult, y_.bufs + sq_.bufs + nrm.bufs, yb_.bufs)
                    pt_ = self.psum[6 + tc % 2][:, :].bitcast(BF16)
                    for cc in range(4):
                        self.tr(pt_[:, cc * 128:(cc + 1) * 128], yb_.ap[:, cc * 128:(cc + 1) * 128], self.identb.ap,
                                yb_.bufs + self.identb.bufs, [self.pb[6 + tc % 2]])
                    self.cp('act', fo.ap, pt_[:, 0:512].rearrange("p (c t) -> p c t", t=128), [self.pb[6 + tc % 2]], fo.bufs)
                    tg = tg0 + tc * 128
                    self.dma('sp', rows(MG)[:, 8:12, tg:tg + 128], fo.ap, fo.bufs, [])
            if o == 0:
                pass
        self.reset()
        self.off = base0


class Kern6(Kern5):
    def phase_outproj(self, l, sbs):
        W = self.W[l]
        w_out = W['w_out']
        MG = self.dscr("MG", [D, T], BF16)
        tiles = [t for sb in sbs for t in sb]
        T0 = tiles[0][0]
        SB = sum(n for (_, n, _) in tiles)
        mg = [self.tile([128, SB], BF16, name=f'mg{k}') for k in range(NCH)]
        for k in range(NCH):
            self.dma('sp', mg[k].ap, MG[k * 128:(k + 1) * 128, T0:T0 + SB], [], mg[k].bufs)
        ws = [self.tile([128, NCH, 128], BF16, name=f'wo{i}') for i in range(3)]
        xr = [self.tile([128, SB], F32, name=f'oxr{i}') for i in range(2)]
        xo = [self.tile([128, SB], F32, name=f'oxo{i}') for i in range(2)]
        bi = 0

        def ld(m_):
            self.dma('pool', ws[m_ % 3].ap, rows(w_out)[:, :, m_ * 128:(m_ + 1) * 128], [], ws[m_ % 3].bufs)
            self.dma('sp', xr[m_ % 2].ap.rearrange("p (b t) -> p b t", t=128), self.xtr(m_, T0, SB), self.xt_bufs([m_], T0, SB), xr[m_ % 2].bufs)
        ld(0)
        for m in range(NCH):
            if m + 1 < NCH:
                ld(m + 1)
            w = ws[m % 3]
            xrt, xot = xr[m % 2], xo[m % 2]
            off = 0
            for ti, (_, n, cond) in enumerate(tiles):
                b = bi % 8
                bi += 1
                for k in range(NCH):
                    self.mm(self.psum[b][:, 0:n], w.ap[:, k, :], mg[k].ap[:, off:off + n], k == 0, k == NCH - 1,
                            w.bufs + mg[k].bufs, [self.pb[b]])
                self.stt('dve', xot.ap[:, off:off + n], self.psum[b][:, 0:n], self.hgcol(1, m, cond),
                         xrt.ap[:, off:off + n], ALU.mult, ALU.add, [self.pb[b]] + self.hg.bufs + xrt.bufs, xot.bufs)
                off += n
            self.dma('sp', self.xtr(m, T0, SB), xot.ap.rearrange("p (b t) -> p b t", t=128), xot.bufs, self.xt_bufs([m], T0, SB))
        self.reset()


def full_phases(nlayers=2):
    ph = [('init',)]
    for l in range(nlayers):
        last = (l == nlayers - 1)
        sb_mix = SB_ALL
        if l == 0:
            ph += [('mod', l)]
        ph += [('sv', l), ('ffn', l, 1, SB_ALL), ('inproj', l, SB_IN2), ('mlaprep', l, not last),
               ('attn', l, not last), ('ret', l, not last), ('hyena', l, 'lat')]
        if not last:
            ph += [('hyena', l, 'ctx')]
        ph += [('outproj', l, SB_LAT if last else SB_ALL), ('ffn', l, 2, SB_LAT if last else SB_ALL)]
    ph += [('final',)]
    return ph
```
